# Optimizing a Trainium2 kernel written in Bass

```python
import math
import jax, jax.numpy as jnp
from jax import lax
import numpy as np

D_MODEL = 2048
BATCH = 4
SEQ = 4096
DEPTH = 2

GRID_W = 64
CTX_LEN = 256
EPS = 1e-6

CHUNK = 128
A_WIDTH = D_MODEL // 2
A_GROUPS = 8
A_GROUP_DIM = A_WIDTH // A_GROUPS

B_HEADS = 8
B_QK_DIM = 64
B_V_DIM = 2 * B_QK_DIM
B_QK_WIDTH = B_HEADS * 2 * B_QK_DIM
B_WIDTH = B_HEADS * B_V_DIM
Q_BLOCK = 128
ROPE_BASE = 10000.0

C_WIDTH = D_MODEL
C_CONV_W = 31
C_PAD = (C_CONV_W - 1) // 2

A_U_END = A_WIDTH
A_V_END = A_U_END + A_WIDTH
A_G_END = A_V_END + A_WIDTH
B_Q_END = A_G_END + B_QK_WIDTH
B_K_END = B_Q_END + B_QK_WIDTH
B_V_END = B_K_END + B_WIDTH
EVEN_IN = B_V_END + B_WIDTH
EVEN_MIX = A_WIDTH + B_WIDTH
ODD_IN = 3 * C_WIDTH

N_EVEN = (DEPTH + 1) // 2
N_ODD = DEPTH // 2

kernel_name = "hybrid_gmlp_diffattn_conformer_prefix_dit"


def rms_norm(x, g):
    xf = x.astype(jnp.float32)
    y = xf * lax.rsqrt(jnp.mean(xf * xf, axis=-1, keepdims=True) + EPS)
    return (y * g.astype(jnp.float32)).astype(x.dtype)


def layer_norm(x, g, b):
    xf = x.astype(jnp.float32)
    mu = jnp.mean(xf, axis=-1, keepdims=True)
    var = jnp.mean(jnp.square(xf - mu), axis=-1, keepdims=True)
    y = (xf - mu) * lax.rsqrt(var + EPS)
    return (y * g.astype(jnp.float32) + b.astype(jnp.float32)).astype(x.dtype)


def adaln(cond, w, b):
    m = jax.nn.silu(cond) @ w + b
    return jnp.split(m, 3, axis=-1)


def modulate(h, shift, scale):
    return h * (1.0 + scale) + shift


def axial_rope_tables(rows, dim):
    axis_dim = dim // 2
    inv = ROPE_BASE ** (-jnp.arange(0, axis_dim, 2, dtype=jnp.float32) / axis_dim)
    row = jnp.repeat(jnp.arange(rows, dtype=jnp.float32), GRID_W)
    col = jnp.tile(jnp.arange(GRID_W, dtype=jnp.float32), rows)
    ar = row[:, None] * inv[None, :]
    ac = col[:, None] * inv[None, :]
    ang = jnp.concatenate([ar, ar, ac, ac], axis=-1)
    return jnp.cos(ang), jnp.sin(ang)


def apply_axial_rope(t, cos, sin):
    x1, x2, x3, x4 = jnp.split(t, 4, axis=-1)
    rot = jnp.concatenate([-x2, x1, -x4, x3], axis=-1)
    return (t.astype(jnp.float32) * cos + rot.astype(jnp.float32) * sin).astype(t.dtype)


def diff_qk(t, g, cos, sin):
    bsz, L, _ = t.shape
    t = rms_norm(t.reshape(bsz, L, B_HEADS, 2, B_QK_DIM), g).transpose(3, 0, 2, 1, 4)
    if cos is not None:
        t = apply_axial_rope(t, cos, sin)
    return t


def heads_v(t):
    bsz, L, _ = t.shape
    return t.reshape(bsz, L, B_HEADS, B_V_DIM).transpose(0, 2, 1, 3)


def diff_attend(q, k, v, lam):
    s = jnp.einsum('mbhqd,mbhkd->mbhqk', q, k).astype(jnp.float32) * (B_QK_DIM ** -0.5)
    p = jax.nn.softmax(s, axis=-1)
    p = p[0] - lam * p[1]
    return jnp.einsum('bhqk,bhkd->bhqd', p.astype(v.dtype), v)


def blocked_diff_attention(q, k_all, v_all, lam):
    _, bsz, nh, L, d = q.shape
    nb = L // Q_BLOCK
    qb = q.reshape(2, bsz, nh, nb, Q_BLOCK, d).transpose(3, 0, 1, 2, 4, 5)
    ob = lax.map(lambda blk: diff_attend(blk, k_all, v_all, lam), qb)
    return ob.transpose(1, 2, 0, 3, 4).reshape(bsz, nh, L, B_V_DIM)


def diff_head_out(o, onorm_g, lambda_init):
    bsz, nh, L, _ = o.shape
    o = rms_norm(o, onorm_g) * (1.0 - lambda_init)
    return o.transpose(0, 2, 1, 3).reshape(bsz, L, B_WIDTH)


def chunk_gmlp(u, v, vnorm_g, w_s, b_s):
    bsz, L, _ = v.shape
    v = rms_norm(v, vnorm_g).reshape(bsz, L // CHUNK, CHUNK, A_GROUPS, A_GROUP_DIM)
    mixed = jnp.einsum('gqp,bnpgc->bnqgc', w_s, v) + b_s.T[None, None, :, :, None]
    return u * mixed.reshape(bsz, L, A_WIDTH)


def even_branch_out(au, av, ag, ob, bg, a_vnorm_g, a_ws, a_bs, onorm_g, lambda_init):
    ya = chunk_gmlp(jax.nn.gelu(au, approximate=False), jax.nn.gelu(av, approximate=False),
                    a_vnorm_g, a_ws, a_bs) * jax.nn.silu(ag)
    yb = diff_head_out(ob, onorm_g, lambda_init) * jax.nn.silu(bg)
    return jnp.concatenate([ya, yb], axis=-1)


def even_layer(x, ctx, c, c_ctx, norm_g, ada_w, ada_b, w_in, a_vnorm_g, a_ws, a_bs,
               qnorm_g, knorm_g, lam_vecs, onorm_g, w_out, lambda_init, cos, sin, update_ctx):
    sh, sc, gt = (t[:, None, :] for t in adaln(c, ada_w, ada_b))
    csh, csc, cgt = adaln(c_ctx, ada_w, ada_b)
    h = modulate(rms_norm(x, norm_g), sh, sc)
    hc = modulate(rms_norm(ctx, norm_g), csh, csc)
    lv = lam_vecs.astype(jnp.float32)
    lam = jnp.exp(jnp.sum(lv[0] * lv[1])) - jnp.exp(jnp.sum(lv[2] * lv[3])) + lambda_init

    if update_ctx:
        pc = jnp.split(hc @ w_in, [A_U_END, A_V_END, A_G_END, B_Q_END, B_K_END, B_V_END], axis=-1)
        kc_t, vc_t = pc[4], pc[5]
    else:
        kc_t, vc_t = jnp.split(hc @ w_in[:, B_Q_END:B_V_END], [B_QK_WIDTH], axis=-1)
    kc = diff_qk(kc_t, knorm_g, None, None)
    vc = heads_v(vc_t)

    au, av, ag, bq, bk, bv, bg = jnp.split(
        h @ w_in, [A_U_END, A_V_END, A_G_END, B_Q_END, B_K_END, B_V_END], axis=-1)
    q = diff_qk(bq, qnorm_g, cos, sin)
    k = diff_qk(bk, knorm_g, cos, sin)
    v = heads_v(bv)
    k_all = jnp.concatenate([k, kc], axis=3)
    v_all = jnp.concatenate([v, vc], axis=2)
    ob = blocked_diff_attention(q, k_all, v_all, lam)
    y = even_branch_out(au, av, ag, ob, bg, a_vnorm_g, a_ws, a_bs, onorm_g, lambda_init)
    x_new = x + gt * (y @ w_out)

    if update_ctx:
        qc = diff_qk(pc[3], qnorm_g, None, None)
        obc = diff_attend(qc, kc, vc, lam)
        yc = even_branch_out(pc[0], pc[1], pc[2], obc, pc[6], a_vnorm_g, a_ws, a_bs,
                             onorm_g, lambda_init)
        ctx = ctx + cgt * (yc @ w_out)
    return x_new, ctx


def conv_module(h, w_in, dw_w, dw_b, ln_g, ln_b):
    a, b, g = jnp.split(h @ w_in, 3, axis=-1)
    y = a * jax.nn.sigmoid(b)
    y = lax.conv_general_dilated(y, dw_w[:, None, :], window_strides=(1,),
                                 padding=[(C_PAD, C_PAD)],
                                 dimension_numbers=('NWC', 'WIO', 'NWC'),
                                 feature_group_count=C_WIDTH) + dw_b
    y = jax.nn.silu(layer_norm(y, ln_g, ln_b))
    return y * jax.nn.silu(g)


def odd_layer(x, ctx, c, c_ctx, norm_g, ada_w, ada_b, w_in, dw_w, dw_b, ln_g, ln_b, w_out,
              update_ctx):
    sh, sc, gt = (t[:, None, :] for t in adaln(c, ada_w, ada_b))
    h = modulate(rms_norm(x, norm_g), sh, sc)
    x_new = x + gt * (conv_module(h, w_in, dw_w, dw_b, ln_g, ln_b) @ w_out)
    if update_ctx:
        csh, csc, cgt = adaln(c_ctx, ada_w, ada_b)
        hc = modulate(rms_norm(ctx, norm_g), csh, csc)
        ctx = ctx + cgt * (conv_module(hc, w_in, dw_w, dw_b, ln_g, ln_b) @ w_out)
    return x_new, ctx


def setup_inputs(seed: int = 0) -> dict:
    key = jax.random.key(seed)
    ks = iter(jax.random.split(key, 40))

    def nrm(shape, scale):
        return jax.random.normal(next(ks), shape, jnp.float32) * scale

    D = D_MODEL
    return {
        "x": nrm((BATCH, SEQ, D), 1.0),
        "c": nrm((BATCH, D), 1.0),
        "ctx": nrm((BATCH, CTX_LEN, D), 1.0),
        "c_ctx": nrm((D,), 1.0),
        "e_norm_g": 1.0 + nrm((N_EVEN, D), 0.02),
        "e_ada_w": nrm((N_EVEN, D, 3 * D), 0.5 * D ** -0.5),
        "e_ada_b": nrm((N_EVEN, 3 * D), 0.01),
        "e_w_in": nrm((N_EVEN, D, EVEN_IN), D ** -0.5),
        "e_a_vnorm_g": 1.0 + nrm((N_EVEN, A_WIDTH), 0.02),
        "e_a_ws": nrm((N_EVEN, A_GROUPS, CHUNK, CHUNK), CHUNK ** -0.5),
        "e_a_bs": 1.0 + nrm((N_EVEN, A_GROUPS, CHUNK), 0.02),
        "e_b_qnorm_g": 1.0 + nrm((N_EVEN, B_QK_DIM), 0.02),
        "e_b_knorm_g": 1.0 + nrm((N_EVEN, B_QK_DIM), 0.02),
        "e_b_lambda": nrm((N_EVEN, 4, B_QK_DIM), 0.1),
        "e_b_onorm_g": 1.0 + nrm((N_EVEN, B_V_DIM), 0.02),
        "e_w_out": nrm((N_EVEN, EVEN_MIX, D), EVEN_MIX ** -0.5),
        "o_norm_g": 1.0 + nrm((N_ODD, D), 0.02),
        "o_ada_w": nrm((N_ODD, D, 3 * D), 0.5 * D ** -0.5),
        "o_ada_b": nrm((N_ODD, 3 * D), 0.01),
        "o_w_in": nrm((N_ODD, D, ODD_IN), D ** -0.5),
        "o_dw_w": nrm((N_ODD, C_CONV_W, C_WIDTH), C_CONV_W ** -0.5),
        "o_dw_b": nrm((N_ODD, C_WIDTH), 0.01),
        "o_ln_g": 1.0 + nrm((N_ODD, C_WIDTH), 0.02),
        "o_ln_b": nrm((N_ODD, C_WIDTH), 0.01),
        "o_w_out": nrm((N_ODD, C_WIDTH, D), C_WIDTH ** -0.5),
    }


def reference(x, c, ctx, c_ctx, e_norm_g, e_ada_w, e_ada_b, e_w_in, e_a_vnorm_g, e_a_ws,
              e_a_bs, e_b_qnorm_g, e_b_knorm_g, e_b_lambda, e_b_onorm_g, e_w_out,
              o_norm_g, o_ada_w, o_ada_b, o_w_in, o_dw_w, o_dw_b, o_ln_g, o_ln_b, o_w_out):
    n_lat = x.shape[1]
    rows = n_lat // GRID_W
    cos, sin = axial_rope_tables(rows, B_QK_DIM)
    for layer in range(DEPTH):
        update_ctx = any(j % 2 == 0 for j in range(layer + 1, DEPTH))
        i = layer // 2
        if layer % 2 == 0:
            lambda_init = 0.8 - 0.6 * math.exp(-0.3 * layer)
            x, ctx = even_layer(x, ctx, c, c_ctx, e_norm_g[i], e_ada_w[i], e_ada_b[i], e_w_in[i],
                                e_a_vnorm_g[i], e_a_ws[i], e_a_bs[i], e_b_qnorm_g[i],
                                e_b_knorm_g[i], e_b_lambda[i], e_b_onorm_g[i], e_w_out[i],
                                lambda_init, cos, sin, update_ctx)
        else:
            x, ctx = odd_layer(x, ctx, c, c_ctx, o_norm_g[i], o_ada_w[i], o_ada_b[i], o_w_in[i],
                               o_dw_w[i], o_dw_b[i], o_ln_g[i], o_ln_b[i], o_w_out[i],
                               update_ctx)
    return x
```

```python
import math
import numpy as np
import ml_dtypes
import concourse.bass as bass
import concourse.mybir as mybir
from concourse.bass_utils import run_bass_kernel_spmd

F32 = mybir.dt.float32
BF16 = mybir.dt.bfloat16
AF = mybir.ActivationFunctionType
ALU = mybir.AluOpType

D = 2048
SEQ = 4096
CTX = 256
NTOK = SEQ + CTX
NF = 2176
NOWN = 2048
EPS = 1e-6
LAMBDA_INIT0 = 0.8 - 0.6 * math.exp(-0.3 * 0)
FT = [(0, 512), (512, 512), (1024, 512), (1536, 512), (2048, 128)]


class Buf:
    __slots__ = ("name", "w", "r")

    def __init__(self, name=""):
        self.name = name
        self.w = None
        self.r = []


class Prog:
    ENGS = ("pe", "act", "dve", "pool", "sp")

    def __init__(self, nc, n_dma_slots=8):
        self.nc = nc
        self.q = {e: [] for e in self.ENGS}
        self.sems = {}
        self.cnt = {}
        self.waited = {e: {} for e in self.ENGS}
        self._ctx = []
        for e in self.ENGS:
            self._mksem("c_" + e)
        self.n_dma_slots = n_dma_slots
        self.dma_i = {e: 0 for e in self.ENGS}
        for e in ("sp", "pool", "act"):
            for j in range(n_dma_slots):
                self._mksem("d_%s_%d" % (e, j))

    def _mksem(self, key):
        cm = self.nc.semaphore(key)
        h = cm.__enter__()
        self._ctx.append(cm)
        self.sems[key] = h
        self.cnt[key] = 0

    def _waits(self, eng, reads, writes, extra=()):
        need = {}

        def add(ev):
            if ev is None:
                return
            k, v = ev
            if need.get(k, 0) < v:
                need[k] = v
        for b in reads:
            add(b.w)
        for b in writes:
            add(b.w)
            for ev in b.r:
                add(ev)
        for ev in extra:
            add(ev)
        out = []
        wd = self.waited[eng]
        for k, v in need.items():
            if wd.get(k, 0) < v:
                wd[k] = v
                out.append((k, v))
        return out

    def _record(self, ev, reads, writes):
        for b in reads:
            b.r.append(ev)
        for b in writes:
            b.w = ev
            b.r = []

    def op(self, eng, fn, reads=(), writes=()):
        waits = self._waits(eng, reads, writes)
        key = "c_" + eng
        self.cnt[key] += 1
        ev = (key, self.cnt[key])
        self._record(ev, reads, writes)
        self.q[eng].append((waits, fn, key, 1))
        return ev

    def dma(self, eng, fn, reads=(), writes=()):
        j = self.dma_i[eng] % self.n_dma_slots
        self.dma_i[eng] += 1
        key = "d_%s_%d" % (eng, j)
        prev = (key, self.cnt[key]) if self.cnt[key] > 0 else None
        waits = self._waits(eng, reads, writes, extra=(prev,) if prev else ())
        self.cnt[key] += 16
        ev = (key, self.cnt[key])
        self._record(ev, reads, writes)
        self.q[eng].append((waits, fn, key, 16))
        return ev

    def barrier(self):
        for eng in self.ENGS:
            waits = []
            wd = self.waited[eng]
            for k, v in self.cnt.items():
                if v > 0 and wd.get(k, 0) < v:
                    wd[k] = v
                    waits.append((k, v))
            if waits:
                self.q[eng].append((waits, None, None, 0))

    def emit(self):
        nc = self.nc
        q = self.q
        sems = self.sems

        def run(e, items):
            for waits, fn, key, inc in items:
                for k, v in waits:
                    e.wait_ge(sems[k], v)
                if fn is not None:
                    ins = fn(e)
                    ins.then_inc(sems[key], inc)

        with nc.Block() as block:
            @block.tensor
            def _(e):
                run(e, q["pe"])

            @block.scalar
            def _(e):
                run(e, q["act"])

            @block.vector
            def _(e):
                run(e, q["dve"])

            @block.gpsimd
            def _(e):
                run(e, q["pool"])

            @block.sync
            def _(e):
                run(e, q["sp"])

    def close(self):
        for cm in reversed(self._ctx):
            cm.__exit__(None, None, None)


class Arena:
    def __init__(self, ap, nwords):
        self.ap = ap
        self.n = nwords
        self.off = 0

    def mark(self):
        return self.off

    def reset(self, m):
        self.off = m

    def alloc(self, free_shape, dtype):
        n = 1
        for s in free_shape:
            n *= s
        words = n if dtype == F32 else (n + 1) // 2
        words = (words + 7) // 8 * 8
        assert self.off + words <= self.n, ("arena overflow", self.off, words, self.n)
        v = self.ap[:, self.off:self.off + words]
        self.off += words
        if dtype != F32:
            v = v.bitcast(dtype)
        v = v[:, 0:n]
        if len(free_shape) == 2:
            v = v.rearrange("p (a b) -> p a b", a=free_shape[0])
        elif len(free_shape) == 3:
            v = v.rearrange("p (a b c) -> p a b c", a=free_shape[0], b=free_shape[1])
        return v


def build_program(debug=False, stop_after=None):
    nc = bass.Bass("TRN2", target_bir_lowering=False)
    P = Prog(nc)

    def din(name, shape, dt=F32):
        return nc.dram_tensor(name, list(shape), dt, kind="ExternalInput").ap()

    def dscr(name, shape, dt):
        return nc.dram_tensor(name, list(shape), dt,
                              kind="ExternalOutput" if debug else "Internal").ap()

    xc = din("xc", [NTOK, D])
    cT_d = din("cT", [128, 2, 16])
    cos_d = din("rope_cos", [128, NTOK])
    sin_d = din("rope_sin", [128, NTOK])
    identb_d = din("ident_bf", [128, 128], BF16)
    identf_d = din("ident_f", [128, 128])
    rotT_d = din("rotT", [128, 128], BF16)
    blk64_d = din("blk64", [128, 128], BF16)
    halo_d = din("halo_mask", [128, 2])
    W = {}
    for L, nin in (("e", 7168), ("o", 6144)):
        W[L + "_w_in"] = din(L + "_w_in", [D, nin])
        W[L + "_w_out"] = din(L + "_w_out", [D, D])
        W[L + "_ada_w"] = din(L + "_ada_w", [D, 6144])
        W[L + "_ada_bT"] = din(L + "_ada_bT", [128, 48])
        W[L + "_norm_gT"] = din(L + "_norm_gT", [128, 16])
    vng_d = din("e_vng_bc", [128, 1024])
    wsT_d = din("e_wsT", [128, 8, 128])
    bs_d = din("e_bs_bc", [128, 8, 128])
    gq_d = din("e_gq", [128, 1])
    gk_d = din("e_gk", [128, 1])
    lam_d = din("e_lam", [1, 256])
    gon_d = din("e_gon", [128, 1])
    dww_d = din("o_dw_wT", [128, 16, 31])
    dwb_d = din("o_dw_bT", [128, 16])
    lng_d = din("o_ln_gT", [128, 16])
    lnb_d = din("o_ln_bT", [128, 16])
    out_d = nc.dram_tensor("out", [NOWN, D], F32, kind="ExternalOutput").ap()

    K_d = dscr("K_s", [8, 128, NTOK], BF16)
    V_d = dscr("V_s", [NTOK, 1024], BF16)
    Q_d = dscr("Q_s", [8, 128, NF], BF16)
    G_d = dscr("G_s", [8, 128, NF], BF16)
    AU_d = dscr("AU_s", [8, 128, NF], BF16)
    AG_d = dscr("AG_s", [8, 128, NF], BF16)
    Y_d = dscr("Y_s", [16, 128, NF], BF16)
    X1_d = dscr("X1_s", [NF, D], F32)
    YC_d = dscr("YC_s", [16, 128, NOWN], F32)
    SG_d = dscr("SG_s", [16, 128, NOWN], BF16)
    if debug:
        HT_d = dscr("HT_s", [16, 128, NF], BF16)
        MOD_d = dscr("MOD_s", [128, 2 * 2 * 48 + 2 * 32], F32)
        GT_d = dscr("GT_s", [2, 128, D], F32)

    ARENA_WORDS = 53000
    arena_t = nc.alloc_sbuf_tensor("arena", [128, ARENA_WORDS], F32)
    AR = Arena(arena_t[:, :], ARENA_WORDS)
    pp = [nc.alloc_psum_tensor("pp%d" % i, [128, 2, 512], F32) for i in range(4)]

    def bank(k):
        return pp[k // 2][:, k % 2, :]
    pbuf = [Buf("bank%d" % k) for k in range(8)]

    identb = AR.alloc([128], BF16)
    identf = AR.alloc([128], F32)
    rotT = AR.alloc([128], BF16)
    blk64 = AR.alloc([128], BF16)
    onesb = AR.alloc([128], BF16)
    onesf = AR.alloc([128], F32)
    epsc = AR.alloc([1], F32)
    halo = AR.alloc([2], F32)
    mT = [AR.alloc([2, 48], F32) for _ in range(2)]
    Gp = [AR.alloc([2, 16], F32) for _ in range(2)]
    gt_bc = [AR.alloc([D], F32) for _ in range(2)]
    adabT = [AR.alloc([48], F32) for _ in range(2)]
    ngT = [AR.alloc([16], F32) for _ in range(2)]
    vng = AR.alloc([1024], F32)
    wsTb = AR.alloc([8, 128], BF16)
    bsb = AR.alloc([8, 128], F32)
    bsh = AR.alloc([8, 128], BF16)
    bsl = AR.alloc([8, 128], BF16)
    gq = AR.alloc([1], F32)
    gk = AR.alloc([1], F32)
    gon = AR.alloc([1], F32)
    nlam = AR.alloc([1], F32)
    lamrow = AR.alloc([256], F32)
    lamtmp = AR.alloc([8], F32)
    dww = AR.alloc([16, 31], F32)
    dwb = AR.alloc([16], F32)
    lng = AR.alloc([16], F32)
    lnb = AR.alloc([16], F32)
    cTs = AR.alloc([2, 16], F32)
    scT = AR.alloc([2, 16], F32)
    scTb = AR.alloc([2, 16], BF16)
    b_const = Buf("const")
    b_mod = Buf("mod")
    PERSIST = AR.mark()

    LNAME = ("e", "o")

    def ld(dst, src):
        P.dma("sp", lambda e: e.dma_start(out=dst, in_=src), writes=[b_const])
    ld(identb, identb_d)
    ld(identf, identf_d)
    ld(rotT, rotT_d)
    ld(blk64, blk64_d)
    ld(halo, halo_d)
    for l in range(2):
        ld(adabT[l], W[LNAME[l] + "_ada_bT"])
        ld(ngT[l], W[LNAME[l] + "_norm_gT"])
    ld(vng, vng_d)
    P.dma("pool", lambda e: e.dma_start(out=wsTb, in_=wsT_d), writes=[b_const])
    ld(bsb, bs_d)
    ld(gq, gq_d)
    ld(gk, gk_d)
    ld(gon, gon_d)
    ld(dww, dww_d)
    ld(dwb, dwb_d)
    ld(lng, lng_d)
    ld(lnb, lnb_d)
    ld(cTs, cT_d)
    P.dma("sp", lambda e: e.dma_start(out=lamrow[0:1, :], in_=lam_d), writes=[b_const])
    P.op("dve", lambda e: e.memset(onesb, 1.0), writes=[b_const])
    P.op("dve", lambda e: e.memset(onesf, 1.0), writes=[b_const])
    P.op("dve", lambda e: e.memset(epsc, EPS), writes=[b_const])
    P.op("dve", lambda e: e.tensor_copy(out=bsh[0:1], in_=bsb[0:1]), reads=[b_const], writes=[b_const])
    P.op("dve", lambda e: e.tensor_tensor(out=bsl[0:1], in0=bsb[0:1], in1=bsh[0:1], op=ALU.subtract),
         reads=[b_const], writes=[b_const])
    P.op("dve", lambda e: e.tensor_scalar(out=gon, in0=gon, scalar1=1.0 - LAMBDA_INIT0, scalar2=None,
                                          op0=ALU.mult), reads=[b_const], writes=[b_const])
    P.op("dve", lambda e: e.tensor_tensor(out=lamrow[0:1, 0:64], in0=lamrow[0:1, 0:64], in1=lamrow[0:1, 64:128],
                                          op=ALU.mult), reads=[b_const], writes=[b_const])
    P.op("dve", lambda e: e.tensor_tensor(out=lamrow[0:1, 128:192], in0=lamrow[0:1, 128:192],
                                          in1=lamrow[0:1, 192:256], op=ALU.mult), reads=[b_const], writes=[b_const])
    P.op("dve", lambda e: e.reduce_sum(out=lamtmp[0:1, 0:1], in_=lamrow[0:1, 0:64], axis=mybir.AxisListType.X),
         reads=[b_const], writes=[b_const])
    P.op("dve", lambda e: e.reduce_sum(out=lamtmp[0:1, 1:2], in_=lamrow[0:1, 128:192], axis=mybir.AxisListType.X),
         reads=[b_const], writes=[b_const])
    P.op("act", lambda e: e.activation(out=lamtmp[0:1, 2:4], in_=lamtmp[0:1, 0:2], func=AF.Exp),
         reads=[b_const], writes=[b_const])
    P.op("dve", lambda e: e.tensor_tensor(out=lamtmp[0:1, 4:5], in0=lamtmp[0:1, 3:4], in1=lamtmp[0:1, 2:3],
                                          op=ALU.subtract), reads=[b_const], writes=[b_const])
    P.op("dve", lambda e: e.tensor_scalar(out=lamtmp[0:1, 4:5], in0=lamtmp[0:1, 4:5], scalar1=-LAMBDA_INIT0,
                                          scalar2=None, op0=ALU.add), reads=[b_const], writes=[b_const])
    P.op("pe", lambda e: e.matmul(bank(0)[:, 0:1], lhsT=onesf[0:1, :], rhs=lamtmp[0:1, 4:5], start=True, stop=True),
         reads=[b_const], writes=[pbuf[0]])
    P.op("dve", lambda e: e.tensor_copy(out=nlam, in_=bank(0)[:, 0:1]), reads=[pbuf[0]], writes=[b_const])
    P.op("act", lambda e: e.activation(out=scT, in_=cTs, func=AF.Silu), reads=[b_const], writes=[b_const])
    P.op("dve", lambda e: e.tensor_copy(out=scTb, in_=scT), reads=[b_const], writes=[b_const])

    m0 = AR.mark()
    NWB = 2
    wblk = [AR.alloc([16, 512], BF16) for _ in range(NWB)]
    b_wblk = [Buf("wblk%d" % i) for i in range(NWB)]
    wctr = [0]

    def wload(src_w, col0, ncols=512):
        i = wctr[0] % NWB
        wctr[0] += 1
        dst = wblk[i][:, :, 0:ncols]
        src = src_w[:, col0:col0 + ncols].rearrange("(c p) n -> p c n", p=128)
        P.dma("pool", lambda e: e.dma_start(out=dst, in_=src), writes=[b_wblk[i]])
        return wblk[i], b_wblk[i]

    dgt = AR.alloc([128], F32)
    b_dgt = Buf("dgt")

    def ada_block(l, blk, wt, wb, pk):
        def mm(e):
            ins = None
            for j in range(4):
                for c in range(16):
                    ins = e.matmul(bank(pk)[:, 2 * j:2 * j + 2], lhsT=wt[:, c, j * 128:(j + 1) * 128],
                                   rhs=scTb[:, :, c], start=(c == 0), stop=(c == 15))
            return ins
        P.op("pe", mm, reads=[wb, b_const], writes=[pbuf[pk]])
        for v in range(2):
            P.op("dve", lambda e, v=v: e.tensor_tensor(
                out=mT[l][:, v, blk * 4:blk * 4 + 4],
                in0=bank(pk)[:, 0:8].rearrange("p (j v) -> p j v", v=2)[:, :, v],
                in1=adabT[l][:, blk * 4:blk * 4 + 4], op=ALU.add),
                reads=[pbuf[pk], b_const], writes=[b_mod])

    def ada_gp(l):
        for v in range(2):
            P.op("dve", lambda e, v=v: e.scalar_tensor_tensor(
                out=Gp[l][:, v, :], in0=mT[l][:, v, 16:32], scalar=1.0, in1=ngT[l], op0=ALU.add, op1=ALU.mult),
                reads=[b_mod, b_const], writes=[b_mod])

    def ada_gate(l, pk):
        for c in range(16):
            P.op("dve", lambda e, c=c: e.tensor_scalar(out=dgt, in0=identf, scalar1=mT[l][:, 0, 32 + c:33 + c],
                                                       scalar2=None, op0=ALU.mult),
                 reads=[b_mod, b_const], writes=[b_dgt])
            P.op("pe", lambda e, c=c: e.matmul(bank(pk)[:, (c % 4) * 128:(c % 4 + 1) * 128], lhsT=onesf, rhs=dgt,
                                               start=True, stop=True),
                 reads=[b_dgt, b_const], writes=[pbuf[pk]])
            if c % 4 == 3:
                P.op("act", lambda e, c=c: e.activation(
                    out=gt_bc[l][:, (c - 3) * 128:(c + 1) * 128], in_=bank(pk), func=AF.Copy),
                    reads=[pbuf[pk]], writes=[b_mod])

    wada0 = W["e_ada_w"]
    nxt = wload(wada0, 0)
    for blk in range(8):
        wt, wb = nxt
        if blk + 1 < 8:
            nxt = wload(wada0, (blk + 1) * 512)
        ada_block(0, blk, wt, wb, blk % 2)
    ada_gp(0)
    lazy = {"items": [("blk", 0, b) for b in range(8, 12)] + [("gate", 0)] +
                     [("blk", 1, b) for b in range(12)] + [("gp", 1), ("gate", 1)],
            "i": 0, "nxt": None}

    def lazy_step(pk):
        it = lazy["items"]
        i = lazy["i"]
        if i >= len(it):
            return False
        lazy["i"] = i + 1
        item = it[i]
        if item[0] == "blk":
            if lazy["nxt"] is None:
                lazy["nxt"] = wload(W[LNAME[item[1]] + "_ada_w"], item[2] * 512)
            wt, wb = lazy["nxt"]
            lazy["nxt"] = None
            for j in range(i + 1, len(it)):
                if it[j][0] == "blk":
                    lazy["nxt"] = wload(W[LNAME[it[j][1]] + "_ada_w"], it[j][2] * 512)
                    break
            ada_block(item[1], item[2], wt, wb, pk)
        elif item[0] == "gp":
            ada_gp(item[1])
        else:
            ada_gate(item[1], pk)
        return True
    P.barrier()
    if stop_after == "P0":
        return _finish(nc, P)

    m_pre_h = AR.mark()
    hT = AR.alloc([16, NF], BF16)
    b_hTc = [Buf("hT%d" % c) for c in range(16)]
    m_h = AR.mark()

    def xprep(l, tile_src, ntiles, vec_of_tile, post_tile=None):
        xn4 = [AR.alloc([4, D], BF16) for _ in range(2)]
        b_xn4 = [Buf("xn4_%d" % i) for i in range(2)]
        junk = AR.alloc([D], BF16)
        b_junk = Buf("junk")
        st = AR.alloc([64], F32)
        b_stk = [Buf("st%d" % k) for k in range(32)]
        ngroups = (ntiles + 3) // 4
        for gi in range(ngroups):
            tl = list(range(gi * 4, min(ntiles, gi * 4 + 4)))
            xb = xn4[gi % 2]
            bx = b_xn4[gi % 2]
            for t in tl:
                xt_ap, xt_b = tile_src(t)
                k = t % 32
                P.op("dve", lambda e, xt_ap=xt_ap, k=k: e.scalar_tensor_tensor(
                    out=junk, in0=xt_ap, scalar=1.0, in1=xt_ap, op0=ALU.mult, op1=ALU.mult,
                    accum_out=st[:, k:k + 1]), reads=[xt_b], writes=[b_junk, b_stk[k]])
                P.op("act", lambda e, k=k: e.activation(out=st[:, 32 + k:33 + k], in_=st[:, k:k + 1], func=AF.Ln,
                                                        bias=epsc, scale=1.0 / D), reads=[b_stk[k], b_const], writes=[b_stk[k]])
                P.op("act", lambda e, k=k: e.activation(out=st[:, 32 + k:33 + k], in_=st[:, 32 + k:33 + k],
                                                        func=AF.Exp, scale=-0.5), reads=[b_stk[k]], writes=[b_stk[k]])
                P.op("act", lambda e, xt_ap=xt_ap, k=k, xb=xb, t=t: e.activation(
                    out=xb[:, t % 4, :], in_=xt_ap, func=AF.Copy, scale=st[:, 32 + k:33 + k]),
                    reads=[xt_b, b_stk[k]], writes=[bx])
                if post_tile is not None:
                    post_tile(t, xt_ap, xt_b)
            nt = len(tl)
            runs = []
            for tt, t in enumerate(tl):
                v = vec_of_tile(t)
                if runs and runs[-1][0] == v:
                    runs[-1][2] += 1
                else:
                    runs.append([v, tt, 1])
            for c in range(16):
                pk = c % 4

                def tr(e, c=c, pk=pk, xb=xb, nt=nt):
                    ins = None
                    pv = bank(pk)[:, 0:256].bitcast(BF16).rearrange("p (a b) -> p a b", a=4)
                    for tt in range(nt):
                        ins = e.transpose(out=pv[:, tt, :], in_=xb[:, tt, c * 128:(c + 1) * 128], identity=identb)
                    return ins
                P.op("pe", tr, reads=[bx, b_const], writes=[pbuf[pk]])
                for (v, tt0, ntt) in runs:
                    if c % 2 == 0:
                        P.op("dve", lambda e, c=c, pk=pk, tt0=tt0, ntt=ntt, t0=tl[0], v=v, l=l: e.tensor_scalar(
                            out=hT[:, c, (t0 + tt0) * 128:(t0 + tt0 + ntt) * 128],
                            in0=bank(pk)[:, 0:256].bitcast(BF16)[:, tt0 * 128:(tt0 + ntt) * 128],
                            scalar1=Gp[l][:, v, c:c + 1], scalar2=mT[l][:, v, c:c + 1], op0=ALU.mult, op1=ALU.add),
                            reads=[pbuf[pk], b_mod], writes=[b_hTc[c]])
                    else:
                        P.op("act", lambda e, c=c, pk=pk, tt0=tt0, ntt=ntt, t0=tl[0], v=v, l=l: e.activation(
                            out=hT[:, c, (t0 + tt0) * 128:(t0 + tt0 + ntt) * 128],
                            in_=bank(pk)[:, 0:256].bitcast(BF16)[:, tt0 * 128:(tt0 + ntt) * 128],
                            func=AF.Identity, scale=Gp[l][:, v, c:c + 1], bias=mT[l][:, v, c:c + 1]),
                            reads=[pbuf[pk], b_mod], writes=[b_hTc[c]])

    def dram_tile_src(src_d, base_tile, src_bufs=None):
        xt = [AR.alloc([D], F32) for _ in range(3)]
        b_xt = [Buf("xt%d" % i) for i in range(3)]

        def src(t):
            i = t % 3
            g = base_tile + t
            rd = [src_bufs[t]] if src_bufs is not None else []
            P.dma("sp", lambda e, i=i, g=g: e.dma_start(out=xt[i], in_=src_d[g * 128:(g + 1) * 128, :]),
                  reads=rd, writes=[b_xt[i]])
            return xt[i], b_xt[i]
        return src

    def proj_phase(which):
        tok_off = NF if which == "B" else 0
        if which == "AV":
            al = lambda shape, dt: None
        else:
            al = AR.alloc
        kg = [al([512], BF16) for _ in range(2)]
        ksq = [al([512], BF16) for _ in range(2)]
        b_kg = [Buf() for _ in range(2)]
        b_ksq = [Buf() for _ in range(2)]
        sq = al([512], F32)
        t1 = al([512], F32)
        t2 = al([512], F32)
        b_sq, b_t1, b_t2 = Buf(), Buf(), Buf()
        cs = [al([2, 512], F32) for _ in range(2)]
        b_cs = [Buf() for _ in range(2)]
        stg = [al([NF], BF16) for _ in range(2)]
        b_stg = [Buf() for _ in range(2)]
        vst = [al([1024], BF16) for _ in range(2)]
        b_vst = [Buf() for _ in range(2)]
        sctr = [0]
        pctr = [0]

        def next_bank(lo=0, n=4):
            k = lo + pctr[0] % n
            pctr[0] += 1
            return k

        def fm_proj(wsrc, col0, nheads, epi, dst_d):
            nblk = (nheads + 3) // 4
            pending = [None]

            def flush():
                if pending[0] is not None:
                    pending[0]()
                    pending[0] = None
            nxt = wload(wsrc, col0)
            for bi in range(nblk):
                wt, wb = nxt
                if bi + 1 < nblk:
                    nxt = wload(wsrc, col0 + (bi + 1) * 512)
                for hh in range(4):
                    hd = bi * 4 + hh
                    si = sctr[0] % 2
                    sctr[0] += 1
                    for ti, (t0, n) in enumerate(FT):
                        pk = next_bank(0, 4)

                        def mm(e, wt=wt, hh=hh, t0=t0, n=n, pk=pk):
                            ins = None
                            for c in range(16):
                                ins = e.matmul(bank(pk)[:, 0:n], lhsT=wt[:, c, hh * 128:(hh + 1) * 128],
                                               rhs=hT[:, c, t0:t0 + n], start=(c == 0), stop=(c == 15))
                            return ins
                        P.op("pe", mm, reads=[wb] + b_hTc, writes=[pbuf[pk]])
                        flush()

                        def ep(pk=pk, t0=t0, n=n, si=si, hd=hd, last=(ti == len(FT) - 1)):
                            epi(pk, t0, n, stg[si], b_stg[si])
                            if last:
                                P.dma("sp", lambda e: e.dma_start(
                                    out=dst_d[hd, :, tok_off:tok_off + NF] if dst_d is K_d else dst_d[hd],
                                    in_=stg[si]), reads=[b_stg[si]])
                        pending[0] = ep
            flush()

        def epi_act(func):
            def epi(pk, t0, n, sg_ap, sg_b):
                P.op("act", lambda e: e.activation(out=sg_ap[:, t0:t0 + n], in_=bank(pk)[:, 0:n], func=func),
                     reads=[pbuf[pk]], writes=[sg_b])
            return epi

        qk_ctr = [0]

        def epi_qk(gcol):
            def epi(pk, t0, n, sg_ap, sg_b):
                i = qk_ctr[0] % 2
                qk_ctr[0] += 1
                P.dma("sp", lambda e: e.dma_start(out=cs[i][:, 0, 0:n], in_=cos_d[:, tok_off + t0:tok_off + t0 + n]),
                      writes=[b_cs[i]])
                P.dma("sp", lambda e: e.dma_start(out=cs[i][:, 1, 0:n], in_=sin_d[:, tok_off + t0:tok_off + t0 + n]),
                      writes=[b_cs[i]])
                P.op("act", lambda e: e.activation(out=kg[i][:, 0:n], in_=bank(pk)[:, 0:n], func=AF.Copy, scale=gcol),
                     reads=[pbuf[pk], b_const], writes=[b_kg[i]])
                P.op("act", lambda e: e.activation(out=ksq[i][:, 0:n], in_=bank(pk)[:, 0:n], func=AF.Square),
                     reads=[pbuf[pk]], writes=[b_ksq[i]])
                p2 = next_bank(4, 4)
                p3 = next_bank(4, 4)
                P.op("pe", lambda e: e.matmul(bank(p2)[:, 0:n], lhsT=blk64, rhs=ksq[i][:, 0:n], start=True, stop=True),
                     reads=[b_ksq[i], b_const], writes=[pbuf[p2]])
                P.op("pe", lambda e: e.matmul(bank(p3)[:, 0:n], lhsT=rotT, rhs=kg[i][:, 0:n], start=True, stop=True),
                     reads=[b_kg[i], b_const], writes=[pbuf[p3]])
                P.op("act", lambda e: e.activation(out=sq[:, 0:n], in_=bank(p2)[:, 0:n], func=AF.Ln, bias=epsc,
                                                   scale=1.0 / 64), reads=[pbuf[p2], b_const], writes=[b_sq])
                P.op("act", lambda e: e.activation(out=sq[:, 0:n], in_=sq[:, 0:n], func=AF.Exp, scale=-0.5),
                     reads=[b_sq], writes=[b_sq])
                P.op("dve", lambda e: e.tensor_tensor(out=t1[:, 0:n], in0=kg[i][:, 0:n], in1=cs[i][:, 0, 0:n],
                                                      op=ALU.mult), reads=[b_kg[i], b_cs[i]], writes=[b_t1])
                P.op("dve", lambda e: e.tensor_tensor(out=t2[:, 0:n], in0=bank(p3)[:, 0:n], in1=cs[i][:, 1, 0:n],
                                                      op=ALU.mult), reads=[pbuf[p3], b_cs[i]], writes=[b_t2])
                P.op("dve", lambda e: e.tensor_tensor(out=t1[:, 0:n], in0=t1[:, 0:n], in1=t2[:, 0:n], op=ALU.add),
                     reads=[b_t1, b_t2], writes=[b_t1])
                P.op("dve", lambda e: e.tensor_tensor(out=sg_ap[:, t0:t0 + n], in0=t1[:, 0:n], in1=sq[:, 0:n],
                                                      op=ALU.mult), reads=[b_t1, b_sq], writes=[sg_b])
            return epi

        def tm_proj(wsrc, col0, epi_tm):
            wA = wload(wsrc, col0)
            wB = wload(wsrc, col0 + 512)
            for t in range(17):
                for bi, (wt, wb) in enumerate((wA, wB)):
                    pk = next_bank(0, 4)

                    def mm(e, wt=wt, t=t, pk=pk):
                        ins = None
                        for c in range(16):
                            ins = e.matmul(bank(pk), lhsT=hT[:, c, t * 128:(t + 1) * 128], rhs=wt[:, c, :],
                                           start=(c == 0), stop=(c == 15))
                        return ins
                    P.op("pe", mm, reads=[wb] + b_hTc, writes=[pbuf[pk]])
                    epi_tm(t, bi, pk)

        def epi_v(t, bi, pk):
            i = t % 2
            P.op("act", lambda e: e.activation(out=vst[i][:, bi * 512:(bi + 1) * 512], in_=bank(pk), func=AF.Copy),
                 reads=[pbuf[pk]], writes=[b_vst[i]])
            if bi == 1:
                g = tok_off + t * 128
                P.dma("sp", lambda e: e.dma_start(out=V_d[g:g + 128, :], in_=vst[i]), reads=[b_vst[i]])

        w_in0 = W["e_w_in"]
        if which != "AV":
            fm_proj(w_in0, 4096, 8, epi_qk(gk), K_d)
            tm_proj(w_in0, 5120, epi_v)
        if which == "A":
            fm_proj(w_in0, 3072, 8, epi_qk(gq), Q_d)
            fm_proj(w_in0, 6144, 8, epi_act(AF.Silu), G_d)
            fm_proj(w_in0, 0, 8, epi_act(AF.Gelu), AU_d)
            fm_proj(w_in0, 2048, 8, epi_act(AF.Silu), AG_d)
        if which == "AV":
            gv = [AR.alloc([1024], F32) for _ in range(2)]
            b_gv = [Buf() for _ in range(2)]
            vjunk = AR.alloc([1024], BF16)
            b_vjunk = Buf()
            ssv = AR.alloc([64], F32)
            b_ssvt = [Buf() for _ in range(17)]

            def epi_av(t, bi, pk):
                i = t % 2
                P.op("act", lambda e: e.activation(out=gv[i][:, bi * 512:(bi + 1) * 512], in_=bank(pk), func=AF.Gelu),
                     reads=[pbuf[pk]], writes=[b_gv[i]])
                if bi == 1:
                    P.op("act", lambda e: e.activation(out=vjunk, in_=gv[i], func=AF.Square,
                                                       accum_out=ssv[:, t:t + 1]),
                         reads=[b_gv[i]], writes=[b_vjunk, b_ssvt[t]])
                    P.op("act", lambda e: e.activation(out=ssv[:, 32 + t:33 + t], in_=ssv[:, t:t + 1], func=AF.Sqrt,
                                                       bias=epsc, scale=1.0 / 1024), reads=[b_ssvt[t], b_const],
                         writes=[b_ssvt[t]])
                    P.op("dve", lambda e: e.reciprocal(out=ssv[:, 32 + t:33 + t], in_=ssv[:, 32 + t:33 + t]),
                         reads=[b_ssvt[t]], writes=[b_ssvt[t]])
                    P.op("dve", lambda e: e.scalar_tensor_tensor(out=vn_all[:, t, :], in0=gv[i],
                                                                 scalar=ssv[:, 32 + t:33 + t], in1=vng,
                                                                 op0=ALU.mult, op1=ALU.mult),
                         reads=[b_gv[i], b_ssvt[t], b_const], writes=[b_vn])
            tm_proj(w_in0, 1024, epi_av)

    vn_all = None
    b_vn = Buf("vn")
    for which in ("B", "A"):
        AR.reset(m_h)
        base_tile = 17 if which == "B" else 0
        xprep(0, dram_tile_src(xc, base_tile), 17,
              (lambda t: 1 if (which == "B" and t >= 15) else 0))
        if debug and which == "A":
            P.dma("sp", lambda e: e.dma_start(out=HT_d.rearrange("c p t -> p c t"), in_=hT), reads=b_hTc)
        P.barrier()
        AR.reset(m_h)
        proj_phase(which)
        P.barrier()
        if which == "A":
            AR.reset(m_h)
            vn_all = AR.alloc([17, 1024], BF16)
            proj_phase("AV")
            P.barrier()
        if stop_after == "P2" + which:
            return _finish(nc, P)
    m_vn = m_h + 17 * 512
    AR.reset(m_vn)

    def mix_phase():
        au = [AR.alloc([NF], BF16) for _ in range(2)]
        ag = [AR.alloc([NF], BF16) for _ in range(2)]
        b_au = [Buf() for _ in range(2)]
        b_ag = [Buf() for _ in range(2)]
        ystg = [AR.alloc([NF], BF16) for _ in range(2)]
        b_ystg = [Buf() for _ in range(2)]
        pc = [0]

        def load(g):
            i = g % 2
            P.dma("sp", lambda e: e.dma_start(out=au[i], in_=AU_d[g]), writes=[b_au[i]])
            P.dma("sp", lambda e: e.dma_start(out=ag[i], in_=AG_d[g]), writes=[b_ag[i]])
        load(0)
        for g in range(8):
            i = g % 2
            if g + 1 < 8:
                load(g + 1)
            P.op("dve", lambda e, i=i: e.tensor_tensor(out=au[i], in0=au[i], in1=ag[i], op=ALU.mult),
                 reads=[b_au[i], b_ag[i]], writes=[b_au[i]])
            for nb in range(5):
                tl = list(range(nb * 4, min(17, nb * 4 + 4)))
                nt = len(tl)
                pk = pc[0] % 4
                pc[0] += 1

                def mm(e, g=g, tl=tl, pk=pk):
                    ins = None
                    for sl, n in enumerate(tl):
                        o = bank(pk)[:, sl * 128:(sl + 1) * 128]
                        e.matmul(o, lhsT=vn_all[:, n, g * 128:(g + 1) * 128], rhs=wsTb[:, g, :], start=True, stop=False)
                        e.matmul(o, lhsT=onesb[0:1, :], rhs=bsh[0:1, g, :], start=False, stop=False)
                        ins = e.matmul(o, lhsT=onesb[0:1, :], rhs=bsl[0:1, g, :], start=False, stop=True)
                    return ins
                P.op("pe", mm, reads=[b_vn, b_const], writes=[pbuf[pk]])
                P.op("dve", lambda e, i=i, nt=nt, t0=tl[0], pk=pk: e.tensor_tensor(
                    out=ystg[i][:, t0 * 128:(t0 + nt) * 128], in0=bank(pk)[:, 0:nt * 128],
                    in1=au[i][:, t0 * 128:(t0 + nt) * 128], op=ALU.mult),
                    reads=[pbuf[pk], b_au[i]], writes=[b_ystg[i]])
            P.dma("sp", lambda e, g=g, i=i: e.dma_start(out=Y_d[g], in_=ystg[i]), reads=[b_ystg[i]], writes=[b_Y])
    b_Y = Buf("Y_d")
    mix_phase()
    P.barrier()
    if stop_after == "P3":
        return _finish(nc, P)

    AR.reset(m_pre_h)

    def attn_phase():
        kT = [AR.alloc([NTOK], BF16) for _ in range(2)]
        vh = [AR.alloc([34, 128], BF16) for _ in range(2)]
        qT = [AR.alloc([NF], BF16) for _ in range(2)]
        sbg = [AR.alloc([NF], BF16) for _ in range(2)]
        b_in = [Buf() for _ in range(2)]
        pT = [AR.alloc([2, 512], BF16) for _ in range(3)]
        b_pT = [Buf() for _ in range(3)]
        rz = AR.alloc([2, 512], F32)
        o1 = AR.alloc([512], F32)
        o2 = AR.alloc([512], F32)
        rs = AR.alloc([512], F32)
        osq = AR.alloc([512], BF16)
        b_rz, b_o1, b_o2, b_rs, b_osq = Buf(), Buf(), Buf(), Buf(), Buf()
        ystg = [AR.alloc([NF], BF16) for _ in range(2)]
        b_ystg = [Buf() for _ in range(2)]
        b_s = [Buf(), Buf()]
        b_o = [pbuf[4], pbuf[5]]
        b_z = pbuf[6]
        zs = AR.alloc([512], F32)
        b_zs = Buf()
        zh = AR.alloc([512], BF16)
        zl = AR.alloc([512], BF16)
        b_zh, b_zl = Buf(), Buf()
        sctr = [0]
        pctr = [0]

        def load_head(hd):
            i = hd % 2
            P.dma("sp", lambda e: e.dma_start(out=kT[i], in_=K_d[hd]), writes=[b_in[i]])
            P.dma("sp", lambda e: e.dma_start(
                out=vh[i], in_=V_d[:, hd * 128:(hd + 1) * 128].rearrange("(t p) d -> p t d", p=128)),
                writes=[b_in[i]])
            P.dma("sp", lambda e: e.dma_start(out=qT[i], in_=Q_d[hd]), writes=[b_in[i]])
            P.dma("sp", lambda e: e.dma_start(out=sbg[i], in_=G_d[hd]), writes=[b_in[i]])

        pend = []

        def flush_one():
            if pend:
                pend.pop(0)()

        its = [(hd, qi, kt) for hd in range(8) for qi in range(len(FT)) for kt in range(34)]
        sis = {}

        def issue_s(j):
            hd, qi, kt = its[j]
            i = hd % 2
            t0, n = FT[qi]
            si = sctr[0] % 2
            sctr[0] += 1
            sis[j] = si

            def f(e):
                e.matmul(pp[si][:, 0, 0:n], lhsT=kT[i][0:64, kt * 128:(kt + 1) * 128],
                         rhs=qT[i][0:64, t0:t0 + n], start=True, stop=True)
                return e.matmul(pp[si][:, 1, 0:n], lhsT=kT[i][64:128, kt * 128:(kt + 1) * 128],
                                rhs=qT[i][64:128, t0:t0 + n], start=True, stop=True)
            P.op("pe", f, reads=[b_in[i]], writes=[b_s[si]])

        def epilogue(hd, qi):
            i = hd % 2
            t0, n = FT[qi]
            P.op("dve", lambda e: e.tensor_copy(out=o1[:, 0:n], in_=bank(4)[:, 0:n]), reads=[b_o[0]], writes=[b_o1])
            P.op("dve", lambda e: e.tensor_copy(out=o2[:, 0:n], in_=bank(5)[:, 0:n]), reads=[b_o[1]], writes=[b_o2])
            P.op("dve", lambda e: e.tensor_copy(out=zs[0:64, 0:n], in_=bank(6)[0:64, 0:n]), reads=[b_z], writes=[b_zs])
            lazy_step(7)

            def stage_0():
                P.op("dve", lambda e: e.reciprocal(out=zs[0:64, 0:n], in_=zs[0:64, 0:n]), reads=[b_zs], writes=[b_zs])
                P.op("dve", lambda e: e.tensor_copy(out=zh[0:64, 0:n], in_=zs[0:64, 0:n]), reads=[b_zs], writes=[b_zh])
                P.op("dve", lambda e: e.tensor_tensor(out=zl[0:64, 0:n], in0=zs[0:64, 0:n], in1=zh[0:64, 0:n],
                                                      op=ALU.subtract), reads=[b_zs, b_zh], writes=[b_zl])

            def zbc(row):
                def f(e):
                    e.matmul(bank(7)[:, 0:n], lhsT=onesb[row:row + 1, :], rhs=zh[row:row + 1, 0:n],
                             start=True, stop=False)
                    return e.matmul(bank(7)[:, 0:n], lhsT=onesb[row:row + 1, :], rhs=zl[row:row + 1, 0:n],
                                    start=False, stop=True)
                return f

            def stage_a():
                P.op("pe", zbc(0), reads=[b_zh, b_zl, b_const], writes=[pbuf[7]])
                P.op("dve", lambda e: e.tensor_tensor(out=o1[:, 0:n], in0=o1[:, 0:n], in1=bank(7)[:, 0:n],
                                                      op=ALU.mult), reads=[b_o1, pbuf[7]], writes=[b_o1])

            def stage_b():
                P.op("pe", zbc(32), reads=[b_zh, b_zl, b_const], writes=[pbuf[7]])
                P.op("dve", lambda e: e.tensor_tensor(out=o2[:, 0:n], in0=o2[:, 0:n], in1=bank(7)[:, 0:n],
                                                      op=ALU.mult), reads=[b_o2, pbuf[7]], writes=[b_o2])
                P.op("dve", lambda e: e.scalar_tensor_tensor(out=o1[:, 0:n], in0=o2[:, 0:n], scalar=nlam,
                                                             in1=o1[:, 0:n], op0=ALU.mult, op1=ALU.add),
                     reads=[b_o1, b_o2, b_const], writes=[b_o1])
                P.op("dve", lambda e: e.tensor_tensor(out=osq[:, 0:n], in0=o1[:, 0:n], in1=o1[:, 0:n], op=ALU.mult),
                     reads=[b_o1], writes=[b_osq])

            def stage_c1():
                P.op("pe", lambda e: e.matmul(bank(7)[:, 0:n], lhsT=onesb, rhs=osq[:, 0:n], start=True, stop=True),
                     reads=[b_osq, b_const], writes=[pbuf[7]])

            def stage_c():
                P.op("act", lambda e: e.activation(out=rs[:, 0:n], in_=bank(7)[:, 0:n], func=AF.Ln,
                                                   bias=epsc, scale=1.0 / 128),
                     reads=[pbuf[7], b_const], writes=[b_rs])
                P.op("act", lambda e: e.activation(out=rs[:, 0:n], in_=rs[:, 0:n], func=AF.Exp, scale=-0.5),
                     reads=[b_rs], writes=[b_rs])
                P.op("dve", lambda e: e.tensor_tensor(out=o1[:, 0:n], in0=o1[:, 0:n], in1=rs[:, 0:n],
                                                      op=ALU.mult), reads=[b_o1, b_rs], writes=[b_o1])
                P.op("dve", lambda e: e.scalar_tensor_tensor(
                    out=ystg[i][:, t0:t0 + n], in0=o1[:, 0:n], scalar=gon, in1=sbg[i][:, t0:t0 + n],
                    op0=ALU.mult, op1=ALU.mult), reads=[b_o1, b_in[i], b_const], writes=[b_ystg[i]])
                if qi == len(FT) - 1:
                    P.dma("sp", lambda e: e.dma_start(out=Y_d[8 + hd], in_=ystg[i]),
                          reads=[b_ystg[i]], writes=[b_Y])
            pend.extend([stage_0, stage_a, stage_b, stage_c1, stage_c])

        load_head(0)
        issue_s(0)
        issue_s(1)
        for j, (hd, qi, kt) in enumerate(its):
            i = hd % 2
            t0, n = FT[qi]
            si = sis[j]
            pi = pctr[0] % 3
            pctr[0] += 1
            P.op("act", lambda e, si=si, pi=pi, n=n: e.activation(
                out=pT[pi][:, :, 0:n], in_=pp[si][:, :, 0:n], func=AF.Exp, scale=0.125),
                reads=[b_s[si]], writes=[b_pT[pi]])
            if j + 2 < len(its):
                issue_s(j + 2)

            def pv(e, kt=kt, pi=pi, i=i, n=n):
                st, sp_ = (kt == 0), (kt == 33)
                e.matmul(bank(4)[:, 0:n], lhsT=vh[i][:, kt, :], rhs=pT[pi][:, 0, 0:n], start=st, stop=sp_)
                e.matmul(bank(5)[:, 0:n], lhsT=vh[i][:, kt, :], rhs=pT[pi][:, 1, 0:n], start=st, stop=sp_)
                e.matmul(bank(6)[0:32, 0:n], lhsT=onesb[:, 0:32], rhs=pT[pi][:, 0, 0:n], start=st, stop=sp_,
                         tile_position=(0, 0))
                return e.matmul(bank(6)[32:64, 0:n], lhsT=onesb[:, 0:32], rhs=pT[pi][:, 1, 0:n], start=st,
                                stop=sp_, tile_position=(0, 32))
            P.op("pe", pv, reads=[b_pT[pi], b_in[i], b_const], writes=[b_o[0], b_o[1], b_z])
            if kt in (3, 8, 13, 18, 21):
                flush_one()
            if kt == 23 and qi == 0 and hd + 1 < 8:
                load_head(hd + 1)
            if kt == 33:
                epilogue(hd, qi)
        while pend:
            flush_one()
    attn_phase()
    while lazy_step(7):
        pass
    if debug:
        for l in range(2):
            P.dma("sp", lambda e, l=l: e.dma_start(out=MOD_d[:, l * 96:(l + 1) * 96],
                                                   in_=mT[l].rearrange("p v c -> p (v c)")), reads=[b_mod])
            P.dma("sp", lambda e, l=l: e.dma_start(out=MOD_d[:, 192 + l * 32:192 + (l + 1) * 32],
                                                   in_=Gp[l].rearrange("p v c -> p (v c)")), reads=[b_mod])
            P.dma("sp", lambda e, l=l: e.dma_start(out=GT_d[l], in_=gt_bc[l]), reads=[b_mod])
    P.barrier()
    if stop_after == "P4":
        return _finish(nc, P)

    def outproj(l, w_out, ntiles, resid_d, dst_d, b_dst):
        xr = [AR.alloc([512], F32) for _ in range(4)]
        b_xr = [Buf() for _ in range(4)]
        t1 = [AR.alloc([512], F32) for _ in range(3)]
        b_t1 = [Buf() for _ in range(3)]
        units = [(j, t) for j in range(4) for t in range(ntiles)]

        def load(u):
            j, t = units[u]
            xi = u % 4
            P.dma("sp", lambda e: e.dma_start(
                out=xr[xi], in_=resid_d[t * 128:(t + 1) * 128, j * 512:(j + 1) * 512]), writes=[b_xr[xi]])
        load(0)
        load(1)
        nxt = wload(w_out, 0)
        wt = wb = None
        for u, (j, t) in enumerate(units):
            if t == 0:
                wt, wb = nxt
                if j + 1 < 4:
                    nxt = wload(w_out, (j + 1) * 512)
            if u + 2 < len(units):
                load(u + 2)
            pk = u % 4
            xi = u % 4
            ti = u % 3

            def mm(e, wt=wt, t=t, pk=pk):
                ins = None
                for c in range(16):
                    ins = e.matmul(bank(pk), lhsT=hT[:, c, t * 128:(t + 1) * 128], rhs=wt[:, c, :],
                                   start=(c == 0), stop=(c == 15))
                return ins
            P.op("pe", mm, reads=[wb] + b_hTc, writes=[pbuf[pk]])
            P.op("dve", lambda e, pk=pk, ti=ti, j=j: e.tensor_tensor(
                out=t1[ti], in0=bank(pk), in1=gt_bc[l][:, j * 512:(j + 1) * 512], op=ALU.mult),
                reads=[pbuf[pk], b_mod], writes=[b_t1[ti]])
            P.op("dve", lambda e, ti=ti, xi=xi: e.tensor_tensor(out=t1[ti], in0=t1[ti], in1=xr[xi], op=ALU.add),
                 reads=[b_t1[ti], b_xr[xi]], writes=[b_t1[ti]])
            P.dma("act", lambda e, ti=ti, t=t, j=j: e.dma_start(
                out=dst_d[t * 128:(t + 1) * 128, j * 512:(j + 1) * 512], in_=t1[ti]),
                reads=[b_t1[ti]], writes=[b_dst[t]])

    AR.reset(m_h)
    for c in range(16):
        P.dma("sp", lambda e, c=c: e.dma_start(out=hT[:, c, :], in_=Y_d[c]), reads=[b_Y], writes=[b_hTc[c]])
    b_X1 = [Buf("x1_%d" % t) for t in range(17)]
    outproj(0, W["e_w_out"], 17, xc, X1_d, b_X1)
    P.barrier()
    if stop_after == "P5":
        return _finish(nc, P)

    AR.reset(m_h)
    xprep(1, dram_tile_src(X1_d, 0, b_X1), 17, (lambda t: 0))
    if debug:
        P.dma("sp", lambda e: e.dma_start(out=HT_d.rearrange("c p t -> p c t"), in_=hT), reads=b_hTc)
    P.barrier()
    if stop_after == "P6":
        return _finish(nc, P)

    AR.reset(m_h)
    acc_s = AR.alloc([NOWN], F32)
    acc_q = AR.alloc([NOWN], F32)
    b_acc = Buf("acc")
    m_acc = AR.mark()
    GL = 15 + NOWN + 15
    w_in1 = W["o_w_in"]
    glub = [AR.alloc([GL + 2], BF16) for _ in range(2)]
    b_glub = [Buf() for _ in range(2)]
    sig = [AR.alloc([512], F32) for _ in range(2)]
    b_sig = [Buf() for _ in range(2)]
    gh = AR.alloc([128], F32)
    b_gh = Buf()
    sgs = AR.alloc([NOWN], BF16)
    b_sgs = Buf()
    NPE = 13
    dg2 = [AR.alloc([NPE, 128], BF16) for _ in range(2)]
    b_dg2 = [Buf() for _ in range(2)]
    dacc = AR.alloc([NOWN], F32)
    b_dacc = Buf()
    ych = [AR.alloc([NOWN], F32) for _ in range(2)]
    b_ych = [Buf() for _ in range(2)]
    sq5 = [AR.alloc([512], F32) for _ in range(2)]
    b_sq5 = [Buf() for _ in range(2)]
    P.op("pool", lambda e: e.memset(acc_s, 0.0), writes=[b_acc])
    P.op("pool", lambda e: e.memset(acc_q, 0.0), writes=[b_acc])
    b_YC = [Buf() for _ in range(16)]
    b_SG = [Buf() for _ in range(16)]

    def w3load(cc):
        i = wctr[0] % NWB
        wctr[0] += 1
        for k3 in range(3):
            dst = wblk[i][:, :, k3 * 128:(k3 + 1) * 128]
            src = w_in1[:, k3 * 2048 + cc * 128:k3 * 2048 + (cc + 1) * 128].rearrange("(c p) n -> p c n", p=128)
            P.dma("pool", lambda e, dst=dst, src=src: e.dma_start(out=dst, in_=src), writes=[b_wblk[i]])
        return wblk[i], b_wblk[i]

    nxt = w3load(0)
    ctr = 0
    tap_q = []
    fin_q = []

    def drain_taps(k):
        for _ in range(min(k, len(tap_q))):
            tap_q.pop(0)()

    for cc in range(16):
        wt, wb = nxt
        if cc + 1 < 16:
            nxt = w3load(cc + 1)
        gi = cc % 2
        dg = dg2[gi]
        b_dg = b_dg2[gi]
        for tap in range(NPE):
            P.op("act", lambda e, tap=tap, cc=cc, dg=dg: e.activation(
                out=dg[:, tap, :], in_=identf, func=AF.Copy, scale=dww[:, cc, tap:tap + 1]),
                reads=[b_const], writes=[b_dg])
        for (t0, n) in FT:
            is_halo = (t0 == NOWN)
            pa, pb_, pg = (ctr * 3) % 6, (ctr * 3 + 1) % 6, (ctr * 3 + 2) % 6
            si = ctr % 2
            ctr += 1

            def mm3(e, wt=wt, t0=t0, n=n, pa=pa, pb_=pb_, pg=pg, is_halo=is_halo):
                ins = None
                for k3, pk in enumerate((pa, pb_, pg)):
                    if k3 == 2 and is_halo:
                        continue
                    for c in range(16):
                        ins = e.matmul(bank(pk)[:, 0:n], lhsT=wt[:, c, k3 * 128:(k3 + 1) * 128],
                                       rhs=hT[:, c, t0:t0 + n], start=(c == 0), stop=(c == 15))
                return ins
            P.op("pe", mm3, reads=[wb] + b_hTc, writes=[pbuf[pa], pbuf[pb_]] + ([] if is_halo else [pbuf[pg]]))
            P.op("act", lambda e, si=si, pb_=pb_, n=n: e.activation(out=sig[si][:, 0:n], in_=bank(pb_)[:, 0:n],
                                                                    func=AF.Sigmoid),
                 reads=[pbuf[pb_]], writes=[b_sig[si]])
            if not is_halo:
                P.op("dve", lambda e, si=si, pa=pa, n=n, t0=t0, gi=gi: e.tensor_tensor(
                    out=glub[gi][:, 15 + t0:15 + t0 + n], in0=bank(pa)[:, 0:n], in1=sig[si][:, 0:n], op=ALU.mult),
                    reads=[pbuf[pa], b_sig[si]], writes=[b_glub[gi]])
                P.op("act", lambda e, pg=pg, n=n, t0=t0: e.activation(out=sgs[:, t0:t0 + n], in_=bank(pg)[:, 0:n],
                                                                      func=AF.Silu),
                     reads=[pbuf[pg]], writes=[b_sgs])
            else:
                P.op("dve", lambda e, si=si, pa=pa: e.tensor_tensor(out=gh, in0=bank(pa)[:, 0:128],
                                                                    in1=sig[si][:, 0:128], op=ALU.mult),
                     reads=[pbuf[pa], b_sig[si]], writes=[b_gh])
                P.op("dve", lambda e, gi=gi: e.tensor_scalar(out=glub[gi][:, 0:15], in0=gh[:, 113:128],
                                                             scalar1=halo[:, 0:1], scalar2=None, op0=ALU.mult),
                     reads=[b_gh, b_const], writes=[b_glub[gi]])
                P.op("dve", lambda e, gi=gi: e.tensor_scalar(out=glub[gi][:, 15 + NOWN:30 + NOWN], in0=gh[:, 0:15],
                                                             scalar1=halo[:, 1:2], scalar2=None, op0=ALU.mult),
                     reads=[b_gh, b_const], writes=[b_glub[gi]])
            drain_taps(5)
        drain_taps(99)
        while fin_q:
            fin_q.pop(0)()
        P.dma("sp", lambda e, cc=cc: e.dma_start(out=SG_d[cc], in_=sgs), reads=[b_sgs], writes=[b_SG[cc]])
        tap_q.append(lambda cc=cc, gi=gi: P.op("dve", lambda e: e.tensor_scalar(
            out=dacc, in0=glub[gi][:, NPE:NPE + NOWN], scalar1=dww[:, cc, NPE:NPE + 1], scalar2=None, op0=ALU.mult),
            reads=[b_glub[gi], b_const], writes=[b_dacc]))
        for tap in range(NPE + 1, 31):
            tap_q.append(lambda cc=cc, gi=gi, tap=tap: P.op("dve", lambda e: e.scalar_tensor_tensor(
                out=dacc, in0=glub[gi][:, tap:tap + NOWN], scalar=dww[:, cc, tap:tap + 1], in1=dacc,
                op0=ALU.mult, op1=ALU.add), reads=[b_glub[gi], b_const, b_dacc], writes=[b_dacc]))
        yi = cc % 2
        for j4 in range(4):
            pk = 6 + j4 % 2

            def cv(e, j4=j4, pk=pk, gi=gi, dg=dg):
                ins = None
                for tap in range(NPE):
                    ins = e.matmul(bank(pk), lhsT=dg[:, tap, :], rhs=glub[gi][:, j4 * 512 + tap:j4 * 512 + tap + 512],
                                   start=(tap == 0), stop=(tap == NPE - 1))
                return ins
            P.op("pe", cv, reads=[b_dg, b_glub[gi]], writes=[pbuf[pk]])
            P.op("act", lambda e, j4=j4, pk=pk, yi=yi, cc=cc: e.activation(
                out=ych[yi][:, j4 * 512:(j4 + 1) * 512], in_=bank(pk), func=AF.Identity, bias=dwb[:, cc:cc + 1]),
                reads=[pbuf[pk], b_const], writes=[b_ych[yi]])

        def finish(cc=cc, yi=yi):
            for j4 in range(4):
                qi = j4 % 2
                sl = slice(j4 * 512, (j4 + 1) * 512)
                P.op("pool", lambda e, sl=sl: e.tensor_tensor(out=ych[yi][:, sl], in0=ych[yi][:, sl], in1=dacc[:, sl],
                                                              op=ALU.add),
                     reads=[b_ych[yi], b_dacc], writes=[b_ych[yi]])
                P.op("act", lambda e, sl=sl, qi=qi: e.activation(out=sq5[qi], in_=ych[yi][:, sl], func=AF.Square),
                     reads=[b_ych[yi]], writes=[b_sq5[qi]])
                P.op("pool", lambda e, sl=sl: e.tensor_tensor(out=acc_s[:, sl], in0=acc_s[:, sl], in1=ych[yi][:, sl],
                                                              op=ALU.add), reads=[b_ych[yi], b_acc], writes=[b_acc])
                P.op("pool", lambda e, sl=sl, qi=qi: e.tensor_tensor(out=acc_q[:, sl], in0=acc_q[:, sl], in1=sq5[qi],
                                                                     op=ALU.add), reads=[b_sq5[qi], b_acc],
                     writes=[b_acc])
            P.dma("sp", lambda e: e.dma_start(out=YC_d[cc], in_=ych[yi]), reads=[b_ych[yi]], writes=[b_YC[cc]])
        fin_q.append(finish)
    drain_taps(99)
    while fin_q:
        fin_q.pop(0)()
    P.barrier()
    if stop_after == "P7":
        return _finish(nc, P)

    AR.reset(m_acc)
    mean_bc = AR.alloc([NOWN], F32)
    rstd_bc = AR.alloc([NOWN], F32)
    b_stat = Buf("lnstat")
    tmpv = AR.alloc([512], F32)
    b_tmpv = Buf()
    for j4 in range(4):
        sl = slice(j4 * 512, (j4 + 1) * 512)
        P.op("pe", lambda e, sl=sl: e.matmul(bank(0), lhsT=onesf, rhs=acc_s[:, sl], start=True, stop=True),
             reads=[b_acc, b_const], writes=[pbuf[0]])
        P.op("pe", lambda e, sl=sl: e.matmul(bank(1), lhsT=onesf, rhs=acc_q[:, sl], start=True, stop=True),
             reads=[b_acc, b_const], writes=[pbuf[1]])
        P.op("dve", lambda e, sl=sl: e.tensor_scalar(out=mean_bc[:, sl], in0=bank(0), scalar1=1.0 / D, scalar2=None,
                                                     op0=ALU.mult), reads=[pbuf[0]], writes=[b_stat])
        P.op("dve", lambda e, sl=sl: e.tensor_tensor(out=tmpv, in0=mean_bc[:, sl], in1=mean_bc[:, sl], op=ALU.mult),
             reads=[b_stat], writes=[b_tmpv])
        P.op("dve", lambda e, sl=sl: e.scalar_tensor_tensor(out=tmpv, in0=bank(1), scalar=1.0 / D, in1=tmpv,
                                                            op0=ALU.mult, op1=ALU.subtract),
             reads=[pbuf[1], b_tmpv], writes=[b_tmpv])
        P.op("act", lambda e, sl=sl: e.activation(out=rstd_bc[:, sl], in_=tmpv, func=AF.Sqrt, bias=epsc),
             reads=[b_tmpv, b_const], writes=[b_stat])
        P.op("dve", lambda e, sl=sl: e.reciprocal(out=rstd_bc[:, sl], in_=rstd_bc[:, sl]), reads=[b_stat],
             writes=[b_stat])
    yl = [AR.alloc([NOWN], F32) for _ in range(2)]
    sl_ = [AR.alloc([NOWN], BF16) for _ in range(2)]
    b_yl = [Buf() for _ in range(2)]
    b_sl = [Buf() for _ in range(2)]
    for cc in range(16):
        i = cc % 2
        P.dma("sp", lambda e, cc=cc, i=i: e.dma_start(out=yl[i], in_=YC_d[cc]), reads=[b_YC[cc]], writes=[b_yl[i]])
        P.dma("sp", lambda e, cc=cc, i=i: e.dma_start(out=sl_[i], in_=SG_d[cc]), reads=[b_SG[cc]], writes=[b_sl[i]])
        P.op("dve", lambda e, i=i: e.tensor_tensor(out=yl[i], in0=yl[i], in1=mean_bc, op=ALU.subtract),
             reads=[b_yl[i], b_stat], writes=[b_yl[i]])
        P.op("dve", lambda e, i=i: e.tensor_tensor(out=yl[i], in0=yl[i], in1=rstd_bc, op=ALU.mult),
             reads=[b_yl[i], b_stat], writes=[b_yl[i]])
        P.op("act", lambda e, i=i, cc=cc: e.activation(out=yl[i], in_=yl[i], func=AF.Silu, bias=lnb[:, cc:cc + 1],
                                                       scale=lng[:, cc:cc + 1]),
             reads=[b_yl[i], b_const], writes=[b_yl[i]])
        P.op("dve", lambda e, i=i, cc=cc: e.tensor_tensor(out=hT[:, cc, 0:NOWN], in0=yl[i], in1=sl_[i], op=ALU.mult),
             reads=[b_yl[i], b_sl[i]], writes=[b_hTc[cc]])
    P.barrier()
    if stop_after == "P7b":
        return _finish(nc, P)

    AR.reset(m_h)
    b_out = [Buf("out%d" % t) for t in range(16)]
    outproj(1, W["o_w_out"], 16, X1_d, out_d, b_out)
    return _finish(nc, P)


def _finish(nc, P):
    P.barrier()
    P.emit()
    P.close()
    return nc


def _rope_tables(pos):
    axis_dim = 32
    inv = (10000.0 ** (-np.arange(0, axis_dim, 2, dtype=np.float32) / np.float32(axis_dim))).astype(np.float32)
    row = (pos // 64).astype(np.float32)
    col = (pos % 64).astype(np.float32)
    ar = row[:, None] * inv[None, :]
    ac = col[:, None] * inv[None, :]
    ang = np.concatenate([ar, ar, ac, ac], axis=-1).astype(np.float32)
    return np.cos(ang).astype(np.float32), np.sin(ang).astype(np.float32)


def _consts():
    ident = np.eye(128, dtype=np.float32)
    R = np.zeros((64, 64), np.float32)
    for i in range(64):
        blk = i // 16
        if blk % 2 == 0:
            R[i, i + 16] = -1.0
        else:
            R[i, i - 16] = 1.0
    R2 = np.zeros((128, 128), np.float32)
    R2[:64, :64] = R
    R2[64:, 64:] = R
    blk = np.zeros((128, 128), np.float32)
    blk[:64, :64] = 1.0
    blk[64:, 64:] = 1.0
    bf = ml_dtypes.bfloat16
    return {"ident_bf": ident.astype(bf), "ident_f": ident, "rotT": np.ascontiguousarray(R2.T).astype(bf),
            "blk64": blk.astype(bf)}


def _pp(v, nchunk):
    return np.ascontiguousarray(np.asarray(v, np.float32).reshape(nchunk, 128).T)


def make_in_maps(inputs, cores=range(8)):
    f = lambda k: np.asarray(inputs[k], np.float32)
    x, c, ctx, c_ctx = f("x"), f("c"), f("ctx"), f("c_ctx")
    shared = dict(_consts())
    for L in ("e", "o"):
        shared[L + "_w_in"] = np.ascontiguousarray(f(L + "_w_in")[0])
        shared[L + "_w_out"] = np.ascontiguousarray(f(L + "_w_out")[0])
        shared[L + "_ada_w"] = np.ascontiguousarray(f(L + "_ada_w")[0])
        shared[L + "_ada_bT"] = _pp(f(L + "_ada_b")[0], 48)
        shared[L + "_norm_gT"] = _pp(f(L + "_norm_g")[0], 16)
    shared["e_vng_bc"] = np.ascontiguousarray(np.broadcast_to(f("e_a_vnorm_g")[0][None, :], (128, 1024)))
    shared["e_wsT"] = np.ascontiguousarray(f("e_a_ws")[0].transpose(2, 0, 1))
    shared["e_bs_bc"] = np.ascontiguousarray(np.broadcast_to(f("e_a_bs")[0][None], (128, 8, 128)))
    shared["e_gq"] = np.ascontiguousarray(np.tile(f("e_b_qnorm_g")[0], 2)[:, None])
    shared["e_gk"] = np.ascontiguousarray(np.tile(f("e_b_knorm_g")[0], 2)[:, None])
    shared["e_lam"] = np.ascontiguousarray(f("e_b_lambda")[0].reshape(1, 256))
    shared["e_gon"] = np.ascontiguousarray(f("e_b_onorm_g")[0][:, None])
    shared["o_dw_wT"] = np.ascontiguousarray(f("o_dw_w")[0].T.reshape(16, 128, 31).transpose(1, 0, 2))
    shared["o_dw_bT"] = _pp(f("o_dw_b")[0], 16)
    shared["o_ln_gT"] = _pp(f("o_ln_g")[0], 16)
    shared["o_ln_bT"] = _pp(f("o_ln_b")[0], 16)
    maps = []
    for core in cores:
        b, s = core // 2, core % 2
        if s == 0:
            order = np.concatenate([np.arange(0, 2048), np.arange(2048, 2176), np.arange(2176, 4096)])
            hm = np.array([0.0, 1.0], np.float32)
        else:
            order = np.concatenate([np.arange(2048, 4096), np.arange(1920, 2048), np.arange(0, 1920)])
            hm = np.array([1.0, 0.0], np.float32)
        xcore = np.concatenate([x[b][order], ctx[b]], axis=0)
        cos, sin = _rope_tables(order)
        cosT = np.ones((128, NTOK), np.float32)
        sinT = np.zeros((128, NTOK), np.float32)
        cosT[:, :SEQ] = np.tile(cos.T, (2, 1))
        sinT[:, :SEQ] = np.tile(sin.T, (2, 1))
        cvec = np.stack([c[b], c_ctx], axis=0)
        cT = np.ascontiguousarray(cvec.reshape(2, 16, 128).transpose(2, 0, 1))
        m = dict(shared)
        m.update({"xc": np.ascontiguousarray(xcore), "cT": cT, "rope_cos": cosT, "rope_sin": sinT,
                  "halo_mask": np.ascontiguousarray(np.broadcast_to(hm[None, :], (128, 2)))})
        maps.append(m)
    return maps


_NC_CACHE = {}


def kernel(**inputs):
    if "nc" not in _NC_CACHE:
        _NC_CACHE["nc"] = build_program()
    nc = _NC_CACHE["nc"]
    maps = make_in_maps(inputs)
    res = run_bass_kernel_spmd(nc, maps, core_ids=list(range(8)))
    out = np.empty((4, SEQ, D), np.float32)
    for core in range(8):
        b, s = core // 2, core % 2
        out[b, s * 2048:(s + 1) * 2048] = np.asarray(res.results[core]["out"])
    return out
```

```python
import math
import numpy as np
import ml_dtypes
import concourse.bass as bass
import concourse.mybir as mybir
from concourse.bass_utils import run_bass_kernel_spmd

F32 = mybir.dt.float32
BF16 = mybir.dt.bfloat16
AF = mybir.ActivationFunctionType
ALU = mybir.AluOpType

D = 2048
SEQ = 4096
CTX = 256
NTOK = SEQ + CTX
NF = 2176
NOWN = 2048
EPS = 1e-6
LAMBDA_INIT0 = 0.8 - 0.6 * math.exp(-0.3 * 0)
FT = [(0, 512), (512, 512), (1024, 512), (1536, 512), (2048, 128)]


class Buf:
    __slots__ = ("name", "w", "r")

    def __init__(self, name=""):
        self.name = name
        self.w = None
        self.r = []


class Prog:
    ENGS = ("pe", "act", "dve", "pool", "sp")

    def __init__(self, nc, n_dma_slots=8):
        self.nc = nc
        self.q = {e: [] for e in self.ENGS}
        self.sems = {}
        self.cnt = {}
        self.waited = {e: {} for e in self.ENGS}
        self._ctx = []
        for e in self.ENGS:
            self._mksem("c_" + e)
        self.n_dma_slots = n_dma_slots
        self.dma_i = {e: 0 for e in self.ENGS}
        for e in ("sp", "pool", "act"):
            for j in range(n_dma_slots):
                self._mksem("d_%s_%d" % (e, j))

    def _mksem(self, key):
        cm = self.nc.semaphore(key)
        h = cm.__enter__()
        self._ctx.append(cm)
        self.sems[key] = h
        self.cnt[key] = 0

    def _waits(self, eng, reads, writes, extra=()):
        need = {}

        def add(ev):
            if ev is None:
                return
            k, v = ev
            if need.get(k, 0) < v:
                need[k] = v
        for b in reads:
            add(b.w)
        for b in writes:
            add(b.w)
            for ev in b.r:
                add(ev)
        for ev in extra:
            add(ev)
        out = []
        wd = self.waited[eng]
        for k, v in need.items():
            if wd.get(k, 0) < v:
                wd[k] = v
                out.append((k, v))
        return out

    def _record(self, ev, reads, writes):
        for b in reads:
            b.r.append(ev)
        for b in writes:
            b.w = ev
            b.r = []

    def op(self, eng, fn, reads=(), writes=()):
        waits = self._waits(eng, reads, writes)
        key = "c_" + eng
        self.cnt[key] += 1
        ev = (key, self.cnt[key])
        self._record(ev, reads, writes)
        self.q[eng].append((waits, fn, key, 1))
        return ev

    def dma(self, eng, fn, reads=(), writes=()):
        j = self.dma_i[eng] % self.n_dma_slots
        self.dma_i[eng] += 1
        key = "d_%s_%d" % (eng, j)
        prev = (key, self.cnt[key]) if self.cnt[key] > 0 else None
        waits = self._waits(eng, reads, writes, extra=(prev,) if prev else ())
        self.cnt[key] += 16
        ev = (key, self.cnt[key])
        self._record(ev, reads, writes)
        self.q[eng].append((waits, fn, key, 16))
        return ev

    def barrier(self):
        for eng in self.ENGS:
            waits = []
            wd = self.waited[eng]
            for k, v in self.cnt.items():
                if v > 0 and wd.get(k, 0) < v:
                    wd[k] = v
                    waits.append((k, v))
            if waits:
                self.q[eng].append((waits, None, None, 0))

    def emit(self):
        nc = self.nc
        q = self.q
        sems = self.sems

        def run(e, items):
            for waits, fn, key, inc in items:
                for k, v in waits:
                    e.wait_ge(sems[k], v)
                if fn is not None:
                    ins = fn(e)
                    ins.then_inc(sems[key], inc)

        with nc.Block() as block:
            @block.tensor
            def _(e):
                run(e, q["pe"])

            @block.scalar
            def _(e):
                run(e, q["act"])

            @block.vector
            def _(e):
                run(e, q["dve"])

            @block.gpsimd
            def _(e):
                run(e, q["pool"])

            @block.sync
            def _(e):
                run(e, q["sp"])

    def close(self):
        for cm in reversed(self._ctx):
            cm.__exit__(None, None, None)


class Arena:
    def __init__(self, ap, nwords):
        self.ap = ap
        self.n = nwords
        self.off = 0

    def mark(self):
        return self.off

    def reset(self, m):
        self.off = m

    def alloc(self, free_shape, dtype):
        n = 1
        for s in free_shape:
            n *= s
        words = n if dtype == F32 else (n + 1) // 2
        words = (words + 7) // 8 * 8
        assert self.off + words <= self.n, ("arena overflow", self.off, words, self.n)
        v = self.ap[:, self.off:self.off + words]
        self.off += words
        if dtype != F32:
            v = v.bitcast(dtype)
        v = v[:, 0:n]
        if len(free_shape) == 2:
            v = v.rearrange("p (a b) -> p a b", a=free_shape[0])
        elif len(free_shape) == 3:
            v = v.rearrange("p (a b c) -> p a b c", a=free_shape[0], b=free_shape[1])
        return v


def build_program(debug=False, stop_after=None):
    nc = bass.Bass("TRN2", target_bir_lowering=False)
    P = Prog(nc)

    def din(name, shape, dt=F32):
        return nc.dram_tensor(name, list(shape), dt, kind="ExternalInput").ap()

    def dscr(name, shape, dt):
        return nc.dram_tensor(name, list(shape), dt,
                              kind="ExternalOutput" if debug else "Internal").ap()

    xc = din("xc", [NTOK, D])
    cT_d = din("cT", [128, 2, 16])
    cos_d = din("rope_cos", [128, NTOK])
    sin_d = din("rope_sin", [128, NTOK])
    identb_d = din("ident_bf", [128, 128], BF16)
    identf_d = din("ident_f", [128, 128])
    rotT_d = din("rotT", [128, 128], BF16)
    blk64_d = din("blk64", [128, 128], BF16)
    halo_d = din("halo_mask", [128, 2])
    W = {}
    for L, nin in (("e", 7168), ("o", 6144)):
        W[L + "_w_in"] = din(L + "_w_in", [D, nin])
        W[L + "_w_out"] = din(L + "_w_out", [D, D])
        W[L + "_ada_w"] = din(L + "_ada_w", [D, 6144])
        W[L + "_ada_bT"] = din(L + "_ada_bT", [128, 48])
        W[L + "_norm_gT"] = din(L + "_norm_gT", [128, 16])
    vng_d = din("e_vng_bc", [128, 1024])
    wsT_d = din("e_wsT", [128, 8, 128])
    bs_d = din("e_bs_bc", [128, 8, 128])
    gq_d = din("e_gq", [128, 1])
    gk_d = din("e_gk", [128, 1])
    lam_d = din("e_lam", [1, 256])
    gon_d = din("e_gon", [128, 1])
    dww_d = din("o_dw_wT", [128, 16, 31])
    dwb_d = din("o_dw_bT", [128, 16])
    lng_d = din("o_ln_gT", [128, 16])
    lnb_d = din("o_ln_bT", [128, 16])
    out_d = nc.dram_tensor("out", [NOWN, D], F32, kind="ExternalOutput").ap()

    K_d = dscr("K_s", [8, 128, NTOK], BF16)
    V_d = dscr("V_s", [NTOK, 1024], BF16)
    Q_d = dscr("Q_s", [8, 128, NF], BF16)
    G_d = dscr("G_s", [8, 128, NF], BF16)
    AU_d = dscr("AU_s", [8, 128, NF], BF16)
    AG_d = dscr("AG_s", [8, 128, NF], BF16)
    Y_d = dscr("Y_s", [16, 128, NF], BF16)
    X1_d = dscr("X1_s", [NF, D], F32)
    YC_d = dscr("YC_s", [16, 128, NOWN], F32)
    SG_d = dscr("SG_s", [16, 128, NOWN], BF16)
    if debug:
        HT_d = dscr("HT_s", [16, 128, NF], BF16)
        MOD_d = dscr("MOD_s", [128, 2 * 2 * 48 + 2 * 32], F32)
        GT_d = dscr("GT_s", [2, 128, D], F32)

    ARENA_WORDS = 53000
    arena_t = nc.alloc_sbuf_tensor("arena", [128, ARENA_WORDS], F32)
    AR = Arena(arena_t[:, :], ARENA_WORDS)
    pp = [nc.alloc_psum_tensor("pp%d" % i, [128, 2, 512], F32) for i in range(4)]

    def bank(k):
        return pp[k // 2][:, k % 2, :]
    pbuf = [Buf("bank%d" % k) for k in range(8)]

    identb = AR.alloc([128], BF16)
    identf = AR.alloc([128], F32)
    rotT = AR.alloc([128], BF16)
    blk64 = AR.alloc([128], BF16)
    onesb = AR.alloc([128], BF16)
    onesf = AR.alloc([128], F32)
    epsc = AR.alloc([1], F32)
    halo = AR.alloc([2], F32)
    mT = [AR.alloc([2, 48], F32) for _ in range(2)]
    Gp = [AR.alloc([2, 16], F32) for _ in range(2)]
    gt_bc = [AR.alloc([D], F32) for _ in range(2)]
    adabT = [AR.alloc([48], F32) for _ in range(2)]
    ngT = [AR.alloc([16], F32) for _ in range(2)]
    vng = AR.alloc([1024], F32)
    wsTb = AR.alloc([8, 128], BF16)
    bsb = AR.alloc([8, 128], F32)
    bsh = AR.alloc([8, 128], BF16)
    bsl = AR.alloc([8, 128], BF16)
    gq = AR.alloc([1], F32)
    gk = AR.alloc([1], F32)
    gon = AR.alloc([1], F32)
    nlam = AR.alloc([1], F32)
    lamrow = AR.alloc([256], F32)
    lamtmp = AR.alloc([8], F32)
    dww = AR.alloc([16, 31], F32)
    dwb = AR.alloc([16], F32)
    lng = AR.alloc([16], F32)
    lnb = AR.alloc([16], F32)
    cTs = AR.alloc([2, 16], F32)
    scT = AR.alloc([2, 16], F32)
    scTb = AR.alloc([2, 16], BF16)
    b_const = Buf("const")
    b_mod = Buf("mod")
    PERSIST = AR.mark()

    LNAME = ("e", "o")

    def ld(dst, src):
        P.dma("sp", lambda e: e.dma_start(out=dst, in_=src), writes=[b_const])
    ld(identb, identb_d)
    ld(identf, identf_d)
    ld(rotT, rotT_d)
    ld(blk64, blk64_d)
    ld(halo, halo_d)
    for l in range(2):
        ld(adabT[l], W[LNAME[l] + "_ada_bT"])
        ld(ngT[l], W[LNAME[l] + "_norm_gT"])
    ld(vng, vng_d)
    P.dma("pool", lambda e: e.dma_start(out=wsTb, in_=wsT_d), writes=[b_const])
    ld(bsb, bs_d)
    ld(gq, gq_d)
    ld(gk, gk_d)
    ld(gon, gon_d)
    ld(dww, dww_d)
    ld(dwb, dwb_d)
    ld(lng, lng_d)
    ld(lnb, lnb_d)
    ld(cTs, cT_d)
    P.dma("sp", lambda e: e.dma_start(out=lamrow[0:1, :], in_=lam_d), writes=[b_const])
    P.op("dve", lambda e: e.memset(onesb, 1.0), writes=[b_const])
    P.op("dve", lambda e: e.memset(onesf, 1.0), writes=[b_const])
    P.op("dve", lambda e: e.memset(epsc, EPS), writes=[b_const])
    P.op("dve", lambda e: e.tensor_copy(out=bsh[0:1], in_=bsb[0:1]), reads=[b_const], writes=[b_const])
    P.op("dve", lambda e: e.tensor_tensor(out=bsl[0:1], in0=bsb[0:1], in1=bsh[0:1], op=ALU.subtract),
         reads=[b_const], writes=[b_const])
    P.op("dve", lambda e: e.tensor_scalar(out=gon, in0=gon, scalar1=1.0 - LAMBDA_INIT0, scalar2=None,
                                          op0=ALU.mult), reads=[b_const], writes=[b_const])
    P.op("dve", lambda e: e.tensor_tensor(out=lamrow[0:1, 0:64], in0=lamrow[0:1, 0:64], in1=lamrow[0:1, 64:128],
                                          op=ALU.mult), reads=[b_const], writes=[b_const])
    P.op("dve", lambda e: e.tensor_tensor(out=lamrow[0:1, 128:192], in0=lamrow[0:1, 128:192],
                                          in1=lamrow[0:1, 192:256], op=ALU.mult), reads=[b_const], writes=[b_const])
    P.op("dve", lambda e: e.reduce_sum(out=lamtmp[0:1, 0:1], in_=lamrow[0:1, 0:64], axis=mybir.AxisListType.X),
         reads=[b_const], writes=[b_const])
    P.op("dve", lambda e: e.reduce_sum(out=lamtmp[0:1, 1:2], in_=lamrow[0:1, 128:192], axis=mybir.AxisListType.X),
         reads=[b_const], writes=[b_const])
    P.op("act", lambda e: e.activation(out=lamtmp[0:1, 2:4], in_=lamtmp[0:1, 0:2], func=AF.Exp),
         reads=[b_const], writes=[b_const])
    P.op("dve", lambda e: e.tensor_tensor(out=lamtmp[0:1, 4:5], in0=lamtmp[0:1, 3:4], in1=lamtmp[0:1, 2:3],
                                          op=ALU.subtract), reads=[b_const], writes=[b_const])
    P.op("dve", lambda e: e.tensor_scalar(out=lamtmp[0:1, 4:5], in0=lamtmp[0:1, 4:5], scalar1=-LAMBDA_INIT0,
                                          scalar2=None, op0=ALU.add), reads=[b_const], writes=[b_const])
    P.op("pe", lambda e: e.matmul(bank(0)[:, 0:1], lhsT=onesf[0:1, :], rhs=lamtmp[0:1, 4:5], start=True, stop=True),
         reads=[b_const], writes=[pbuf[0]])
    P.op("dve", lambda e: e.tensor_copy(out=nlam, in_=bank(0)[:, 0:1]), reads=[pbuf[0]], writes=[b_const])
    P.op("act", lambda e: e.activation(out=scT, in_=cTs, func=AF.Silu), reads=[b_const], writes=[b_const])
    P.op("dve", lambda e: e.tensor_copy(out=scTb, in_=scT), reads=[b_const], writes=[b_const])

    m0 = AR.mark()
    NWB = 2
    wblk = [AR.alloc([16, 512], BF16) for _ in range(NWB)]
    b_wblk = [Buf("wblk%d" % i) for i in range(NWB)]
    wctr = [0]

    def wload(src_w, col0, ncols=512):
        i = wctr[0] % NWB
        wctr[0] += 1
        dst = wblk[i][:, :, 0:ncols]
        src = src_w[:, col0:col0 + ncols].rearrange("(c p) n -> p c n", p=128)
        P.dma("pool", lambda e: e.dma_start(out=dst, in_=src), writes=[b_wblk[i]])
        return wblk[i], b_wblk[i]

    dgt = AR.alloc([128], F32)
    b_dgt = Buf("dgt")

    def ada_block(l, blk, wt, wb, pk):
        def mm(e):
            ins = None
            for j in range(4):
                for c in range(16):
                    ins = e.matmul(bank(pk)[:, 2 * j:2 * j + 2], lhsT=wt[:, c, j * 128:(j + 1) * 128],
                                   rhs=scTb[:, :, c], start=(c == 0), stop=(c == 15))
            return ins
        P.op("pe", mm, reads=[wb, b_const], writes=[pbuf[pk]])
        for v in range(2):
            P.op("dve", lambda e, v=v: e.tensor_tensor(
                out=mT[l][:, v, blk * 4:blk * 4 + 4],
                in0=bank(pk)[:, 0:8].rearrange("p (j v) -> p j v", v=2)[:, :, v],
                in1=adabT[l][:, blk * 4:blk * 4 + 4], op=ALU.add),
                reads=[pbuf[pk], b_const], writes=[b_mod])

    def ada_gp(l):
        for v in range(2):
            P.op("dve", lambda e, v=v: e.scalar_tensor_tensor(
                out=Gp[l][:, v, :], in0=mT[l][:, v, 16:32], scalar=1.0, in1=ngT[l], op0=ALU.add, op1=ALU.mult),
                reads=[b_mod, b_const], writes=[b_mod])

    def ada_gate(l, pk):
        for c in range(16):
            P.op("dve", lambda e, c=c: e.tensor_scalar(out=dgt, in0=identf, scalar1=mT[l][:, 0, 32 + c:33 + c],
                                                       scalar2=None, op0=ALU.mult),
                 reads=[b_mod, b_const], writes=[b_dgt])
            P.op("pe", lambda e, c=c: e.matmul(bank(pk)[:, (c % 4) * 128:(c % 4 + 1) * 128], lhsT=onesf, rhs=dgt,
                                               start=True, stop=True),
                 reads=[b_dgt, b_const], writes=[pbuf[pk]])
            if c % 4 == 3:
                P.op("act", lambda e, c=c: e.activation(
                    out=gt_bc[l][:, (c - 3) * 128:(c + 1) * 128], in_=bank(pk), func=AF.Copy),
                    reads=[pbuf[pk]], writes=[b_mod])

    wada0 = W["e_ada_w"]
    nxt = wload(wada0, 0)
    for blk in range(8):
        wt, wb = nxt
        if blk + 1 < 8:
            nxt = wload(wada0, (blk + 1) * 512)
        ada_block(0, blk, wt, wb, blk % 2)
    ada_gp(0)
    lazy = {"items": [("blk", 0, b) for b in range(8, 12)] + [("gate", 0)] +
                     [("blk", 1, b) for b in range(12)] + [("gp", 1), ("gate", 1)],
            "i": 0, "nxt": None}

    def lazy_step(pk):
        it = lazy["items"]
        i = lazy["i"]
        if i >= len(it):
            return False
        lazy["i"] = i + 1
        item = it[i]
        if item[0] == "blk":
            if lazy["nxt"] is None:
                lazy["nxt"] = wload(W[LNAME[item[1]] + "_ada_w"], item[2] * 512)
            wt, wb = lazy["nxt"]
            lazy["nxt"] = None
            for j in range(i + 1, len(it)):
                if it[j][0] == "blk":
                    lazy["nxt"] = wload(W[LNAME[it[j][1]] + "_ada_w"], it[j][2] * 512)
                    break
            ada_block(item[1], item[2], wt, wb, pk)
        elif item[0] == "gp":
            ada_gp(item[1])
        else:
            ada_gate(item[1], pk)
        return True
    P.barrier()
    if stop_after == "P0":
        return _finish(nc, P)

    m_pre_h = AR.mark()
    hT = AR.alloc([16, NF], BF16)
    b_hTc = [Buf("hT%d" % c) for c in range(16)]
    m_h = AR.mark()

    def xprep(l, tile_src, ntiles, vec_of_tile, post_tile=None):
        xn4 = [AR.alloc([4, D], BF16) for _ in range(2)]
        b_xn4 = [Buf("xn4_%d" % i) for i in range(2)]
        junk = AR.alloc([D], BF16)
        b_junk = Buf("junk")
        st = AR.alloc([64], F32)
        b_stk = [Buf("st%d" % k) for k in range(32)]
        ngroups = (ntiles + 3) // 4
        for gi in range(ngroups):
            tl = list(range(gi * 4, min(ntiles, gi * 4 + 4)))
            xb = xn4[gi % 2]
            bx = b_xn4[gi % 2]
            for t in tl:
                xt_ap, xt_b = tile_src(t)
                k = t % 32
                P.op("dve", lambda e, xt_ap=xt_ap, k=k: e.scalar_tensor_tensor(
                    out=junk, in0=xt_ap, scalar=1.0, in1=xt_ap, op0=ALU.mult, op1=ALU.mult,
                    accum_out=st[:, k:k + 1]), reads=[xt_b], writes=[b_junk, b_stk[k]])
                P.op("act", lambda e, k=k: e.activation(out=st[:, 32 + k:33 + k], in_=st[:, k:k + 1], func=AF.Ln,
                                                        bias=epsc, scale=1.0 / D), reads=[b_stk[k], b_const], writes=[b_stk[k]])
                P.op("act", lambda e, k=k: e.activation(out=st[:, 32 + k:33 + k], in_=st[:, 32 + k:33 + k],
                                                        func=AF.Exp, scale=-0.5), reads=[b_stk[k]], writes=[b_stk[k]])
                P.op("act", lambda e, xt_ap=xt_ap, k=k, xb=xb, t=t: e.activation(
                    out=xb[:, t % 4, :], in_=xt_ap, func=AF.Copy, scale=st[:, 32 + k:33 + k]),
                    reads=[xt_b, b_stk[k]], writes=[bx])
                if post_tile is not None:
                    post_tile(t, xt_ap, xt_b)
            nt = len(tl)
            runs = []
            for tt, t in enumerate(tl):
                v = vec_of_tile(t)
                if runs and runs[-1][0] == v:
                    runs[-1][2] += 1
                else:
                    runs.append([v, tt, 1])
            for c in range(16):
                pk = c % 4

                def tr(e, c=c, pk=pk, xb=xb, nt=nt):
                    ins = None
                    pv = bank(pk)[:, 0:256].bitcast(BF16).rearrange("p (a b) -> p a b", a=4)
                    for tt in range(nt):
                        ins = e.transpose(out=pv[:, tt, :], in_=xb[:, tt, c * 128:(c + 1) * 128], identity=identb)
                    return ins
                P.op("pe", tr, reads=[bx, b_const], writes=[pbuf[pk]])
                for (v, tt0, ntt) in runs:
                    if c % 2 == 0:
                        P.op("dve", lambda e, c=c, pk=pk, tt0=tt0, ntt=ntt, t0=tl[0], v=v, l=l: e.tensor_scalar(
                            out=hT[:, c, (t0 + tt0) * 128:(t0 + tt0 + ntt) * 128],
                            in0=bank(pk)[:, 0:256].bitcast(BF16)[:, tt0 * 128:(tt0 + ntt) * 128],
                            scalar1=Gp[l][:, v, c:c + 1], scalar2=mT[l][:, v, c:c + 1], op0=ALU.mult, op1=ALU.add),
                            reads=[pbuf[pk], b_mod], writes=[b_hTc[c]])
                    else:
                        P.op("act", lambda e, c=c, pk=pk, tt0=tt0, ntt=ntt, t0=tl[0], v=v, l=l: e.activation(
                            out=hT[:, c, (t0 + tt0) * 128:(t0 + tt0 + ntt) * 128],
                            in_=bank(pk)[:, 0:256].bitcast(BF16)[:, tt0 * 128:(tt0 + ntt) * 128],
                            func=AF.Identity, scale=Gp[l][:, v, c:c + 1], bias=mT[l][:, v, c:c + 1]),
                            reads=[pbuf[pk], b_mod], writes=[b_hTc[c]])

    def dram_tile_src(src_d, base_tile, src_bufs=None):
        xt = [AR.alloc([D], F32) for _ in range(3)]
        b_xt = [Buf("xt%d" % i) for i in range(3)]

        def src(t):
            i = t % 3
            g = base_tile + t
            rd = [src_bufs[t]] if src_bufs is not None else []
            P.dma("sp", lambda e, i=i, g=g: e.dma_start(out=xt[i], in_=src_d[g * 128:(g + 1) * 128, :]),
                  reads=rd, writes=[b_xt[i]])
            return xt[i], b_xt[i]
        return src

    def proj_phase(which):
        tok_off = NF if which == "B" else 0
        if which == "AV":
            al = lambda shape, dt: None
        else:
            al = AR.alloc
        kg = [al([512], BF16) for _ in range(2)]
        ksq = [al([512], BF16) for _ in range(2)]
        b_kg = [Buf() for _ in range(2)]
        b_ksq = [Buf() for _ in range(2)]
        sq = al([512], F32)
        t1 = al([512], F32)
        t2 = al([512], F32)
        b_sq, b_t1, b_t2 = Buf(), Buf(), Buf()
        cs = [al([2, 512], F32) for _ in range(2)]
        b_cs = [Buf() for _ in range(2)]
        stg = [al([NF], BF16) for _ in range(2)]
        b_stg = [Buf() for _ in range(2)]
        vst = [al([1024], BF16) for _ in range(2)]
        b_vst = [Buf() for _ in range(2)]
        sctr = [0]
        pctr = [0]

        def next_bank(lo=0, n=4):
            k = lo + pctr[0] % n
            pctr[0] += 1
            return k

        def fm_proj(wsrc, col0, nheads, epi, dst_d):
            nblk = (nheads + 3) // 4
            pending = [None]

            def flush():
                if pending[0] is not None:
                    pending[0]()
                    pending[0] = None
            nxt = wload(wsrc, col0)
            for bi in range(nblk):
                wt, wb = nxt
                if bi + 1 < nblk:
                    nxt = wload(wsrc, col0 + (bi + 1) * 512)
                for hh in range(4):
                    hd = bi * 4 + hh
                    si = sctr[0] % 2
                    sctr[0] += 1
                    for ti, (t0, n) in enumerate(FT):
                        pk = next_bank(0, 4)

                        def mm(e, wt=wt, hh=hh, t0=t0, n=n, pk=pk):
                            ins = None
                            for c in range(16):
                                ins = e.matmul(bank(pk)[:, 0:n], lhsT=wt[:, c, hh * 128:(hh + 1) * 128],
                                               rhs=hT[:, c, t0:t0 + n], start=(c == 0), stop=(c == 15))
                            return ins
                        P.op("pe", mm, reads=[wb] + b_hTc, writes=[pbuf[pk]])
                        flush()

                        def ep(pk=pk, t0=t0, n=n, si=si, hd=hd, last=(ti == len(FT) - 1)):
                            epi(pk, t0, n, stg[si], b_stg[si])
                            if last:
                                P.dma("sp", lambda e: e.dma_start(
                                    out=dst_d[hd, :, tok_off:tok_off + NF] if dst_d is K_d else dst_d[hd],
                                    in_=stg[si]), reads=[b_stg[si]])
                        pending[0] = ep
            flush()

        def epi_act(func):
            def epi(pk, t0, n, sg_ap, sg_b):
                P.op("act", lambda e: e.activation(out=sg_ap[:, t0:t0 + n], in_=bank(pk)[:, 0:n], func=func),
                     reads=[pbuf[pk]], writes=[sg_b])
            return epi

        qk_ctr = [0]

        def epi_qk(gcol):
            def epi(pk, t0, n, sg_ap, sg_b):
                i = qk_ctr[0] % 2
                qk_ctr[0] += 1
                P.dma("sp", lambda e: e.dma_start(out=cs[i][:, 0, 0:n], in_=cos_d[:, tok_off + t0:tok_off + t0 + n]),
                      writes=[b_cs[i]])
                P.dma("sp", lambda e: e.dma_start(out=cs[i][:, 1, 0:n], in_=sin_d[:, tok_off + t0:tok_off + t0 + n]),
                      writes=[b_cs[i]])
                P.op("act", lambda e: e.activation(out=kg[i][:, 0:n], in_=bank(pk)[:, 0:n], func=AF.Copy, scale=gcol),
                     reads=[pbuf[pk], b_const], writes=[b_kg[i]])
                P.op("act", lambda e: e.activation(out=ksq[i][:, 0:n], in_=bank(pk)[:, 0:n], func=AF.Square),
                     reads=[pbuf[pk]], writes=[b_ksq[i]])
                p2 = next_bank(4, 4)
                p3 = next_bank(4, 4)
                P.op("pe", lambda e: e.matmul(bank(p2)[:, 0:n], lhsT=blk64, rhs=ksq[i][:, 0:n], start=True, stop=True),
                     reads=[b_ksq[i], b_const], writes=[pbuf[p2]])
                P.op("pe", lambda e: e.matmul(bank(p3)[:, 0:n], lhsT=rotT, rhs=kg[i][:, 0:n], start=True, stop=True),
                     reads=[b_kg[i], b_const], writes=[pbuf[p3]])
                P.op("act", lambda e: e.activation(out=sq[:, 0:n], in_=bank(p2)[:, 0:n], func=AF.Ln, bias=epsc,
                                                   scale=1.0 / 64), reads=[pbuf[p2], b_const], writes=[b_sq])
                P.op("act", lambda e: e.activation(out=sq[:, 0:n], in_=sq[:, 0:n], func=AF.Exp, scale=-0.5),
                     reads=[b_sq], writes=[b_sq])
                P.op("dve", lambda e: e.tensor_tensor(out=t1[:, 0:n], in0=kg[i][:, 0:n], in1=cs[i][:, 0, 0:n],
                                                      op=ALU.mult), reads=[b_kg[i], b_cs[i]], writes=[b_t1])
                P.op("dve", lambda e: e.tensor_tensor(out=t2[:, 0:n], in0=bank(p3)[:, 0:n], in1=cs[i][:, 1, 0:n],
                                                      op=ALU.mult), reads=[pbuf[p3], b_cs[i]], writes=[b_t2])
                P.op("dve", lambda e: e.tensor_tensor(out=t1[:, 0:n], in0=t1[:, 0:n], in1=t2[:, 0:n], op=ALU.add),
                     reads=[b_t1, b_t2], writes=[b_t1])
                P.op("dve", lambda e: e.tensor_tensor(out=sg_ap[:, t0:t0 + n], in0=t1[:, 0:n], in1=sq[:, 0:n],
                                                      op=ALU.mult), reads=[b_t1, b_sq], writes=[sg_b])
            return epi

        def tm_proj(wsrc, col0, epi_tm):
            wA = wload(wsrc, col0)
            wB = wload(wsrc, col0 + 512)
            for t in range(17):
                for bi, (wt, wb) in enumerate((wA, wB)):
                    pk = next_bank(0, 4)

                    def mm(e, wt=wt, t=t, pk=pk):
                        ins = None
                        for c in range(16):
                            ins = e.matmul(bank(pk), lhsT=hT[:, c, t * 128:(t + 1) * 128], rhs=wt[:, c, :],
                                           start=(c == 0), stop=(c == 15))
                        return ins
                    P.op("pe", mm, reads=[wb] + b_hTc, writes=[pbuf[pk]])
                    epi_tm(t, bi, pk)

        def epi_v(t, bi, pk):
            i = t % 2
            P.op("act", lambda e: e.activation(out=vst[i][:, bi * 512:(bi + 1) * 512], in_=bank(pk), func=AF.Copy),
                 reads=[pbuf[pk]], writes=[b_vst[i]])
            if bi == 1:
                g = tok_off + t * 128
                P.dma("sp", lambda e: e.dma_start(out=V_d[g:g + 128, :], in_=vst[i]), reads=[b_vst[i]])

        w_in0 = W["e_w_in"]
        if which != "AV":
            fm_proj(w_in0, 4096, 8, epi_qk(gk), K_d)
            tm_proj(w_in0, 5120, epi_v)
        if which == "A":
            fm_proj(w_in0, 3072, 8, epi_qk(gq), Q_d)
            fm_proj(w_in0, 6144, 8, epi_act(AF.Silu), G_d)
            fm_proj(w_in0, 0, 8, epi_act(AF.Gelu), AU_d)
            fm_proj(w_in0, 2048, 8, epi_act(AF.Silu), AG_d)
        if which == "AV":
            gv = [AR.alloc([1024], F32) for _ in range(2)]
            b_gv = [Buf() for _ in range(2)]
            vjunk = AR.alloc([1024], BF16)
            b_vjunk = Buf()
            ssv = AR.alloc([64], F32)
            b_ssvt = [Buf() for _ in range(17)]

            def epi_av(t, bi, pk):
                i = t % 2
                P.op("act", lambda e: e.activation(out=gv[i][:, bi * 512:(bi + 1) * 512], in_=bank(pk), func=AF.Gelu),
                     reads=[pbuf[pk]], writes=[b_gv[i]])
                if bi == 1:
                    P.op("act", lambda e: e.activation(out=vjunk, in_=gv[i], func=AF.Square,
                                                       accum_out=ssv[:, t:t + 1]),
                         reads=[b_gv[i]], writes=[b_vjunk, b_ssvt[t]])
                    P.op("act", lambda e: e.activation(out=ssv[:, 32 + t:33 + t], in_=ssv[:, t:t + 1], func=AF.Sqrt,
                                                       bias=epsc, scale=1.0 / 1024), reads=[b_ssvt[t], b_const],
                         writes=[b_ssvt[t]])
                    P.op("dve", lambda e: e.reciprocal(out=ssv[:, 32 + t:33 + t], in_=ssv[:, 32 + t:33 + t]),
                         reads=[b_ssvt[t]], writes=[b_ssvt[t]])
                    P.op("dve", lambda e: e.scalar_tensor_tensor(out=vn_all[:, t, :], in0=gv[i],
                                                                 scalar=ssv[:, 32 + t:33 + t], in1=vng,
                                                                 op0=ALU.mult, op1=ALU.mult),
                         reads=[b_gv[i], b_ssvt[t], b_const], writes=[b_vn])
            tm_proj(w_in0, 1024, epi_av)

    vn_all = None
    b_vn = Buf("vn")
    for which in ("B", "A"):
        AR.reset(m_h)
        base_tile = 17 if which == "B" else 0
        xprep(0, dram_tile_src(xc, base_tile), 17,
              (lambda t: 1 if (which == "B" and t >= 15) else 0))
        if debug and which == "A":
            P.dma("sp", lambda e: e.dma_start(out=HT_d.rearrange("c p t -> p c t"), in_=hT), reads=b_hTc)
        P.barrier()
        AR.reset(m_h)
        proj_phase(which)
        P.barrier()
        if which == "A":
            AR.reset(m_h)
            vn_all = AR.alloc([17, 1024], BF16)
            proj_phase("AV")
            P.barrier()
        if stop_after == "P2" + which:
            return _finish(nc, P)
    m_vn = m_h + 17 * 512
    AR.reset(m_vn)

    def mix_phase():
        au = [AR.alloc([NF], BF16) for _ in range(2)]
        ag = [AR.alloc([NF], BF16) for _ in range(2)]
        b_au = [Buf() for _ in range(2)]
        b_ag = [Buf() for _ in range(2)]
        ystg = [AR.alloc([NF], BF16) for _ in range(2)]
        b_ystg = [Buf() for _ in range(2)]
        pc = [0]

        def load(g):
            i = g % 2
            P.dma("sp", lambda e: e.dma_start(out=au[i], in_=AU_d[g]), writes=[b_au[i]])
            P.dma("sp", lambda e: e.dma_start(out=ag[i], in_=AG_d[g]), writes=[b_ag[i]])
        load(0)
        for g in range(8):
            i = g % 2
            if g + 1 < 8:
                load(g + 1)
            P.op("dve", lambda e, i=i: e.tensor_tensor(out=au[i], in0=au[i], in1=ag[i], op=ALU.mult),
                 reads=[b_au[i], b_ag[i]], writes=[b_au[i]])
            for nb in range(5):
                tl = list(range(nb * 4, min(17, nb * 4 + 4)))
                nt = len(tl)
                pk = pc[0] % 4
                pc[0] += 1

                def mm(e, g=g, tl=tl, pk=pk):
                    ins = None
                    for sl, n in enumerate(tl):
                        o = bank(pk)[:, sl * 128:(sl + 1) * 128]
                        e.matmul(o, lhsT=vn_all[:, n, g * 128:(g + 1) * 128], rhs=wsTb[:, g, :], start=True, stop=False)
                        e.matmul(o, lhsT=onesb[0:1, :], rhs=bsh[0:1, g, :], start=False, stop=False)
                        ins = e.matmul(o, lhsT=onesb[0:1, :], rhs=bsl[0:1, g, :], start=False, stop=True)
                    return ins
                P.op("pe", mm, reads=[b_vn, b_const], writes=[pbuf[pk]])
                P.op("dve", lambda e, i=i, nt=nt, t0=tl[0], pk=pk: e.tensor_tensor(
                    out=ystg[i][:, t0 * 128:(t0 + nt) * 128], in0=bank(pk)[:, 0:nt * 128],
                    in1=au[i][:, t0 * 128:(t0 + nt) * 128], op=ALU.mult),
                    reads=[pbuf[pk], b_au[i]], writes=[b_ystg[i]])
            P.dma("sp", lambda e, g=g, i=i: e.dma_start(out=Y_d[g], in_=ystg[i]), reads=[b_ystg[i]], writes=[b_Y])
    b_Y = Buf("Y_d")
    mix_phase()
    P.barrier()
    if stop_after == "P3":
        return _finish(nc, P)

    AR.reset(m_pre_h)

    def attn_phase():
        kT = [AR.alloc([NTOK], BF16) for _ in range(2)]
        vh = [AR.alloc([34, 128], BF16) for _ in range(2)]
        qT = [AR.alloc([NF], BF16) for _ in range(2)]
        sbg = [AR.alloc([NF], BF16) for _ in range(2)]
        b_in = [Buf() for _ in range(2)]
        pT = [AR.alloc([2, 512], BF16) for _ in range(3)]
        b_pT = [Buf() for _ in range(3)]
        rz = AR.alloc([2, 512], F32)
        o1 = AR.alloc([512], F32)
        o2 = AR.alloc([512], F32)
        rs = AR.alloc([512], F32)
        osq = AR.alloc([512], BF16)
        b_rz, b_o1, b_o2, b_rs, b_osq = Buf(), Buf(), Buf(), Buf(), Buf()
        ystg = [AR.alloc([NF], BF16) for _ in range(2)]
        b_ystg = [Buf() for _ in range(2)]
        b_s = [Buf(), Buf()]
        b_o = [pbuf[4], pbuf[5]]
        b_z = pbuf[6]
        zs = AR.alloc([512], F32)
        b_zs = Buf()
        zh = AR.alloc([512], BF16)
        zl = AR.alloc([512], BF16)
        b_zh, b_zl = Buf(), Buf()
        sctr = [0]
        pctr = [0]

        def load_head(hd):
            i = hd % 2
            P.dma("sp", lambda e: e.dma_start(out=kT[i], in_=K_d[hd]), writes=[b_in[i]])
            P.dma("sp", lambda e: e.dma_start(
                out=vh[i], in_=V_d[:, hd * 128:(hd + 1) * 128].rearrange("(t p) d -> p t d", p=128)),
                writes=[b_in[i]])
            P.dma("sp", lambda e: e.dma_start(out=qT[i], in_=Q_d[hd]), writes=[b_in[i]])
            P.dma("sp", lambda e: e.dma_start(out=sbg[i], in_=G_d[hd]), writes=[b_in[i]])

        pend = []

        def flush_one():
            if pend:
                pend.pop(0)()

        its = [(hd, qi, kt) for hd in range(8) for qi in range(len(FT)) for kt in range(34)]
        sis = {}

        def issue_s(j):
            hd, qi, kt = its[j]
            i = hd % 2
            t0, n = FT[qi]
            si = sctr[0] % 2
            sctr[0] += 1
            sis[j] = si

            def f(e):
                e.matmul(pp[si][:, 0, 0:n], lhsT=kT[i][0:64, kt * 128:(kt + 1) * 128],
                         rhs=qT[i][0:64, t0:t0 + n], start=True, stop=True)
                return e.matmul(pp[si][:, 1, 0:n], lhsT=kT[i][64:128, kt * 128:(kt + 1) * 128],
                                rhs=qT[i][64:128, t0:t0 + n], start=True, stop=True)
            P.op("pe", f, reads=[b_in[i]], writes=[b_s[si]])

        def epilogue(hd, qi):
            i = hd % 2
            t0, n = FT[qi]
            P.op("dve", lambda e: e.tensor_copy(out=o1[:, 0:n], in_=bank(4)[:, 0:n]), reads=[b_o[0]], writes=[b_o1])
            P.op("dve", lambda e: e.tensor_copy(out=o2[:, 0:n], in_=bank(5)[:, 0:n]), reads=[b_o[1]], writes=[b_o2])
            P.op("dve", lambda e: e.tensor_copy(out=zs[0:64, 0:n], in_=bank(6)[0:64, 0:n]), reads=[b_z], writes=[b_zs])
            lazy_step(7)

            def stage_0():
                P.op("dve", lambda e: e.reciprocal(out=zs[0:64, 0:n], in_=zs[0:64, 0:n]), reads=[b_zs], writes=[b_zs])
                P.op("dve", lambda e: e.tensor_copy(out=zh[0:64, 0:n], in_=zs[0:64, 0:n]), reads=[b_zs], writes=[b_zh])
                P.op("dve", lambda e: e.tensor_tensor(out=zl[0:64, 0:n], in0=zs[0:64, 0:n], in1=zh[0:64, 0:n],
                                                      op=ALU.subtract), reads=[b_zs, b_zh], writes=[b_zl])

            def zbc(row):
                def f(e):
                    e.matmul(bank(7)[:, 0:n], lhsT=onesb[row:row + 1, :], rhs=zh[row:row + 1, 0:n],
                             start=True, stop=False)
                    return e.matmul(bank(7)[:, 0:n], lhsT=onesb[row:row + 1, :], rhs=zl[row:row + 1, 0:n],
                                    start=False, stop=True)
                return f

            def stage_a():
                P.op("pe", zbc(0), reads=[b_zh, b_zl, b_const], writes=[pbuf[7]])
                P.op("dve", lambda e: e.tensor_tensor(out=o1[:, 0:n], in0=o1[:, 0:n], in1=bank(7)[:, 0:n],
                                                      op=ALU.mult), reads=[b_o1, pbuf[7]], writes=[b_o1])

            def stage_b():
                P.op("pe", zbc(32), reads=[b_zh, b_zl, b_const], writes=[pbuf[7]])
                P.op("dve", lambda e: e.tensor_tensor(out=o2[:, 0:n], in0=o2[:, 0:n], in1=bank(7)[:, 0:n],
                                                      op=ALU.mult), reads=[b_o2, pbuf[7]], writes=[b_o2])
                P.op("dve", lambda e: e.scalar_tensor_tensor(out=o1[:, 0:n], in0=o2[:, 0:n], scalar=nlam,
                                                             in1=o1[:, 0:n], op0=ALU.mult, op1=ALU.add),
                     reads=[b_o1, b_o2, b_const], writes=[b_o1])
                P.op("dve", lambda e: e.tensor_tensor(out=osq[:, 0:n], in0=o1[:, 0:n], in1=o1[:, 0:n], op=ALU.mult),
                     reads=[b_o1], writes=[b_osq])

            def stage_c1():
                P.op("pe", lambda e: e.matmul(bank(7)[:, 0:n], lhsT=onesb, rhs=osq[:, 0:n], start=True, stop=True),
                     reads=[b_osq, b_const], writes=[pbuf[7]])

            def stage_c():
                P.op("act", lambda e: e.activation(out=rs[:, 0:n], in_=bank(7)[:, 0:n], func=AF.Ln,
                                                   bias=epsc, scale=1.0 / 128),
                     reads=[pbuf[7], b_const], writes=[b_rs])
                P.op("act", lambda e: e.activation(out=rs[:, 0:n], in_=rs[:, 0:n], func=AF.Exp, scale=-0.5),
                     reads=[b_rs], writes=[b_rs])
                P.op("dve", lambda e: e.tensor_tensor(out=o1[:, 0:n], in0=o1[:, 0:n], in1=rs[:, 0:n],
                                                      op=ALU.mult), reads=[b_o1, b_rs], writes=[b_o1])
                P.op("dve", lambda e: e.scalar_tensor_tensor(
                    out=ystg[i][:, t0:t0 + n], in0=o1[:, 0:n], scalar=gon, in1=sbg[i][:, t0:t0 + n],
                    op0=ALU.mult, op1=ALU.mult), reads=[b_o1, b_in[i], b_const], writes=[b_ystg[i]])
                if qi == len(FT) - 1:
                    P.dma("sp", lambda e: e.dma_start(out=Y_d[8 + hd], in_=ystg[i]),
                          reads=[b_ystg[i]], writes=[b_Y])
            pend.extend([stage_0, stage_a, stage_b, stage_c1, stage_c])

        load_head(0)
        issue_s(0)
        issue_s(1)
        for j, (hd, qi, kt) in enumerate(its):
            i = hd % 2
            t0, n = FT[qi]
            si = sis[j]
            pi = pctr[0] % 3
            pctr[0] += 1
            P.op("act", lambda e, si=si, pi=pi, n=n: e.activation(
                out=pT[pi][:, :, 0:n], in_=pp[si][:, :, 0:n], func=AF.Exp, scale=0.125),
                reads=[b_s[si]], writes=[b_pT[pi]])
            if j + 2 < len(its):
                issue_s(j + 2)

            def pv(e, kt=kt, pi=pi, i=i, n=n):
                st, sp_ = (kt == 0), (kt == 33)
                e.matmul(bank(4)[:, 0:n], lhsT=vh[i][:, kt, :], rhs=pT[pi][:, 0, 0:n], start=st, stop=sp_)
                e.matmul(bank(5)[:, 0:n], lhsT=vh[i][:, kt, :], rhs=pT[pi][:, 1, 0:n], start=st, stop=sp_)
                e.matmul(bank(6)[0:32, 0:n], lhsT=onesb[:, 0:32], rhs=pT[pi][:, 0, 0:n], start=st, stop=sp_,
                         tile_position=(0, 0))
                return e.matmul(bank(6)[32:64, 0:n], lhsT=onesb[:, 0:32], rhs=pT[pi][:, 1, 0:n], start=st,
                                stop=sp_, tile_position=(0, 32))
            P.op("pe", pv, reads=[b_pT[pi], b_in[i], b_const], writes=[b_o[0], b_o[1], b_z])
            if kt in (3, 8, 13, 18, 21):
                flush_one()
            if kt == 23 and qi == 0 and hd + 1 < 8:
                load_head(hd + 1)
            if kt == 33:
                epilogue(hd, qi)
        while pend:
            flush_one()
    attn_phase()
    while lazy_step(7):
        pass
    if debug:
        for l in range(2):
            P.dma("sp", lambda e, l=l: e.dma_start(out=MOD_d[:, l * 96:(l + 1) * 96],
                                                   in_=mT[l].rearrange("p v c -> p (v c)")), reads=[b_mod])
            P.dma("sp", lambda e, l=l: e.dma_start(out=MOD_d[:, 192 + l * 32:192 + (l + 1) * 32],
                                                   in_=Gp[l].rearrange("p v c -> p (v c)")), reads=[b_mod])
            P.dma("sp", lambda e, l=l: e.dma_start(out=GT_d[l], in_=gt_bc[l]), reads=[b_mod])
    P.barrier()
    if stop_after == "P4":
        return _finish(nc, P)

    def outproj(l, w_out, ntiles, resid_d, dst_d, b_dst):
        xr = [AR.alloc([512], F32) for _ in range(4)]
        b_xr = [Buf() for _ in range(4)]
        t1 = [AR.alloc([512], F32) for _ in range(3)]
        b_t1 = [Buf() for _ in range(3)]
        units = [(j, t) for j in range(4) for t in range(ntiles)]

        def load(u):
            j, t = units[u]
            xi = u % 4
            P.dma("sp", lambda e: e.dma_start(
                out=xr[xi], in_=resid_d[t * 128:(t + 1) * 128, j * 512:(j + 1) * 512]), writes=[b_xr[xi]])
        load(0)
        load(1)
        nxt = wload(w_out, 0)
        wt = wb = None
        for u, (j, t) in enumerate(units):
            if t == 0:
                wt, wb = nxt
                if j + 1 < 4:
                    nxt = wload(w_out, (j + 1) * 512)
            if u + 2 < len(units):
                load(u + 2)
            pk = u % 4
            xi = u % 4
            ti = u % 3

            def mm(e, wt=wt, t=t, pk=pk):
                ins = None
                for c in range(16):
                    ins = e.matmul(bank(pk), lhsT=hT[:, c, t * 128:(t + 1) * 128], rhs=wt[:, c, :],
                                   start=(c == 0), stop=(c == 15))
                return ins
            P.op("pe", mm, reads=[wb] + b_hTc, writes=[pbuf[pk]])
            P.op("dve", lambda e, pk=pk, ti=ti, j=j: e.tensor_tensor(
                out=t1[ti], in0=bank(pk), in1=gt_bc[l][:, j * 512:(j + 1) * 512], op=ALU.mult),
                reads=[pbuf[pk], b_mod], writes=[b_t1[ti]])
            P.op("dve", lambda e, ti=ti, xi=xi: e.tensor_tensor(out=t1[ti], in0=t1[ti], in1=xr[xi], op=ALU.add),
                 reads=[b_t1[ti], b_xr[xi]], writes=[b_t1[ti]])
            P.dma("act", lambda e, ti=ti, t=t, j=j: e.dma_start(
                out=dst_d[t * 128:(t + 1) * 128, j * 512:(j + 1) * 512], in_=t1[ti]),
                reads=[b_t1[ti]], writes=[b_dst[t]])

    AR.reset(m_h)
    for c in range(16):
        P.dma("sp", lambda e, c=c: e.dma_start(out=hT[:, c, :], in_=Y_d[c]), reads=[b_Y], writes=[b_hTc[c]])
    b_X1 = [Buf("x1_%d" % t) for t in range(17)]
    outproj(0, W["e_w_out"], 17, xc, X1_d, b_X1)
    P.barrier()
    if stop_after == "P5":
        return _finish(nc, P)

    AR.reset(m_h)
    xprep(1, dram_tile_src(X1_d, 0, b_X1), 17, (lambda t: 0))
    if debug:
        P.dma("sp", lambda e: e.dma_start(out=HT_d.rearrange("c p t -> p c t"), in_=hT), reads=b_hTc)
    P.barrier()
    if stop_after == "P6":
        return _finish(nc, P)

    AR.reset(m_h)
    acc_s = AR.alloc([NOWN], F32)
    acc_q = AR.alloc([NOWN], F32)
    b_acc = Buf("acc")
    m_acc = AR.mark()
    GL = 15 + NOWN + 15
    w_in1 = W["o_w_in"]
    glub = [AR.alloc([GL + 2], BF16) for _ in range(2)]
    b_glub = [Buf() for _ in range(2)]
    sig = [AR.alloc([512], F32) for _ in range(2)]
    b_sig = [Buf() for _ in range(2)]
    gh = AR.alloc([128], F32)
    b_gh = Buf()
    sgs = AR.alloc([NOWN], BF16)
    b_sgs = Buf()
    NPE = 16
    dg2 = [AR.alloc([NPE, 128], BF16) for _ in range(2)]
    b_dg2 = [Buf() for _ in range(2)]
    dacc = AR.alloc([NOWN], F32)
    b_dacc = Buf()
    ych = [AR.alloc([NOWN], F32) for _ in range(2)]
    b_ych = [Buf() for _ in range(2)]
    sq5 = [AR.alloc([512], F32) for _ in range(2)]
    b_sq5 = [Buf() for _ in range(2)]
    P.op("pool", lambda e: e.memset(acc_s, 0.0), writes=[b_acc])
    P.op("pool", lambda e: e.memset(acc_q, 0.0), writes=[b_acc])
    b_YC = [Buf() for _ in range(16)]
    b_SG = [Buf() for _ in range(16)]

    def w3load(cc):
        i = wctr[0] % NWB
        wctr[0] += 1
        for k3 in range(3):
            dst = wblk[i][:, :, k3 * 128:(k3 + 1) * 128]
            src = w_in1[:, k3 * 2048 + cc * 128:k3 * 2048 + (cc + 1) * 128].rearrange("(c p) n -> p c n", p=128)
            P.dma("pool", lambda e, dst=dst, src=src: e.dma_start(out=dst, in_=src), writes=[b_wblk[i]])
        return wblk[i], b_wblk[i]

    nxt = w3load(0)
    ctr = 0
    tap_q = []
    fin_q = []

    def drain_taps(k):
        for _ in range(min(k, len(tap_q))):
            tap_q.pop(0)()

    for cc in range(16):
        wt, wb = nxt
        if cc + 1 < 16:
            nxt = w3load(cc + 1)
        gi = cc % 2
        dg = dg2[gi]
        b_dg = b_dg2[gi]
        for tap in range(NPE):
            P.op("act", lambda e, tap=tap, cc=cc, dg=dg: e.activation(
                out=dg[:, tap, :], in_=identf, func=AF.Copy, scale=dww[:, cc, tap:tap + 1]),
                reads=[b_const], writes=[b_dg])
        for (t0, n) in FT:
            is_halo = (t0 == NOWN)
            pa, pb_, pg = (ctr * 3) % 6, (ctr * 3 + 1) % 6, (ctr * 3 + 2) % 6
            si = ctr % 2
            ctr += 1

            def mm3(e, wt=wt, t0=t0, n=n, pa=pa, pb_=pb_, pg=pg, is_halo=is_halo):
                ins = None
                for k3, pk in enumerate((pa, pb_, pg)):
                    if k3 == 2 and is_halo:
                        continue
                    for c in range(16):
                        ins = e.matmul(bank(pk)[:, 0:n], lhsT=wt[:, c, k3 * 128:(k3 + 1) * 128],
                                       rhs=hT[:, c, t0:t0 + n], start=(c == 0), stop=(c == 15))
                return ins
            P.op("pe", mm3, reads=[wb] + b_hTc, writes=[pbuf[pa], pbuf[pb_]] + ([] if is_halo else [pbuf[pg]]))
            P.op("act", lambda e, si=si, pb_=pb_, n=n: e.activation(out=sig[si][:, 0:n], in_=bank(pb_)[:, 0:n],
                                                                    func=AF.Sigmoid),
                 reads=[pbuf[pb_]], writes=[b_sig[si]])
            if not is_halo:
                P.op("dve", lambda e, si=si, pa=pa, n=n, t0=t0, gi=gi: e.tensor_tensor(
                    out=glub[gi][:, 15 + t0:15 + t0 + n], in0=bank(pa)[:, 0:n], in1=sig[si][:, 0:n], op=ALU.mult),
                    reads=[pbuf[pa], b_sig[si]], writes=[b_glub[gi]])
                P.op("act", lambda e, pg=pg, n=n, t0=t0: e.activation(out=sgs[:, t0:t0 + n], in_=bank(pg)[:, 0:n],
                                                                      func=AF.Silu),
                     reads=[pbuf[pg]], writes=[b_sgs])
            else:
                P.op("dve", lambda e, si=si, pa=pa: e.tensor_tensor(out=gh, in0=bank(pa)[:, 0:128],
                                                                    in1=sig[si][:, 0:128], op=ALU.mult),
                     reads=[pbuf[pa], b_sig[si]], writes=[b_gh])
                P.op("dve", lambda e, gi=gi: e.tensor_scalar(out=glub[gi][:, 0:15], in0=gh[:, 113:128],
                                                             scalar1=halo[:, 0:1], scalar2=None, op0=ALU.mult),
                     reads=[b_gh, b_const], writes=[b_glub[gi]])
                P.op("dve", lambda e, gi=gi: e.tensor_scalar(out=glub[gi][:, 15 + NOWN:30 + NOWN], in0=gh[:, 0:15],
                                                             scalar1=halo[:, 1:2], scalar2=None, op0=ALU.mult),
                     reads=[b_gh, b_const], writes=[b_glub[gi]])
            drain_taps(3)
        drain_taps(99)
        while fin_q:
            fin_q.pop(0)()
        P.dma("sp", lambda e, cc=cc: e.dma_start(out=SG_d[cc], in_=sgs), reads=[b_sgs], writes=[b_SG[cc]])
        tap_q.append(lambda cc=cc, gi=gi: P.op("dve", lambda e: e.tensor_scalar(
            out=dacc, in0=glub[gi][:, NPE:NPE + NOWN], scalar1=dww[:, cc, NPE:NPE + 1], scalar2=None, op0=ALU.mult),
            reads=[b_glub[gi], b_const], writes=[b_dacc]))
        for tap in range(NPE + 1, 31):
            tap_q.append(lambda cc=cc, gi=gi, tap=tap: P.op("dve", lambda e: e.scalar_tensor_tensor(
                out=dacc, in0=glub[gi][:, tap:tap + NOWN], scalar=dww[:, cc, tap:tap + 1], in1=dacc,
                op0=ALU.mult, op1=ALU.add), reads=[b_glub[gi], b_const, b_dacc], writes=[b_dacc]))
        yi = cc % 2
        for j4 in range(4):
            pk = 6 + j4 % 2

            def cv(e, j4=j4, pk=pk, gi=gi, dg=dg):
                ins = None
                for tap in range(NPE):
                    ins = e.matmul(bank(pk), lhsT=dg[:, tap, :], rhs=glub[gi][:, j4 * 512 + tap:j4 * 512 + tap + 512],
                                   start=(tap == 0), stop=(tap == NPE - 1))
                return ins
            P.op("pe", cv, reads=[b_dg, b_glub[gi]], writes=[pbuf[pk]])
            P.op("act", lambda e, j4=j4, pk=pk, yi=yi, cc=cc: e.activation(
                out=ych[yi][:, j4 * 512:(j4 + 1) * 512], in_=bank(pk), func=AF.Identity, bias=dwb[:, cc:cc + 1]),
                reads=[pbuf[pk], b_const], writes=[b_ych[yi]])

        def finish(cc=cc, yi=yi):
            for j4 in range(4):
                qi = j4 % 2
                sl = slice(j4 * 512, (j4 + 1) * 512)
                P.op("pool", lambda e, sl=sl: e.tensor_tensor(out=ych[yi][:, sl], in0=ych[yi][:, sl], in1=dacc[:, sl],
                                                              op=ALU.add),
                     reads=[b_ych[yi], b_dacc], writes=[b_ych[yi]])
                P.op("act", lambda e, sl=sl, qi=qi: e.activation(out=sq5[qi], in_=ych[yi][:, sl], func=AF.Square),
                     reads=[b_ych[yi]], writes=[b_sq5[qi]])
                P.op("pool", lambda e, sl=sl: e.tensor_tensor(out=acc_s[:, sl], in0=acc_s[:, sl], in1=ych[yi][:, sl],
                                                              op=ALU.add), reads=[b_ych[yi], b_acc], writes=[b_acc])
                P.op("pool", lambda e, sl=sl, qi=qi: e.tensor_tensor(out=acc_q[:, sl], in0=acc_q[:, sl], in1=sq5[qi],
                                                                     op=ALU.add), reads=[b_sq5[qi], b_acc],
                     writes=[b_acc])
            P.dma("sp", lambda e: e.dma_start(out=YC_d[cc], in_=ych[yi]), reads=[b_ych[yi]], writes=[b_YC[cc]])
        fin_q.append(finish)
    drain_taps(99)
    while fin_q:
        fin_q.pop(0)()
    P.barrier()
    if stop_after == "P7":
        return _finish(nc, P)

    AR.reset(m_acc)
    mean_bc = AR.alloc([NOWN], F32)
    rstd_bc = AR.alloc([NOWN], F32)
    b_stat = Buf("lnstat")
    tmpv = AR.alloc([512], F32)
    b_tmpv = Buf()
    for j4 in range(4):
        sl = slice(j4 * 512, (j4 + 1) * 512)
        P.op("pe", lambda e, sl=sl: e.matmul(bank(0), lhsT=onesf, rhs=acc_s[:, sl], start=True, stop=True),
             reads=[b_acc, b_const], writes=[pbuf[0]])
        P.op("pe", lambda e, sl=sl: e.matmul(bank(1), lhsT=onesf, rhs=acc_q[:, sl], start=True, stop=True),
             reads=[b_acc, b_const], writes=[pbuf[1]])
        P.op("dve", lambda e, sl=sl: e.tensor_scalar(out=mean_bc[:, sl], in0=bank(0), scalar1=1.0 / D, scalar2=None,
                                                     op0=ALU.mult), reads=[pbuf[0]], writes=[b_stat])
        P.op("dve", lambda e, sl=sl: e.tensor_tensor(out=tmpv, in0=mean_bc[:, sl], in1=mean_bc[:, sl], op=ALU.mult),
             reads=[b_stat], writes=[b_tmpv])
        P.op("dve", lambda e, sl=sl: e.scalar_tensor_tensor(out=tmpv, in0=bank(1), scalar=1.0 / D, in1=tmpv,
                                                            op0=ALU.mult, op1=ALU.subtract),
             reads=[pbuf[1], b_tmpv], writes=[b_tmpv])
        P.op("act", lambda e, sl=sl: e.activation(out=rstd_bc[:, sl], in_=tmpv, func=AF.Sqrt, bias=epsc),
             reads=[b_tmpv, b_const], writes=[b_stat])
        P.op("dve", lambda e, sl=sl: e.reciprocal(out=rstd_bc[:, sl], in_=rstd_bc[:, sl]), reads=[b_stat],
             writes=[b_stat])
    yl = [AR.alloc([NOWN], F32) for _ in range(2)]
    sl_ = [AR.alloc([NOWN], BF16) for _ in range(2)]
    b_yl = [Buf() for _ in range(2)]
    b_sl = [Buf() for _ in range(2)]
    for cc in range(16):
        i = cc % 2
        P.dma("sp", lambda e, cc=cc, i=i: e.dma_start(out=yl[i], in_=YC_d[cc]), reads=[b_YC[cc]], writes=[b_yl[i]])
        P.dma("sp", lambda e, cc=cc, i=i: e.dma_start(out=sl_[i], in_=SG_d[cc]), reads=[b_SG[cc]], writes=[b_sl[i]])
        P.op("dve", lambda e, i=i: e.tensor_tensor(out=yl[i], in0=yl[i], in1=mean_bc, op=ALU.subtract),
             reads=[b_yl[i], b_stat], writes=[b_yl[i]])
        P.op("dve", lambda e, i=i: e.tensor_tensor(out=yl[i], in0=yl[i], in1=rstd_bc, op=ALU.mult),
             reads=[b_yl[i], b_stat], writes=[b_yl[i]])
        P.op("act", lambda e, i=i, cc=cc: e.activation(out=yl[i], in_=yl[i], func=AF.Silu, bias=lnb[:, cc:cc + 1],
                                                       scale=lng[:, cc:cc + 1]),
             reads=[b_yl[i], b_const], writes=[b_yl[i]])
        P.op("dve", lambda e, i=i, cc=cc: e.tensor_tensor(out=hT[:, cc, 0:NOWN], in0=yl[i], in1=sl_[i], op=ALU.mult),
             reads=[b_yl[i], b_sl[i]], writes=[b_hTc[cc]])
    P.barrier()
    if stop_after == "P7b":
        return _finish(nc, P)

    AR.reset(m_h)
    b_out = [Buf("out%d" % t) for t in range(16)]
    outproj(1, W["o_w_out"], 16, X1_d, out_d, b_out)
    return _finish(nc, P)


def _finish(nc, P):
    P.barrier()
    P.emit()
    P.close()
    return nc


def _rope_tables(pos):
    axis_dim = 32
    inv = (10000.0 ** (-np.arange(0, axis_dim, 2, dtype=np.float32) / np.float32(axis_dim))).astype(np.float32)
    row = (pos // 64).astype(np.float32)
    col = (pos % 64).astype(np.float32)
    ar = row[:, None] * inv[None, :]
    ac = col[:, None] * inv[None, :]
    ang = np.concatenate([ar, ar, ac, ac], axis=-1).astype(np.float32)
    return np.cos(ang).astype(np.float32), np.sin(ang).astype(np.float32)


def _consts():
    ident = np.eye(128, dtype=np.float32)
    R = np.zeros((64, 64), np.float32)
    for i in range(64):
        blk = i // 16
        if blk % 2 == 0:
            R[i, i + 16] = -1.0
        else:
            R[i, i - 16] = 1.0
    R2 = np.zeros((128, 128), np.float32)
    R2[:64, :64] = R
    R2[64:, 64:] = R
    blk = np.zeros((128, 128), np.float32)
    blk[:64, :64] = 1.0
    blk[64:, 64:] = 1.0
    bf = ml_dtypes.bfloat16
    return {"ident_bf": ident.astype(bf), "ident_f": ident, "rotT": np.ascontiguousarray(R2.T).astype(bf),
            "blk64": blk.astype(bf)}


def _pp(v, nchunk):
    return np.ascontiguousarray(np.asarray(v, np.float32).reshape(nchunk, 128).T)


def make_in_maps(inputs, cores=range(8)):
    f = lambda k: np.asarray(inputs[k], np.float32)
    x, c, ctx, c_ctx = f("x"), f("c"), f("ctx"), f("c_ctx")
    shared = dict(_consts())
    for L in ("e", "o"):
        shared[L + "_w_in"] = np.ascontiguousarray(f(L + "_w_in")[0])
        shared[L + "_w_out"] = np.ascontiguousarray(f(L + "_w_out")[0])
        shared[L + "_ada_w"] = np.ascontiguousarray(f(L + "_ada_w")[0])
        shared[L + "_ada_bT"] = _pp(f(L + "_ada_b")[0], 48)
        shared[L + "_norm_gT"] = _pp(f(L + "_norm_g")[0], 16)
    shared["e_vng_bc"] = np.ascontiguousarray(np.broadcast_to(f("e_a_vnorm_g")[0][None, :], (128, 1024)))
    shared["e_wsT"] = np.ascontiguousarray(f("e_a_ws")[0].transpose(2, 0, 1))
    shared["e_bs_bc"] = np.ascontiguousarray(np.broadcast_to(f("e_a_bs")[0][None], (128, 8, 128)))
    shared["e_gq"] = np.ascontiguousarray(np.tile(f("e_b_qnorm_g")[0], 2)[:, None])
    shared["e_gk"] = np.ascontiguousarray(np.tile(f("e_b_knorm_g")[0], 2)[:, None])
    shared["e_lam"] = np.ascontiguousarray(f("e_b_lambda")[0].reshape(1, 256))
    shared["e_gon"] = np.ascontiguousarray(f("e_b_onorm_g")[0][:, None])
    shared["o_dw_wT"] = np.ascontiguousarray(f("o_dw_w")[0].T.reshape(16, 128, 31).transpose(1, 0, 2))
    shared["o_dw_bT"] = _pp(f("o_dw_b")[0], 16)
    shared["o_ln_gT"] = _pp(f("o_ln_g")[0], 16)
    shared["o_ln_bT"] = _pp(f("o_ln_b")[0], 16)
    maps = []
    for core in cores:
        b, s = core // 2, core % 2
        if s == 0:
            order = np.concatenate([np.arange(0, 2048), np.arange(2048, 2176), np.arange(2176, 4096)])
            hm = np.array([0.0, 1.0], np.float32)
        else:
            order = np.concatenate([np.arange(2048, 4096), np.arange(1920, 2048), np.arange(0, 1920)])
            hm = np.array([1.0, 0.0], np.float32)
        xcore = np.concatenate([x[b][order], ctx[b]], axis=0)
        cos, sin = _rope_tables(order)
        cosT = np.ones((128, NTOK), np.float32)
        sinT = np.zeros((128, NTOK), np.float32)
        cosT[:, :SEQ] = np.tile(cos.T, (2, 1))
        sinT[:, :SEQ] = np.tile(sin.T, (2, 1))
        cvec = np.stack([c[b], c_ctx], axis=0)
        cT = np.ascontiguousarray(cvec.reshape(2, 16, 128).transpose(2, 0, 1))
        m = dict(shared)
        m.update({"xc": np.ascontiguousarray(xcore), "cT": cT, "rope_cos": cosT, "rope_sin": sinT,
                  "halo_mask": np.ascontiguousarray(np.broadcast_to(hm[None, :], (128, 2)))})
        maps.append(m)
    return maps


_NC_CACHE = {}


def kernel(**inputs):
    if "nc" not in _NC_CACHE:
        _NC_CACHE["nc"] = build_program()
    nc = _NC_CACHE["nc"]
    maps = make_in_maps(inputs)
    res = run_bass_kernel_spmd(nc, maps, core_ids=list(range(8)))
    out = np.empty((4, SEQ, D), np.float32)
    for core in range(8):
        b, s = core // 2, core % 2
        out[b, s * 2048:(s + 1) * 2048] = np.asarray(res.results[core]["out"])
    return out
```

```python
import math
import numpy as np
import ml_dtypes
import concourse.bass as bass
import concourse.mybir as mybir
from concourse.bass_utils import run_bass_kernel_spmd

F32 = mybir.dt.float32
BF16 = mybir.dt.bfloat16
AF = mybir.ActivationFunctionType
ALU = mybir.AluOpType

D = 2048
SEQ = 4096
CTX = 256
NTOK = SEQ + CTX
NF = 2176
NOWN = 2048
EPS = 1e-6
LAMBDA_INIT0 = 0.8 - 0.6 * math.exp(-0.3 * 0)
FT = [(0, 512), (512, 512), (1024, 512), (1536, 512), (2048, 128)]


class Buf:
    __slots__ = ("name", "w", "r")

    def __init__(self, name=""):
        self.name = name
        self.w = None
        self.r = []


class Prog:
    ENGS = ("pe", "act", "dve", "pool", "sp")

    def __init__(self, nc, n_dma_slots=8):
        self.nc = nc
        self.q = {e: [] for e in self.ENGS}
        self.sems = {}
        self.cnt = {}
        self.waited = {e: {} for e in self.ENGS}
        self._ctx = []
        for e in self.ENGS:
            self._mksem("c_" + e)
        self.n_dma_slots = n_dma_slots
        self.dma_i = {e: 0 for e in self.ENGS}
        for e in ("sp", "pool", "act"):
            for j in range(n_dma_slots):
                self._mksem("d_%s_%d" % (e, j))

    def _mksem(self, key):
        cm = self.nc.semaphore(key)
        h = cm.__enter__()
        self._ctx.append(cm)
        self.sems[key] = h
        self.cnt[key] = 0

    def _waits(self, eng, reads, writes, extra=()):
        need = {}

        def add(ev):
            if ev is None:
                return
            k, v = ev
            if need.get(k, 0) < v:
                need[k] = v
        for b in reads:
            add(b.w)
        for b in writes:
            add(b.w)
            for ev in b.r:
                add(ev)
        for ev in extra:
            add(ev)
        out = []
        wd = self.waited[eng]
        for k, v in need.items():
            if wd.get(k, 0) < v:
                wd[k] = v
                out.append((k, v))
        return out

    def _record(self, ev, reads, writes):
        for b in reads:
            b.r.append(ev)
        for b in writes:
            b.w = ev
            b.r = []

    def op(self, eng, fn, reads=(), writes=()):
        waits = self._waits(eng, reads, writes)
        key = "c_" + eng
        self.cnt[key] += 1
        ev = (key, self.cnt[key])
        self._record(ev, reads, writes)
        self.q[eng].append((waits, fn, key, 1))
        return ev

    def dma(self, eng, fn, reads=(), writes=()):
        j = self.dma_i[eng] % self.n_dma_slots
        self.dma_i[eng] += 1
        key = "d_%s_%d" % (eng, j)
        prev = (key, self.cnt[key]) if self.cnt[key] > 0 else None
        waits = self._waits(eng, reads, writes, extra=(prev,) if prev else ())
        self.cnt[key] += 16
        ev = (key, self.cnt[key])
        self._record(ev, reads, writes)
        self.q[eng].append((waits, fn, key, 16))
        return ev

    def barrier(self):
        for eng in self.ENGS:
            waits = []
            wd = self.waited[eng]
            for k, v in self.cnt.items():
                if v > 0 and wd.get(k, 0) < v:
                    wd[k] = v
                    waits.append((k, v))
            if waits:
                self.q[eng].append((waits, None, None, 0))

    def emit(self):
        nc = self.nc
        q = self.q
        sems = self.sems

        def run(e, items):
            for waits, fn, key, inc in items:
                for k, v in waits:
                    e.wait_ge(sems[k], v)
                if fn is not None:
                    ins = fn(e)
                    ins.then_inc(sems[key], inc)

        with nc.Block() as block:
            @block.tensor
            def _(e):
                run(e, q["pe"])

            @block.scalar
            def _(e):
                run(e, q["act"])

            @block.vector
            def _(e):
                run(e, q["dve"])

            @block.gpsimd
            def _(e):
                run(e, q["pool"])

            @block.sync
            def _(e):
                run(e, q["sp"])

    def close(self):
        for cm in reversed(self._ctx):
            cm.__exit__(None, None, None)


class Arena:
    def __init__(self, ap, nwords):
        self.ap = ap
        self.n = nwords
        self.off = 0

    def mark(self):
        return self.off

    def reset(self, m):
        self.off = m

    def alloc(self, free_shape, dtype):
        n = 1
        for s in free_shape:
            n *= s
        words = n if dtype == F32 else (n + 1) // 2
        words = (words + 7) // 8 * 8
        assert self.off + words <= self.n, ("arena overflow", self.off, words, self.n)
        v = self.ap[:, self.off:self.off + words]
        self.off += words
        if dtype != F32:
            v = v.bitcast(dtype)
        v = v[:, 0:n]
        if len(free_shape) == 2:
            v = v.rearrange("p (a b) -> p a b", a=free_shape[0])
        elif len(free_shape) == 3:
            v = v.rearrange("p (a b c) -> p a b c", a=free_shape[0], b=free_shape[1])
        return v


def build_program(debug=False, stop_after=None):
    nc = bass.Bass("TRN2", target_bir_lowering=False)
    P = Prog(nc)

    def din(name, shape, dt=F32):
        return nc.dram_tensor(name, list(shape), dt, kind="ExternalInput").ap()

    def dscr(name, shape, dt):
        return nc.dram_tensor(name, list(shape), dt,
                              kind="ExternalOutput" if debug else "Internal").ap()

    xc = din("xc", [NTOK, D])
    cT_d = din("cT", [128, 2, 16])
    cos_d = din("rope_cos", [128, NTOK])
    sin_d = din("rope_sin", [128, NTOK])
    identb_d = din("ident_bf", [128, 128], BF16)
    identf_d = din("ident_f", [128, 128])
    rotT_d = din("rotT", [128, 128], BF16)
    blk64_d = din("blk64", [128, 128], BF16)
    halo_d = din("halo_mask", [128, 2])
    W = {}
    for L, nin in (("e", 7168), ("o", 6144)):
        W[L + "_w_in"] = din(L + "_w_in", [D, nin])
        W[L + "_w_out"] = din(L + "_w_out", [D, D])
        W[L + "_ada_w"] = din(L + "_ada_w", [D, 6144])
        W[L + "_ada_bT"] = din(L + "_ada_bT", [128, 48])
        W[L + "_norm_gT"] = din(L + "_norm_gT", [128, 16])
    vng_d = din("e_vng_bc", [128, 1024])
    wsT_d = din("e_wsT", [128, 8, 128])
    bs_d = din("e_bs_bc", [128, 8, 128])
    gq_d = din("e_gq", [128, 1])
    gk_d = din("e_gk", [128, 1])
    lam_d = din("e_lam", [1, 256])
    gon_d = din("e_gon", [128, 1])
    dww_d = din("o_dw_wT", [128, 16, 31])
    dwb_d = din("o_dw_bT", [128, 16])
    lng_d = din("o_ln_gT", [128, 16])
    lnb_d = din("o_ln_bT", [128, 16])
    out_d = nc.dram_tensor("out", [NOWN, D], F32, kind="ExternalOutput").ap()

    K_d = dscr("K_s", [8, 128, NTOK], BF16)
    V_d = dscr("V_s", [NTOK, 1024], BF16)
    Q_d = dscr("Q_s", [8, 128, NF], BF16)
    G_d = dscr("G_s", [8, 128, NF], BF16)
    AU_d = dscr("AU_s", [8, 128, NF], BF16)
    AG_d = dscr("AG_s", [8, 128, NF], BF16)
    Y_d = dscr("Y_s", [16, 128, NF], BF16)
    X1_d = dscr("X1_s", [NF, D], F32)
    YC_d = dscr("YC_s", [16, 128, NOWN], F32)
    SG_d = dscr("SG_s", [16, 128, NOWN], BF16)
    if debug:
        HT_d = dscr("HT_s", [16, 128, NF], BF16)
        MOD_d = dscr("MOD_s", [128, 2 * 2 * 48 + 2 * 32], F32)
        GT_d = dscr("GT_s", [2, 128, D], F32)

    ARENA_WORDS = 53200
    arena_t = nc.alloc_sbuf_tensor("arena", [128, ARENA_WORDS], F32)
    AR = Arena(arena_t[:, :], ARENA_WORDS)
    pp = [nc.alloc_psum_tensor("pp%d" % i, [128, 2, 512], F32) for i in range(4)]

    def bank(k):
        return pp[k // 2][:, k % 2, :]
    pbuf = [Buf("bank%d" % k) for k in range(8)]

    identb = AR.alloc([128], BF16)
    identf = AR.alloc([128], F32)
    rotT = AR.alloc([128], BF16)
    blk64 = AR.alloc([128], BF16)
    onesb = AR.alloc([128], BF16)
    onesf = AR.alloc([128], F32)
    epsc = AR.alloc([1], F32)
    halo = AR.alloc([2], F32)
    mT = [AR.alloc([2, 48], F32) for _ in range(2)]
    Gp = [AR.alloc([2, 16], F32) for _ in range(2)]
    gt_bc = [AR.alloc([D], F32) for _ in range(2)]
    adabT = [AR.alloc([48], F32) for _ in range(2)]
    ngT = [AR.alloc([16], F32) for _ in range(2)]
    vng = AR.alloc([1024], F32)
    wsTb = AR.alloc([8, 128], BF16)
    bsb = AR.alloc([8, 128], F32)
    bsh = AR.alloc([8, 128], BF16)
    bsl = AR.alloc([8, 128], BF16)
    gq = AR.alloc([1], F32)
    gk = AR.alloc([1], F32)
    gon = AR.alloc([1], F32)
    nlam = AR.alloc([1], F32)
    lamrow = AR.alloc([256], F32)
    lamtmp = AR.alloc([8], F32)
    dww = AR.alloc([16, 31], F32)
    dwb = AR.alloc([16], F32)
    lng = AR.alloc([16], F32)
    lnb = AR.alloc([16], F32)
    cTs = AR.alloc([2, 16], F32)
    scT = AR.alloc([2, 16], F32)
    scTb = AR.alloc([2, 16], BF16)
    b_const = Buf("const")
    b_mod = Buf("mod")
    PERSIST = AR.mark()

    LNAME = ("e", "o")

    def ld(dst, src):
        P.dma("sp", lambda e: e.dma_start(out=dst, in_=src), writes=[b_const])
    ld(identb, identb_d)
    ld(identf, identf_d)
    ld(rotT, rotT_d)
    ld(blk64, blk64_d)
    ld(halo, halo_d)
    for l in range(2):
        ld(adabT[l], W[LNAME[l] + "_ada_bT"])
        ld(ngT[l], W[LNAME[l] + "_norm_gT"])
    ld(vng, vng_d)
    P.dma("pool", lambda e: e.dma_start(out=wsTb, in_=wsT_d), writes=[b_const])
    ld(bsb, bs_d)
    ld(gq, gq_d)
    ld(gk, gk_d)
    ld(gon, gon_d)
    ld(dww, dww_d)
    ld(dwb, dwb_d)
    ld(lng, lng_d)
    ld(lnb, lnb_d)
    ld(cTs, cT_d)
    P.dma("sp", lambda e: e.dma_start(out=lamrow[0:1, :], in_=lam_d), writes=[b_const])
    P.op("dve", lambda e: e.memset(onesb, 1.0), writes=[b_const])
    P.op("dve", lambda e: e.memset(onesf, 1.0), writes=[b_const])
    P.op("dve", lambda e: e.memset(epsc, EPS), writes=[b_const])
    P.op("dve", lambda e: e.tensor_copy(out=bsh[0:1], in_=bsb[0:1]), reads=[b_const], writes=[b_const])
    P.op("dve", lambda e: e.tensor_tensor(out=bsl[0:1], in0=bsb[0:1], in1=bsh[0:1], op=ALU.subtract),
         reads=[b_const], writes=[b_const])
    P.op("dve", lambda e: e.tensor_scalar(out=gon, in0=gon, scalar1=1.0 - LAMBDA_INIT0, scalar2=None,
                                          op0=ALU.mult), reads=[b_const], writes=[b_const])
    P.op("dve", lambda e: e.tensor_tensor(out=lamrow[0:1, 0:64], in0=lamrow[0:1, 0:64], in1=lamrow[0:1, 64:128],
                                          op=ALU.mult), reads=[b_const], writes=[b_const])
    P.op("dve", lambda e: e.tensor_tensor(out=lamrow[0:1, 128:192], in0=lamrow[0:1, 128:192],
                                          in1=lamrow[0:1, 192:256], op=ALU.mult), reads=[b_const], writes=[b_const])
    P.op("dve", lambda e: e.reduce_sum(out=lamtmp[0:1, 0:1], in_=lamrow[0:1, 0:64], axis=mybir.AxisListType.X),
         reads=[b_const], writes=[b_const])
    P.op("dve", lambda e: e.reduce_sum(out=lamtmp[0:1, 1:2], in_=lamrow[0:1, 128:192], axis=mybir.AxisListType.X),
         reads=[b_const], writes=[b_const])
    P.op("act", lambda e: e.activation(out=lamtmp[0:1, 2:4], in_=lamtmp[0:1, 0:2], func=AF.Exp),
         reads=[b_const], writes=[b_const])
    P.op("dve", lambda e: e.tensor_tensor(out=lamtmp[0:1, 4:5], in0=lamtmp[0:1, 3:4], in1=lamtmp[0:1, 2:3],
                                          op=ALU.subtract), reads=[b_const], writes=[b_const])
    P.op("dve", lambda e: e.tensor_scalar(out=lamtmp[0:1, 4:5], in0=lamtmp[0:1, 4:5], scalar1=-LAMBDA_INIT0,
                                          scalar2=None, op0=ALU.add), reads=[b_const], writes=[b_const])
    P.op("pe", lambda e: e.matmul(bank(0)[:, 0:1], lhsT=onesf[0:1, :], rhs=lamtmp[0:1, 4:5], start=True, stop=True),
         reads=[b_const], writes=[pbuf[0]])
    P.op("dve", lambda e: e.tensor_copy(out=nlam, in_=bank(0)[:, 0:1]), reads=[pbuf[0]], writes=[b_const])
    P.op("act", lambda e: e.activation(out=scT, in_=cTs, func=AF.Silu), reads=[b_const], writes=[b_const])
    P.op("dve", lambda e: e.tensor_copy(out=scTb, in_=scT), reads=[b_const], writes=[b_const])

    m0 = AR.mark()
    NWB = 2
    wblk = [AR.alloc([16, 512], BF16) for _ in range(NWB)]
    b_wblk = [Buf("wblk%d" % i) for i in range(NWB)]
    wctr = [0]

    def wload(src_w, col0, ncols=512):
        i = wctr[0] % NWB
        wctr[0] += 1
        dst = wblk[i][:, :, 0:ncols]
        src = src_w[:, col0:col0 + ncols].rearrange("(c p) n -> p c n", p=128)
        P.dma("pool", lambda e: e.dma_start(out=dst, in_=src), writes=[b_wblk[i]])
        return wblk[i], b_wblk[i]

    dgt = AR.alloc([128], F32)
    b_dgt = Buf("dgt")

    def ada_block(l, blk, wt, wb, pk):
        def mm(e):
            ins = None
            for j in range(4):
                for c in range(16):
                    ins = e.matmul(bank(pk)[:, 2 * j:2 * j + 2], lhsT=wt[:, c, j * 128:(j + 1) * 128],
                                   rhs=scTb[:, :, c], start=(c == 0), stop=(c == 15))
            return ins
        P.op("pe", mm, reads=[wb, b_const], writes=[pbuf[pk]])
        for v in range(2):
            P.op("dve", lambda e, v=v: e.tensor_tensor(
                out=mT[l][:, v, blk * 4:blk * 4 + 4],
                in0=bank(pk)[:, 0:8].rearrange("p (j v) -> p j v", v=2)[:, :, v],
                in1=adabT[l][:, blk * 4:blk * 4 + 4], op=ALU.add),
                reads=[pbuf[pk], b_const], writes=[b_mod])

    def ada_gp(l):
        for v in range(2):
            P.op("dve", lambda e, v=v: e.scalar_tensor_tensor(
                out=Gp[l][:, v, :], in0=mT[l][:, v, 16:32], scalar=1.0, in1=ngT[l], op0=ALU.add, op1=ALU.mult),
                reads=[b_mod, b_const], writes=[b_mod])

    def ada_gate(l, pk):
        for c in range(16):
            P.op("dve", lambda e, c=c: e.tensor_scalar(out=dgt, in0=identf, scalar1=mT[l][:, 0, 32 + c:33 + c],
                                                       scalar2=None, op0=ALU.mult),
                 reads=[b_mod, b_const], writes=[b_dgt])
            P.op("pe", lambda e, c=c: e.matmul(bank(pk)[:, (c % 4) * 128:(c % 4 + 1) * 128], lhsT=onesf, rhs=dgt,
                                               start=True, stop=True),
                 reads=[b_dgt, b_const], writes=[pbuf[pk]])
            if c % 4 == 3:
                P.op("act", lambda e, c=c: e.activation(
                    out=gt_bc[l][:, (c - 3) * 128:(c + 1) * 128], in_=bank(pk), func=AF.Copy),
                    reads=[pbuf[pk]], writes=[b_mod])

    wada0 = W["e_ada_w"]
    nxt = wload(wada0, 0)
    for blk in range(8):
        wt, wb = nxt
        if blk + 1 < 8:
            nxt = wload(wada0, (blk + 1) * 512)
        ada_block(0, blk, wt, wb, blk % 2)
    ada_gp(0)
    lazy = {"items": [("blk", 0, b) for b in range(8, 12)] + [("gate", 0)] +
                     [("blk", 1, b) for b in range(12)] + [("gp", 1), ("gate", 1)],
            "i": 0, "nxt": None}

    def lazy_step(pk):
        it = lazy["items"]
        i = lazy["i"]
        if i >= len(it):
            return False
        lazy["i"] = i + 1
        item = it[i]
        if item[0] == "blk":
            if lazy["nxt"] is None:
                lazy["nxt"] = wload(W[LNAME[item[1]] + "_ada_w"], item[2] * 512)
            wt, wb = lazy["nxt"]
            lazy["nxt"] = None
            for j in range(i + 1, len(it)):
                if it[j][0] == "blk":
                    lazy["nxt"] = wload(W[LNAME[it[j][1]] + "_ada_w"], it[j][2] * 512)
                    break
            ada_block(item[1], item[2], wt, wb, pk)
        elif item[0] == "gp":
            ada_gp(item[1])
        else:
            ada_gate(item[1], pk)
        return True
    P.barrier()
    if stop_after == "P0":
        return _finish(nc, P)

    m_pre_h = AR.mark()
    hT = AR.alloc([16, NF], BF16)
    b_hTc = [Buf("hT%d" % c) for c in range(16)]
    m_h = AR.mark()

    def xprep(l, tile_src, ntiles, vec_of_tile, post_tile=None):
        xn4 = [AR.alloc([4, D], BF16) for _ in range(2)]
        b_xn4 = [Buf("xn4_%d" % i) for i in range(2)]
        junk = AR.alloc([D], BF16)
        b_junk = Buf("junk")
        st = AR.alloc([64], F32)
        b_stk = [Buf("st%d" % k) for k in range(32)]
        ngroups = (ntiles + 3) // 4
        for gi in range(ngroups):
            tl = list(range(gi * 4, min(ntiles, gi * 4 + 4)))
            xb = xn4[gi % 2]
            bx = b_xn4[gi % 2]
            for t in tl:
                xt_ap, xt_b = tile_src(t)
                k = t % 32
                P.op("dve", lambda e, xt_ap=xt_ap, k=k: e.scalar_tensor_tensor(
                    out=junk, in0=xt_ap, scalar=1.0, in1=xt_ap, op0=ALU.mult, op1=ALU.mult,
                    accum_out=st[:, k:k + 1]), reads=[xt_b], writes=[b_junk, b_stk[k]])
                P.op("act", lambda e, k=k: e.activation(out=st[:, 32 + k:33 + k], in_=st[:, k:k + 1], func=AF.Ln,
                                                        bias=epsc, scale=1.0 / D), reads=[b_stk[k], b_const], writes=[b_stk[k]])
                P.op("act", lambda e, k=k: e.activation(out=st[:, 32 + k:33 + k], in_=st[:, 32 + k:33 + k],
                                                        func=AF.Exp, scale=-0.5), reads=[b_stk[k]], writes=[b_stk[k]])
                P.op("act", lambda e, xt_ap=xt_ap, k=k, xb=xb, t=t: e.activation(
                    out=xb[:, t % 4, :], in_=xt_ap, func=AF.Copy, scale=st[:, 32 + k:33 + k]),
                    reads=[xt_b, b_stk[k]], writes=[bx])
                if post_tile is not None:
                    post_tile(t, xt_ap, xt_b)
            nt = len(tl)
            runs = []
            for tt, t in enumerate(tl):
                v = vec_of_tile(t)
                if runs and runs[-1][0] == v:
                    runs[-1][2] += 1
                else:
                    runs.append([v, tt, 1])
            for c in range(16):
                pk = c % 4

                def tr(e, c=c, pk=pk, xb=xb, nt=nt):
                    ins = None
                    pv = bank(pk)[:, 0:256].bitcast(BF16).rearrange("p (a b) -> p a b", a=4)
                    for tt in range(nt):
                        ins = e.transpose(out=pv[:, tt, :], in_=xb[:, tt, c * 128:(c + 1) * 128], identity=identb)
                    return ins
                P.op("pe", tr, reads=[bx, b_const], writes=[pbuf[pk]])
                for (v, tt0, ntt) in runs:
                    if c % 2 == 0:
                        P.op("dve", lambda e, c=c, pk=pk, tt0=tt0, ntt=ntt, t0=tl[0], v=v, l=l: e.tensor_scalar(
                            out=hT[:, c, (t0 + tt0) * 128:(t0 + tt0 + ntt) * 128],
                            in0=bank(pk)[:, 0:256].bitcast(BF16)[:, tt0 * 128:(tt0 + ntt) * 128],
                            scalar1=Gp[l][:, v, c:c + 1], scalar2=mT[l][:, v, c:c + 1], op0=ALU.mult, op1=ALU.add),
                            reads=[pbuf[pk], b_mod], writes=[b_hTc[c]])
                    else:
                        P.op("act", lambda e, c=c, pk=pk, tt0=tt0, ntt=ntt, t0=tl[0], v=v, l=l: e.activation(
                            out=hT[:, c, (t0 + tt0) * 128:(t0 + tt0 + ntt) * 128],
                            in_=bank(pk)[:, 0:256].bitcast(BF16)[:, tt0 * 128:(tt0 + ntt) * 128],
                            func=AF.Identity, scale=Gp[l][:, v, c:c + 1], bias=mT[l][:, v, c:c + 1]),
                            reads=[pbuf[pk], b_mod], writes=[b_hTc[c]])

    def dram_tile_src(src_d, base_tile, src_bufs=None):
        xt = [AR.alloc([D], F32) for _ in range(3)]
        b_xt = [Buf("xt%d" % i) for i in range(3)]

        def src(t):
            i = t % 3
            g = base_tile + t
            rd = [src_bufs[t]] if src_bufs is not None else []
            P.dma("sp", lambda e, i=i, g=g: e.dma_start(out=xt[i], in_=src_d[g * 128:(g + 1) * 128, :]),
                  reads=rd, writes=[b_xt[i]])
            return xt[i], b_xt[i]
        return src

    def proj_phase(which):
        tok_off = NF if which == "B" else 0
        if which == "AV":
            al = lambda shape, dt: None
        else:
            al = AR.alloc
        kg = [al([512], BF16) for _ in range(2)]
        ksq = [al([512], BF16) for _ in range(2)]
        b_kg = [Buf() for _ in range(2)]
        b_ksq = [Buf() for _ in range(2)]
        sq = al([512], F32)
        t1 = al([512], F32)
        t2 = al([512], F32)
        b_sq, b_t1, b_t2 = Buf(), Buf(), Buf()
        cs = [al([2, 512], F32) for _ in range(2)]
        b_cs = [Buf() for _ in range(2)]
        stg = [al([NF], BF16) for _ in range(2)]
        b_stg = [Buf() for _ in range(2)]
        vst = [al([1024], BF16) for _ in range(2)]
        b_vst = [Buf() for _ in range(2)]
        sctr = [0]
        pctr = [0]

        def next_bank(lo=0, n=4):
            k = lo + pctr[0] % n
            pctr[0] += 1
            return k

        def fm_proj(wsrc, col0, nheads, epi, dst_d):
            nblk = (nheads + 3) // 4
            pending = [None]

            def flush():
                if pending[0] is not None:
                    pending[0]()
                    pending[0] = None
            nxt = wload(wsrc, col0)
            for bi in range(nblk):
                wt, wb = nxt
                if bi + 1 < nblk:
                    nxt = wload(wsrc, col0 + (bi + 1) * 512)
                for hh in range(4):
                    hd = bi * 4 + hh
                    si = sctr[0] % 2
                    sctr[0] += 1
                    for ti, (t0, n) in enumerate(FT):
                        pk = next_bank(0, 4)

                        def mm(e, wt=wt, hh=hh, t0=t0, n=n, pk=pk):
                            ins = None
                            for c in range(16):
                                ins = e.matmul(bank(pk)[:, 0:n], lhsT=wt[:, c, hh * 128:(hh + 1) * 128],
                                               rhs=hT[:, c, t0:t0 + n], start=(c == 0), stop=(c == 15))
                            return ins
                        P.op("pe", mm, reads=[wb] + b_hTc, writes=[pbuf[pk]])
                        flush()

                        def ep(pk=pk, t0=t0, n=n, si=si, hd=hd, last=(ti == len(FT) - 1)):
                            epi(pk, t0, n, stg[si], b_stg[si])
                            if last:
                                P.dma("sp", lambda e: e.dma_start(
                                    out=dst_d[hd, :, tok_off:tok_off + NF] if dst_d is K_d else dst_d[hd],
                                    in_=stg[si]), reads=[b_stg[si]])
                        pending[0] = ep
            flush()

        def epi_act(func):
            def epi(pk, t0, n, sg_ap, sg_b):
                P.op("act", lambda e: e.activation(out=sg_ap[:, t0:t0 + n], in_=bank(pk)[:, 0:n], func=func),
                     reads=[pbuf[pk]], writes=[sg_b])
            return epi

        qk_ctr = [0]

        def epi_qk(gcol):
            def epi(pk, t0, n, sg_ap, sg_b):
                i = qk_ctr[0] % 2
                qk_ctr[0] += 1
                P.dma("sp", lambda e: e.dma_start(out=cs[i][:, 0, 0:n], in_=cos_d[:, tok_off + t0:tok_off + t0 + n]),
                      writes=[b_cs[i]])
                P.dma("sp", lambda e: e.dma_start(out=cs[i][:, 1, 0:n], in_=sin_d[:, tok_off + t0:tok_off + t0 + n]),
                      writes=[b_cs[i]])
                P.op("act", lambda e: e.activation(out=kg[i][:, 0:n], in_=bank(pk)[:, 0:n], func=AF.Copy, scale=gcol),
                     reads=[pbuf[pk], b_const], writes=[b_kg[i]])
                P.op("act", lambda e: e.activation(out=ksq[i][:, 0:n], in_=bank(pk)[:, 0:n], func=AF.Square),
                     reads=[pbuf[pk]], writes=[b_ksq[i]])
                p2 = next_bank(4, 4)
                p3 = next_bank(4, 4)
                P.op("pe", lambda e: e.matmul(bank(p2)[:, 0:n], lhsT=blk64, rhs=ksq[i][:, 0:n], start=True, stop=True),
                     reads=[b_ksq[i], b_const], writes=[pbuf[p2]])
                P.op("pe", lambda e: e.matmul(bank(p3)[:, 0:n], lhsT=rotT, rhs=kg[i][:, 0:n], start=True, stop=True),
                     reads=[b_kg[i], b_const], writes=[pbuf[p3]])
                P.op("act", lambda e: e.activation(out=sq[:, 0:n], in_=bank(p2)[:, 0:n], func=AF.Ln, bias=epsc,
                                                   scale=1.0 / 64), reads=[pbuf[p2], b_const], writes=[b_sq])
                P.op("act", lambda e: e.activation(out=sq[:, 0:n], in_=sq[:, 0:n], func=AF.Exp, scale=-0.5),
                     reads=[b_sq], writes=[b_sq])
                P.op("dve", lambda e: e.tensor_tensor(out=t1[:, 0:n], in0=kg[i][:, 0:n], in1=cs[i][:, 0, 0:n],
                                                      op=ALU.mult), reads=[b_kg[i], b_cs[i]], writes=[b_t1])
                P.op("dve", lambda e: e.tensor_tensor(out=t2[:, 0:n], in0=bank(p3)[:, 0:n], in1=cs[i][:, 1, 0:n],
                                                      op=ALU.mult), reads=[pbuf[p3], b_cs[i]], writes=[b_t2])
                P.op("dve", lambda e: e.tensor_tensor(out=t1[:, 0:n], in0=t1[:, 0:n], in1=t2[:, 0:n], op=ALU.add),
                     reads=[b_t1, b_t2], writes=[b_t1])
                P.op("dve", lambda e: e.tensor_tensor(out=sg_ap[:, t0:t0 + n], in0=t1[:, 0:n], in1=sq[:, 0:n],
                                                      op=ALU.mult), reads=[b_t1, b_sq], writes=[sg_b])
            return epi

        def tm_proj(wsrc, col0, epi_tm):
            wA = wload(wsrc, col0)
            wB = wload(wsrc, col0 + 512)
            for t in range(17):
                for bi, (wt, wb) in enumerate((wA, wB)):
                    pk = next_bank(0, 4)

                    def mm(e, wt=wt, t=t, pk=pk):
                        ins = None
                        for c in range(16):
                            ins = e.matmul(bank(pk), lhsT=hT[:, c, t * 128:(t + 1) * 128], rhs=wt[:, c, :],
                                           start=(c == 0), stop=(c == 15))
                        return ins
                    P.op("pe", mm, reads=[wb] + b_hTc, writes=[pbuf[pk]])
                    epi_tm(t, bi, pk)

        def epi_v(t, bi, pk):
            i = t % 2
            P.op("act", lambda e: e.activation(out=vst[i][:, bi * 512:(bi + 1) * 512], in_=bank(pk), func=AF.Copy),
                 reads=[pbuf[pk]], writes=[b_vst[i]])
            if bi == 1:
                g = tok_off + t * 128
                P.dma("sp", lambda e: e.dma_start(out=V_d[g:g + 128, :], in_=vst[i]), reads=[b_vst[i]])

        w_in0 = W["e_w_in"]
        if which != "AV":
            fm_proj(w_in0, 4096, 8, epi_qk(gk), K_d)
            tm_proj(w_in0, 5120, epi_v)
        if which == "A":
            fm_proj(w_in0, 3072, 8, epi_qk(gq), Q_d)
            fm_proj(w_in0, 6144, 8, epi_act(AF.Silu), G_d)
            fm_proj(w_in0, 0, 8, epi_act(AF.Gelu), AU_d)
            fm_proj(w_in0, 2048, 8, epi_act(AF.Silu), AG_d)
        if which == "AV":
            gv = [AR.alloc([1024], F32) for _ in range(2)]
            b_gv = [Buf() for _ in range(2)]
            vjunk = AR.alloc([1024], BF16)
            b_vjunk = Buf()
            ssv = AR.alloc([64], F32)
            b_ssvt = [Buf() for _ in range(17)]

            def epi_av(t, bi, pk):
                i = t % 2
                P.op("act", lambda e: e.activation(out=gv[i][:, bi * 512:(bi + 1) * 512], in_=bank(pk), func=AF.Gelu),
                     reads=[pbuf[pk]], writes=[b_gv[i]])
                if bi == 1:
                    P.op("act", lambda e: e.activation(out=vjunk, in_=gv[i], func=AF.Square,
                                                       accum_out=ssv[:, t:t + 1]),
                         reads=[b_gv[i]], writes=[b_vjunk, b_ssvt[t]])
                    P.op("act", lambda e: e.activation(out=ssv[:, 32 + t:33 + t], in_=ssv[:, t:t + 1], func=AF.Sqrt,
                                                       bias=epsc, scale=1.0 / 1024), reads=[b_ssvt[t], b_const],
                         writes=[b_ssvt[t]])
                    P.op("dve", lambda e: e.reciprocal(out=ssv[:, 32 + t:33 + t], in_=ssv[:, 32 + t:33 + t]),
                         reads=[b_ssvt[t]], writes=[b_ssvt[t]])
                    P.op("dve", lambda e: e.scalar_tensor_tensor(out=vn_all[:, t, :], in0=gv[i],
                                                                 scalar=ssv[:, 32 + t:33 + t], in1=vng,
                                                                 op0=ALU.mult, op1=ALU.mult),
                         reads=[b_gv[i], b_ssvt[t], b_const], writes=[b_vn])
            tm_proj(w_in0, 1024, epi_av)

    vn_all = None
    b_vn = Buf("vn")
    for which in ("B", "A"):
        AR.reset(m_h)
        base_tile = 17 if which == "B" else 0
        xprep(0, dram_tile_src(xc, base_tile), 17,
              (lambda t: 1 if (which == "B" and t >= 15) else 0))
        if debug and which == "A":
            P.dma("sp", lambda e: e.dma_start(out=HT_d.rearrange("c p t -> p c t"), in_=hT), reads=b_hTc)
        P.barrier()
        AR.reset(m_h)
        proj_phase(which)
        P.barrier()
        if which == "A":
            AR.reset(m_h)
            vn_all = AR.alloc([17, 1024], BF16)
            proj_phase("AV")
            P.barrier()
        if stop_after == "P2" + which:
            return _finish(nc, P)
    m_vn = m_h + 17 * 512
    AR.reset(m_vn)

    def mix_phase():
        au = [AR.alloc([NF], BF16) for _ in range(2)]
        ag = [AR.alloc([NF], BF16) for _ in range(2)]
        b_au = [Buf() for _ in range(2)]
        b_ag = [Buf() for _ in range(2)]
        ystg = [AR.alloc([NF], BF16) for _ in range(2)]
        b_ystg = [Buf() for _ in range(2)]
        pc = [0]

        def load(g):
            i = g % 2
            P.dma("sp", lambda e: e.dma_start(out=au[i], in_=AU_d[g]), writes=[b_au[i]])
            P.dma("sp", lambda e: e.dma_start(out=ag[i], in_=AG_d[g]), writes=[b_ag[i]])
        load(0)
        for g in range(8):
            i = g % 2
            if g + 1 < 8:
                load(g + 1)
            P.op("dve", lambda e, i=i: e.tensor_tensor(out=au[i], in0=au[i], in1=ag[i], op=ALU.mult),
                 reads=[b_au[i], b_ag[i]], writes=[b_au[i]])
            for nb in range(5):
                tl = list(range(nb * 4, min(17, nb * 4 + 4)))
                nt = len(tl)
                pk = pc[0] % 4
                pc[0] += 1

                def mm(e, g=g, tl=tl, pk=pk):
                    ins = None
                    for sl, n in enumerate(tl):
                        o = bank(pk)[:, sl * 128:(sl + 1) * 128]
                        e.matmul(o, lhsT=vn_all[:, n, g * 128:(g + 1) * 128], rhs=wsTb[:, g, :], start=True, stop=False)
                        e.matmul(o, lhsT=onesb[0:1, :], rhs=bsh[0:1, g, :], start=False, stop=False)
                        ins = e.matmul(o, lhsT=onesb[0:1, :], rhs=bsl[0:1, g, :], start=False, stop=True)
                    return ins
                P.op("pe", mm, reads=[b_vn, b_const], writes=[pbuf[pk]])
                P.op("dve", lambda e, i=i, nt=nt, t0=tl[0], pk=pk: e.tensor_tensor(
                    out=ystg[i][:, t0 * 128:(t0 + nt) * 128], in0=bank(pk)[:, 0:nt * 128],
                    in1=au[i][:, t0 * 128:(t0 + nt) * 128], op=ALU.mult),
                    reads=[pbuf[pk], b_au[i]], writes=[b_ystg[i]])
            P.dma("sp", lambda e, g=g, i=i: e.dma_start(out=Y_d[g], in_=ystg[i]), reads=[b_ystg[i]], writes=[b_Y])
    b_Y = Buf("Y_d")
    mix_phase()
    P.barrier()
    if stop_after == "P3":
        return _finish(nc, P)

    AR.reset(m_pre_h)

    def attn_phase():
        kT = [AR.alloc([NTOK], BF16) for _ in range(2)]
        vh = [AR.alloc([34, 128], BF16) for _ in range(2)]
        qT = [AR.alloc([NF], BF16) for _ in range(2)]
        sbg = [AR.alloc([NF], BF16) for _ in range(2)]
        b_in = [Buf() for _ in range(2)]
        pT = [AR.alloc([2, 512], BF16) for _ in range(3)]
        b_pT = [Buf() for _ in range(3)]
        rz = AR.alloc([2, 512], F32)
        o1 = AR.alloc([512], F32)
        o2 = AR.alloc([512], F32)
        rs = AR.alloc([512], F32)
        osq = AR.alloc([512], BF16)
        b_rz, b_o1, b_o2, b_rs, b_osq = Buf(), Buf(), Buf(), Buf(), Buf()
        ystg = [AR.alloc([NF], BF16) for _ in range(2)]
        b_ystg = [Buf() for _ in range(2)]
        b_s = [Buf(), Buf()]
        b_o = [pbuf[4], pbuf[5]]
        b_z = pbuf[6]
        zs = AR.alloc([512], F32)
        b_zs = Buf()
        zh = AR.alloc([512], BF16)
        zl = AR.alloc([512], BF16)
        b_zh, b_zl = Buf(), Buf()
        sctr = [0]
        pctr = [0]

        def load_head(hd):
            i = hd % 2
            P.dma("sp", lambda e: e.dma_start(out=kT[i], in_=K_d[hd]), writes=[b_in[i]])
            P.dma("sp", lambda e: e.dma_start(
                out=vh[i], in_=V_d[:, hd * 128:(hd + 1) * 128].rearrange("(t p) d -> p t d", p=128)),
                writes=[b_in[i]])
            P.dma("sp", lambda e: e.dma_start(out=qT[i], in_=Q_d[hd]), writes=[b_in[i]])
            P.dma("sp", lambda e: e.dma_start(out=sbg[i], in_=G_d[hd]), writes=[b_in[i]])

        pend = []

        def flush_one():
            if pend:
                pend.pop(0)()

        its = [(hd, qi, kt) for hd in range(8) for qi in range(len(FT)) for kt in range(34)]
        sis = {}

        def issue_s(j):
            hd, qi, kt = its[j]
            i = hd % 2
            t0, n = FT[qi]
            si = sctr[0] % 2
            sctr[0] += 1
            sis[j] = si

            def f(e):
                e.matmul(pp[si][:, 0, 0:n], lhsT=kT[i][0:64, kt * 128:(kt + 1) * 128],
                         rhs=qT[i][0:64, t0:t0 + n], start=True, stop=True)
                return e.matmul(pp[si][:, 1, 0:n], lhsT=kT[i][64:128, kt * 128:(kt + 1) * 128],
                                rhs=qT[i][64:128, t0:t0 + n], start=True, stop=True)
            P.op("pe", f, reads=[b_in[i]], writes=[b_s[si]])

        def epilogue(hd, qi):
            i = hd % 2
            t0, n = FT[qi]
            P.op("dve", lambda e: e.tensor_copy(out=o1[:, 0:n], in_=bank(4)[:, 0:n]), reads=[b_o[0]], writes=[b_o1])
            P.op("dve", lambda e: e.tensor_copy(out=o2[:, 0:n], in_=bank(5)[:, 0:n]), reads=[b_o[1]], writes=[b_o2])
            P.op("dve", lambda e: e.tensor_copy(out=zs[0:64, 0:n], in_=bank(6)[0:64, 0:n]), reads=[b_z], writes=[b_zs])
            lazy_step(7)

            def stage_0():
                P.op("dve", lambda e: e.reciprocal(out=zs[0:64, 0:n], in_=zs[0:64, 0:n]), reads=[b_zs], writes=[b_zs])
                P.op("dve", lambda e: e.tensor_copy(out=zh[0:64, 0:n], in_=zs[0:64, 0:n]), reads=[b_zs], writes=[b_zh])
                P.op("dve", lambda e: e.tensor_tensor(out=zl[0:64, 0:n], in0=zs[0:64, 0:n], in1=zh[0:64, 0:n],
                                                      op=ALU.subtract), reads=[b_zs, b_zh], writes=[b_zl])

            def zbc(row):
                def f(e):
                    e.matmul(bank(7)[:, 0:n], lhsT=onesb[row:row + 1, :], rhs=zh[row:row + 1, 0:n],
                             start=True, stop=False)
                    return e.matmul(bank(7)[:, 0:n], lhsT=onesb[row:row + 1, :], rhs=zl[row:row + 1, 0:n],
                                    start=False, stop=True)
                return f

            def stage_a():
                P.op("pe", zbc(0), reads=[b_zh, b_zl, b_const], writes=[pbuf[7]])
                P.op("dve", lambda e: e.tensor_tensor(out=o1[:, 0:n], in0=o1[:, 0:n], in1=bank(7)[:, 0:n],
                                                      op=ALU.mult), reads=[b_o1, pbuf[7]], writes=[b_o1])

            def stage_b():
                P.op("pe", zbc(32), reads=[b_zh, b_zl, b_const], writes=[pbuf[7]])
                P.op("dve", lambda e: e.tensor_tensor(out=o2[:, 0:n], in0=o2[:, 0:n], in1=bank(7)[:, 0:n],
                                                      op=ALU.mult), reads=[b_o2, pbuf[7]], writes=[b_o2])
                P.op("dve", lambda e: e.scalar_tensor_tensor(out=o1[:, 0:n], in0=o2[:, 0:n], scalar=nlam,
                                                             in1=o1[:, 0:n], op0=ALU.mult, op1=ALU.add),
                     reads=[b_o1, b_o2, b_const], writes=[b_o1])
                P.op("dve", lambda e: e.tensor_tensor(out=osq[:, 0:n], in0=o1[:, 0:n], in1=o1[:, 0:n], op=ALU.mult),
                     reads=[b_o1], writes=[b_osq])

            def stage_c1():
                P.op("pe", lambda e: e.matmul(bank(7)[:, 0:n], lhsT=onesb, rhs=osq[:, 0:n], start=True, stop=True),
                     reads=[b_osq, b_const], writes=[pbuf[7]])

            def stage_c():
                P.op("act", lambda e: e.activation(out=rs[:, 0:n], in_=bank(7)[:, 0:n], func=AF.Ln,
                                                   bias=epsc, scale=1.0 / 128),
                     reads=[pbuf[7], b_const], writes=[b_rs])
                P.op("act", lambda e: e.activation(out=rs[:, 0:n], in_=rs[:, 0:n], func=AF.Exp, scale=-0.5),
                     reads=[b_rs], writes=[b_rs])
                P.op("dve", lambda e: e.tensor_tensor(out=o1[:, 0:n], in0=o1[:, 0:n], in1=rs[:, 0:n],
                                                      op=ALU.mult), reads=[b_o1, b_rs], writes=[b_o1])
                P.op("dve", lambda e: e.scalar_tensor_tensor(
                    out=ystg[i][:, t0:t0 + n], in0=o1[:, 0:n], scalar=gon, in1=sbg[i][:, t0:t0 + n],
                    op0=ALU.mult, op1=ALU.mult), reads=[b_o1, b_in[i], b_const], writes=[b_ystg[i]])
                if qi == len(FT) - 1:
                    P.dma("sp", lambda e: e.dma_start(out=Y_d[8 + hd], in_=ystg[i]),
                          reads=[b_ystg[i]], writes=[b_Y])
            pend.extend([stage_0, stage_a, stage_b, stage_c1, stage_c])

        load_head(0)
        issue_s(0)
        issue_s(1)
        for j, (hd, qi, kt) in enumerate(its):
            i = hd % 2
            t0, n = FT[qi]
            si = sis[j]
            pi = pctr[0] % 3
            pctr[0] += 1
            P.op("act", lambda e, si=si, pi=pi, n=n: e.activation(
                out=pT[pi][:, :, 0:n], in_=pp[si][:, :, 0:n], func=AF.Exp, scale=0.125),
                reads=[b_s[si]], writes=[b_pT[pi]])
            if j + 2 < len(its):
                issue_s(j + 2)

            def pv(e, kt=kt, pi=pi, i=i, n=n):
                st, sp_ = (kt == 0), (kt == 33)
                e.matmul(bank(4)[:, 0:n], lhsT=vh[i][:, kt, :], rhs=pT[pi][:, 0, 0:n], start=st, stop=sp_)
                e.matmul(bank(5)[:, 0:n], lhsT=vh[i][:, kt, :], rhs=pT[pi][:, 1, 0:n], start=st, stop=sp_)
                e.matmul(bank(6)[0:32, 0:n], lhsT=onesb[:, 0:32], rhs=pT[pi][:, 0, 0:n], start=st, stop=sp_,
                         tile_position=(0, 0))
                return e.matmul(bank(6)[32:64, 0:n], lhsT=onesb[:, 0:32], rhs=pT[pi][:, 1, 0:n], start=st,
                                stop=sp_, tile_position=(0, 32))
            P.op("pe", pv, reads=[b_pT[pi], b_in[i], b_const], writes=[b_o[0], b_o[1], b_z])
            if kt in (3, 8, 13, 18, 21):
                flush_one()
            if kt == 23 and qi == 0 and hd + 1 < 8:
                load_head(hd + 1)
            if kt == 33:
                epilogue(hd, qi)
        while pend:
            flush_one()
    attn_phase()
    while lazy_step(7):
        pass
    if debug:
        for l in range(2):
            P.dma("sp", lambda e, l=l: e.dma_start(out=MOD_d[:, l * 96:(l + 1) * 96],
                                                   in_=mT[l].rearrange("p v c -> p (v c)")), reads=[b_mod])
            P.dma("sp", lambda e, l=l: e.dma_start(out=MOD_d[:, 192 + l * 32:192 + (l + 1) * 32],
                                                   in_=Gp[l].rearrange("p v c -> p (v c)")), reads=[b_mod])
            P.dma("sp", lambda e, l=l: e.dma_start(out=GT_d[l], in_=gt_bc[l]), reads=[b_mod])
    P.barrier()
    if stop_after == "P4":
        return _finish(nc, P)

    def outproj(l, w_out, ntiles, resid_d, dst_d, b_dst):
        xr = [AR.alloc([512], F32) for _ in range(4)]
        b_xr = [Buf() for _ in range(4)]
        t1 = [AR.alloc([512], F32) for _ in range(3)]
        b_t1 = [Buf() for _ in range(3)]
        units = [(j, t) for j in range(4) for t in range(ntiles)]

        def load(u):
            j, t = units[u]
            xi = u % 4
            P.dma("sp", lambda e: e.dma_start(
                out=xr[xi], in_=resid_d[t * 128:(t + 1) * 128, j * 512:(j + 1) * 512]), writes=[b_xr[xi]])
        load(0)
        load(1)
        nxt = wload(w_out, 0)
        wt = wb = None
        for u, (j, t) in enumerate(units):
            if t == 0:
                wt, wb = nxt
                if j + 1 < 4:
                    nxt = wload(w_out, (j + 1) * 512)
            if u + 2 < len(units):
                load(u + 2)
            pk = u % 4
            xi = u % 4
            ti = u % 3

            def mm(e, wt=wt, t=t, pk=pk):
                ins = None
                for c in range(16):
                    ins = e.matmul(bank(pk), lhsT=hT[:, c, t * 128:(t + 1) * 128], rhs=wt[:, c, :],
                                   start=(c == 0), stop=(c == 15))
                return ins
            P.op("pe", mm, reads=[wb] + b_hTc, writes=[pbuf[pk]])
            P.op("dve", lambda e, pk=pk, ti=ti, j=j: e.tensor_tensor(
                out=t1[ti], in0=bank(pk), in1=gt_bc[l][:, j * 512:(j + 1) * 512], op=ALU.mult),
                reads=[pbuf[pk], b_mod], writes=[b_t1[ti]])
            P.op("dve", lambda e, ti=ti, xi=xi: e.tensor_tensor(out=t1[ti], in0=t1[ti], in1=xr[xi], op=ALU.add),
                 reads=[b_t1[ti], b_xr[xi]], writes=[b_t1[ti]])
            P.dma("act", lambda e, ti=ti, t=t, j=j: e.dma_start(
                out=dst_d[t * 128:(t + 1) * 128, j * 512:(j + 1) * 512], in_=t1[ti]),
                reads=[b_t1[ti]], writes=[b_dst[t]])

    AR.reset(m_h)
    for c in range(16):
        P.dma("sp", lambda e, c=c: e.dma_start(out=hT[:, c, :], in_=Y_d[c]), reads=[b_Y], writes=[b_hTc[c]])
    b_X1 = [Buf("x1_%d" % t) for t in range(17)]
    outproj(0, W["e_w_out"], 17, xc, X1_d, b_X1)
    P.barrier()
    if stop_after == "P5":
        return _finish(nc, P)

    AR.reset(m_h)
    xprep(1, dram_tile_src(X1_d, 0, b_X1), 17, (lambda t: 0))
    if debug:
        P.dma("sp", lambda e: e.dma_start(out=HT_d.rearrange("c p t -> p c t"), in_=hT), reads=b_hTc)
    P.barrier()
    if stop_after == "P6":
        return _finish(nc, P)

    AR.reset(m_h)
    acc_s = AR.alloc([NOWN], F32)
    acc_q = AR.alloc([NOWN], F32)
    b_acc = Buf("acc")
    m_acc = AR.mark()
    GL = 15 + NOWN + 15
    w_in1 = W["o_w_in"]
    glub = [AR.alloc([GL + 2], BF16) for _ in range(2)]
    b_glub = [Buf() for _ in range(2)]
    _sig0 = vng[:, 512:1024]
    sig = [_sig0, _sig0]
    _bsig0 = Buf()
    b_sig = [_bsig0, _bsig0]
    gh = lamrow[:, 0:128]
    b_gh = Buf()
    sgs = AR.alloc([NOWN], BF16)
    b_sgs = Buf()
    NPE = 16
    dg2 = [AR.alloc([NPE, 128], BF16) for _ in range(2)]
    b_dg2 = [Buf() for _ in range(2)]
    dacc2 = [AR.alloc([NOWN], F32) for _ in range(2)]
    b_dacc2 = [Buf() for _ in range(2)]
    ych = [AR.alloc([NOWN], F32) for _ in range(2)]
    b_ych = [Buf() for _ in range(2)]
    _sq0 = vng[:, 0:512]
    sq5 = [_sq0, _sq0]
    _bsq0 = Buf()
    b_sq5 = [_bsq0, _bsq0]
    P.op("pool", lambda e: e.memset(acc_s, 0.0), writes=[b_acc])
    P.op("pool", lambda e: e.memset(acc_q, 0.0), writes=[b_acc])
    b_YC = [Buf() for _ in range(16)]
    b_SG = [Buf() for _ in range(16)]

    def w3load(cc):
        i = wctr[0] % NWB
        wctr[0] += 1
        for k3 in range(3):
            dst = wblk[i][:, :, k3 * 128:(k3 + 1) * 128]
            src = w_in1[:, k3 * 2048 + cc * 128:k3 * 2048 + (cc + 1) * 128].rearrange("(c p) n -> p c n", p=128)
            P.dma("pool", lambda e, dst=dst, src=src: e.dma_start(out=dst, in_=src), writes=[b_wblk[i]])
        return wblk[i], b_wblk[i]

    nxt = w3load(0)
    ctr = 0
    tap_q = []
    fin_q = []

    def drain_taps(k):
        for _ in range(min(k, len(tap_q))):
            tap_q.pop(0)()

    for cc in range(16):
        wt, wb = nxt
        if cc + 1 < 16:
            nxt = w3load(cc + 1)
        gi = cc % 2
        dg = dg2[gi]
        b_dg = b_dg2[gi]
        for tap in range(NPE):
            P.op("act", lambda e, tap=tap, cc=cc, dg=dg: e.activation(
                out=dg[:, tap, :], in_=identf, func=AF.Copy, scale=dww[:, cc, tap:tap + 1]),
                reads=[b_const], writes=[b_dg])
        for (t0, n) in FT:
            is_halo = (t0 == NOWN)
            pa, pb_, pg = (ctr * 3) % 6, (ctr * 3 + 1) % 6, (ctr * 3 + 2) % 6
            si = ctr % 2
            ctr += 1

            def mm3(e, wt=wt, t0=t0, n=n, pa=pa, pb_=pb_, pg=pg, is_halo=is_halo):
                ins = None
                for k3, pk in enumerate((pa, pb_, pg)):
                    if k3 == 2 and is_halo:
                        continue
                    for c in range(16):
                        ins = e.matmul(bank(pk)[:, 0:n], lhsT=wt[:, c, k3 * 128:(k3 + 1) * 128],
                                       rhs=hT[:, c, t0:t0 + n], start=(c == 0), stop=(c == 15))
                return ins
            P.op("pe", mm3, reads=[wb] + b_hTc, writes=[pbuf[pa], pbuf[pb_]] + ([] if is_halo else [pbuf[pg]]))
            P.op("act", lambda e, si=si, pb_=pb_, n=n: e.activation(out=sig[si][:, 0:n], in_=bank(pb_)[:, 0:n],
                                                                    func=AF.Sigmoid),
                 reads=[pbuf[pb_]], writes=[b_sig[si]])
            if not is_halo:
                P.op("dve", lambda e, si=si, pa=pa, n=n, t0=t0, gi=gi: e.tensor_tensor(
                    out=glub[gi][:, 15 + t0:15 + t0 + n], in0=bank(pa)[:, 0:n], in1=sig[si][:, 0:n], op=ALU.mult),
                    reads=[pbuf[pa], b_sig[si]], writes=[b_glub[gi]])
                P.op("act", lambda e, pg=pg, n=n, t0=t0: e.activation(out=sgs[:, t0:t0 + n], in_=bank(pg)[:, 0:n],
                                                                      func=AF.Silu),
                     reads=[pbuf[pg]], writes=[b_sgs])
            else:
                P.op("dve", lambda e, si=si, pa=pa: e.tensor_tensor(out=gh, in0=bank(pa)[:, 0:128],
                                                                    in1=sig[si][:, 0:128], op=ALU.mult),
                     reads=[pbuf[pa], b_sig[si]], writes=[b_gh])
                P.op("dve", lambda e, gi=gi: e.tensor_scalar(out=glub[gi][:, 0:15], in0=gh[:, 113:128],
                                                             scalar1=halo[:, 0:1], scalar2=None, op0=ALU.mult),
                     reads=[b_gh, b_const], writes=[b_glub[gi]])
                P.op("dve", lambda e, gi=gi: e.tensor_scalar(out=glub[gi][:, 15 + NOWN:30 + NOWN], in0=gh[:, 0:15],
                                                             scalar1=halo[:, 1:2], scalar2=None, op0=ALU.mult),
                     reads=[b_gh, b_const], writes=[b_glub[gi]])
            drain_taps(3)
        drain_taps(99)
        while fin_q:
            fin_q.pop(0)()
        P.dma("sp", lambda e, cc=cc: e.dma_start(out=SG_d[cc], in_=sgs), reads=[b_sgs], writes=[b_SG[cc]])
        dacc = dacc2[gi]
        b_dacc = b_dacc2[gi]
        tap_q.append(lambda cc=cc, gi=gi, dacc=dacc, b_dacc=b_dacc: P.op("dve", lambda e: e.tensor_scalar(
            out=dacc, in0=glub[gi][:, NPE:NPE + NOWN], scalar1=dww[:, cc, NPE:NPE + 1], scalar2=None, op0=ALU.mult),
            reads=[b_glub[gi], b_const], writes=[b_dacc]))
        for tap in range(NPE + 1, 31):
            tap_q.append(lambda cc=cc, gi=gi, tap=tap, dacc=dacc, b_dacc=b_dacc: P.op(
                "dve", lambda e: e.scalar_tensor_tensor(
                    out=dacc, in0=glub[gi][:, tap:tap + NOWN], scalar=dww[:, cc, tap:tap + 1], in1=dacc,
                    op0=ALU.mult, op1=ALU.add), reads=[b_glub[gi], b_const, b_dacc], writes=[b_dacc]))
        yi = cc % 2
        for j4 in range(4):
            pk = 6 + j4 % 2

            def cv(e, j4=j4, pk=pk, gi=gi, dg=dg):
                ins = None
                for tap in range(NPE):
                    ins = e.matmul(bank(pk), lhsT=dg[:, tap, :], rhs=glub[gi][:, j4 * 512 + tap:j4 * 512 + tap + 512],
                                   start=(tap == 0), stop=(tap == NPE - 1))
                return ins
            P.op("pe", cv, reads=[b_dg, b_glub[gi]], writes=[pbuf[pk]])
            P.op("act", lambda e, j4=j4, pk=pk, yi=yi, cc=cc: e.activation(
                out=ych[yi][:, j4 * 512:(j4 + 1) * 512], in_=bank(pk), func=AF.Identity, bias=dwb[:, cc:cc + 1]),
                reads=[pbuf[pk], b_const], writes=[b_ych[yi]])

        def finish(cc=cc, yi=yi, dacc=dacc, b_dacc=b_dacc):
            for j4 in range(4):
                qi = j4 % 2
                sl = slice(j4 * 512, (j4 + 1) * 512)
                P.op("pool", lambda e, sl=sl: e.tensor_tensor(out=ych[yi][:, sl], in0=ych[yi][:, sl], in1=dacc[:, sl],
                                                              op=ALU.add),
                     reads=[b_ych[yi], b_dacc], writes=[b_ych[yi]])
                P.op("act", lambda e, sl=sl, qi=qi: e.activation(out=sq5[qi], in_=ych[yi][:, sl], func=AF.Square),
                     reads=[b_ych[yi]], writes=[b_sq5[qi]])
                P.op("pool", lambda e, sl=sl: e.tensor_tensor(out=acc_s[:, sl], in0=acc_s[:, sl], in1=ych[yi][:, sl],
                                                              op=ALU.add), reads=[b_ych[yi], b_acc], writes=[b_acc])
                P.op("pool", lambda e, sl=sl, qi=qi: e.tensor_tensor(out=acc_q[:, sl], in0=acc_q[:, sl], in1=sq5[qi],
                                                                     op=ALU.add), reads=[b_sq5[qi], b_acc],
                     writes=[b_acc])
            P.dma("sp", lambda e: e.dma_start(out=YC_d[cc], in_=ych[yi]), reads=[b_ych[yi]], writes=[b_YC[cc]])
        fin_q.append(finish)
    drain_taps(99)
    while fin_q:
        fin_q.pop(0)()
    P.barrier()
    if stop_after == "P7":
        return _finish(nc, P)

    AR.reset(m_acc)
    mean_bc = AR.alloc([NOWN], F32)
    rstd_bc = AR.alloc([NOWN], F32)
    b_stat = Buf("lnstat")
    tmpv = AR.alloc([512], F32)
    b_tmpv = Buf()
    for j4 in range(4):
        sl = slice(j4 * 512, (j4 + 1) * 512)
        P.op("pe", lambda e, sl=sl: e.matmul(bank(0), lhsT=onesf, rhs=acc_s[:, sl], start=True, stop=True),
             reads=[b_acc, b_const], writes=[pbuf[0]])
        P.op("pe", lambda e, sl=sl: e.matmul(bank(1), lhsT=onesf, rhs=acc_q[:, sl], start=True, stop=True),
             reads=[b_acc, b_const], writes=[pbuf[1]])
        P.op("dve", lambda e, sl=sl: e.tensor_scalar(out=mean_bc[:, sl], in0=bank(0), scalar1=1.0 / D, scalar2=None,
                                                     op0=ALU.mult), reads=[pbuf[0]], writes=[b_stat])
        P.op("dve", lambda e, sl=sl: e.tensor_tensor(out=tmpv, in0=mean_bc[:, sl], in1=mean_bc[:, sl], op=ALU.mult),
             reads=[b_stat], writes=[b_tmpv])
        P.op("dve", lambda e, sl=sl: e.scalar_tensor_tensor(out=tmpv, in0=bank(1), scalar=1.0 / D, in1=tmpv,
                                                            op0=ALU.mult, op1=ALU.subtract),
             reads=[pbuf[1], b_tmpv], writes=[b_tmpv])
        P.op("act", lambda e, sl=sl: e.activation(out=rstd_bc[:, sl], in_=tmpv, func=AF.Sqrt, bias=epsc),
             reads=[b_tmpv, b_const], writes=[b_stat])
        P.op("dve", lambda e, sl=sl: e.reciprocal(out=rstd_bc[:, sl], in_=rstd_bc[:, sl]), reads=[b_stat],
             writes=[b_stat])
    yl = [AR.alloc([NOWN], F32) for _ in range(2)]
    sl_ = [AR.alloc([NOWN], BF16) for _ in range(2)]
    b_yl = [Buf() for _ in range(2)]
    b_sl = [Buf() for _ in range(2)]
    for cc in range(16):
        i = cc % 2
        P.dma("sp", lambda e, cc=cc, i=i: e.dma_start(out=yl[i], in_=YC_d[cc]), reads=[b_YC[cc]], writes=[b_yl[i]])
        P.dma("sp", lambda e, cc=cc, i=i: e.dma_start(out=sl_[i], in_=SG_d[cc]), reads=[b_SG[cc]], writes=[b_sl[i]])
        P.op("dve", lambda e, i=i: e.tensor_tensor(out=yl[i], in0=yl[i], in1=mean_bc, op=ALU.subtract),
             reads=[b_yl[i], b_stat], writes=[b_yl[i]])
        P.op("dve", lambda e, i=i: e.tensor_tensor(out=yl[i], in0=yl[i], in1=rstd_bc, op=ALU.mult),
             reads=[b_yl[i], b_stat], writes=[b_yl[i]])
        P.op("act", lambda e, i=i, cc=cc: e.activation(out=yl[i], in_=yl[i], func=AF.Silu, bias=lnb[:, cc:cc + 1],
                                                       scale=lng[:, cc:cc + 1]),
             reads=[b_yl[i], b_const], writes=[b_yl[i]])
        P.op("dve", lambda e, i=i, cc=cc: e.tensor_tensor(out=hT[:, cc, 0:NOWN], in0=yl[i], in1=sl_[i], op=ALU.mult),
             reads=[b_yl[i], b_sl[i]], writes=[b_hTc[cc]])
    P.barrier()
    if stop_after == "P7b":
        return _finish(nc, P)

    AR.reset(m_h)
    b_out = [Buf("out%d" % t) for t in range(16)]
    outproj(1, W["o_w_out"], 16, X1_d, out_d, b_out)
    return _finish(nc, P)


def _finish(nc, P):
    P.barrier()
    P.emit()
    P.close()
    return nc


def _rope_tables(pos):
    axis_dim = 32
    inv = (10000.0 ** (-np.arange(0, axis_dim, 2, dtype=np.float32) / np.float32(axis_dim))).astype(np.float32)
    row = (pos // 64).astype(np.float32)
    col = (pos % 64).astype(np.float32)
    ar = row[:, None] * inv[None, :]
    ac = col[:, None] * inv[None, :]
    ang = np.concatenate([ar, ar, ac, ac], axis=-1).astype(np.float32)
    return np.cos(ang).astype(np.float32), np.sin(ang).astype(np.float32)


def _consts():
    ident = np.eye(128, dtype=np.float32)
    R = np.zeros((64, 64), np.float32)
    for i in range(64):
        blk = i // 16
        if blk % 2 == 0:
            R[i, i + 16] = -1.0
        else:
            R[i, i - 16] = 1.0
    R2 = np.zeros((128, 128), np.float32)
    R2[:64, :64] = R
    R2[64:, 64:] = R
    blk = np.zeros((128, 128), np.float32)
    blk[:64, :64] = 1.0
    blk[64:, 64:] = 1.0
    bf = ml_dtypes.bfloat16
    return {"ident_bf": ident.astype(bf), "ident_f": ident, "rotT": np.ascontiguousarray(R2.T).astype(bf),
            "blk64": blk.astype(bf)}


def _pp(v, nchunk):
    return np.ascontiguousarray(np.asarray(v, np.float32).reshape(nchunk, 128).T)


def make_in_maps(inputs, cores=range(8)):
    f = lambda k: np.asarray(inputs[k], np.float32)
    x, c, ctx, c_ctx = f("x"), f("c"), f("ctx"), f("c_ctx")
    shared = dict(_consts())
    for L in ("e", "o"):
        shared[L + "_w_in"] = np.ascontiguousarray(f(L + "_w_in")[0])
        shared[L + "_w_out"] = np.ascontiguousarray(f(L + "_w_out")[0])
        shared[L + "_ada_w"] = np.ascontiguousarray(f(L + "_ada_w")[0])
        shared[L + "_ada_bT"] = _pp(f(L + "_ada_b")[0], 48)
        shared[L + "_norm_gT"] = _pp(f(L + "_norm_g")[0], 16)
    shared["e_vng_bc"] = np.ascontiguousarray(np.broadcast_to(f("e_a_vnorm_g")[0][None, :], (128, 1024)))
    shared["e_wsT"] = np.ascontiguousarray(f("e_a_ws")[0].transpose(2, 0, 1))
    shared["e_bs_bc"] = np.ascontiguousarray(np.broadcast_to(f("e_a_bs")[0][None], (128, 8, 128)))
    shared["e_gq"] = np.ascontiguousarray(np.tile(f("e_b_qnorm_g")[0], 2)[:, None])
    shared["e_gk"] = np.ascontiguousarray(np.tile(f("e_b_knorm_g")[0], 2)[:, None])
    shared["e_lam"] = np.ascontiguousarray(f("e_b_lambda")[0].reshape(1, 256))
    shared["e_gon"] = np.ascontiguousarray(f("e_b_onorm_g")[0][:, None])
    shared["o_dw_wT"] = np.ascontiguousarray(f("o_dw_w")[0].T.reshape(16, 128, 31).transpose(1, 0, 2))
    shared["o_dw_bT"] = _pp(f("o_dw_b")[0], 16)
    shared["o_ln_gT"] = _pp(f("o_ln_g")[0], 16)
    shared["o_ln_bT"] = _pp(f("o_ln_b")[0], 16)
    maps = []
    for core in cores:
        b, s = core // 2, core % 2
        if s == 0:
            order = np.concatenate([np.arange(0, 2048), np.arange(2048, 2176), np.arange(2176, 4096)])
            hm = np.array([0.0, 1.0], np.float32)
        else:
            order = np.concatenate([np.arange(2048, 4096), np.arange(1920, 2048), np.arange(0, 1920)])
            hm = np.array([1.0, 0.0], np.float32)
        xcore = np.concatenate([x[b][order], ctx[b]], axis=0)
        cos, sin = _rope_tables(order)
        cosT = np.ones((128, NTOK), np.float32)
        sinT = np.zeros((128, NTOK), np.float32)
        cosT[:, :SEQ] = np.tile(cos.T, (2, 1))
        sinT[:, :SEQ] = np.tile(sin.T, (2, 1))
        cvec = np.stack([c[b], c_ctx], axis=0)
        cT = np.ascontiguousarray(cvec.reshape(2, 16, 128).transpose(2, 0, 1))
        m = dict(shared)
        m.update({"xc": np.ascontiguousarray(xcore), "cT": cT, "rope_cos": cosT, "rope_sin": sinT,
                  "halo_mask": np.ascontiguousarray(np.broadcast_to(hm[None, :], (128, 2)))})
        maps.append(m)
    return maps


_NC_CACHE = {}


def kernel(**inputs):
    if "nc" not in _NC_CACHE:
        _NC_CACHE["nc"] = build_program()
    nc = _NC_CACHE["nc"]
    maps = make_in_maps(inputs)
    res = run_bass_kernel_spmd(nc, maps, core_ids=list(range(8)))
    out = np.empty((4, SEQ, D), np.float32)
    for core in range(8):
        b, s = core // 2, core % 2
        out[b, s * 2048:(s + 1) * 2048] = np.asarray(res.results[core]["out"])
    return out
```

```python
import math
import numpy as np
import ml_dtypes
import concourse.bass as bass
import concourse.mybir as mybir
from concourse.bass_utils import run_bass_kernel_spmd

F32 = mybir.dt.float32
BF16 = mybir.dt.bfloat16
AF = mybir.ActivationFunctionType
ALU = mybir.AluOpType

D = 2048
SEQ = 4096
CTX = 256
NTOK = SEQ + CTX
NF = 2176
NOWN = 2048
EPS = 1e-6
LAMBDA_INIT0 = 0.8 - 0.6 * math.exp(-0.3 * 0)
FT = [(0, 512), (512, 512), (1024, 512), (1536, 512), (2048, 128)]


class Buf:
    __slots__ = ("name", "w", "r")

    def __init__(self, name=""):
        self.name = name
        self.w = None
        self.r = []


class Prog:
    ENGS = ("pe", "act", "dve", "pool", "sp")

    def __init__(self, nc, n_dma_slots=8):
        self.nc = nc
        self.q = {e: [] for e in self.ENGS}
        self.sems = {}
        self.cnt = {}
        self.waited = {e: {} for e in self.ENGS}
        self._ctx = []
        for e in self.ENGS:
            self._mksem("c_" + e)
        self.n_dma_slots = n_dma_slots
        self.dma_i = {e: 0 for e in self.ENGS}
        for e in ("sp", "pool", "act"):
            for j in range(n_dma_slots):
                self._mksem("d_%s_%d" % (e, j))

    def _mksem(self, key):
        cm = self.nc.semaphore(key)
        h = cm.__enter__()
        self._ctx.append(cm)
        self.sems[key] = h
        self.cnt[key] = 0

    def _waits(self, eng, reads, writes, extra=()):
        need = {}

        def add(ev):
            if ev is None:
                return
            k, v = ev
            if need.get(k, 0) < v:
                need[k] = v
        for b in reads:
            add(b.w)
        for b in writes:
            add(b.w)
            for ev in b.r:
                add(ev)
        for ev in extra:
            add(ev)
        out = []
        wd = self.waited[eng]
        for k, v in need.items():
            if wd.get(k, 0) < v:
                wd[k] = v
                out.append((k, v))
        return out

    def _record(self, ev, reads, writes):
        for b in reads:
            b.r.append(ev)
        for b in writes:
            b.w = ev
            b.r = []

    def op(self, eng, fn, reads=(), writes=()):
        waits = self._waits(eng, reads, writes)
        key = "c_" + eng
        self.cnt[key] += 1
        ev = (key, self.cnt[key])
        self._record(ev, reads, writes)
        self.q[eng].append((waits, fn, key, 1))
        return ev

    def dma(self, eng, fn, reads=(), writes=()):
        j = self.dma_i[eng] % self.n_dma_slots
        self.dma_i[eng] += 1
        key = "d_%s_%d" % (eng, j)
        prev = (key, self.cnt[key]) if self.cnt[key] > 0 else None
        waits = self._waits(eng, reads, writes, extra=(prev,) if prev else ())
        self.cnt[key] += 16
        ev = (key, self.cnt[key])
        self._record(ev, reads, writes)
        self.q[eng].append((waits, fn, key, 16))
        return ev

    def barrier(self):
        for eng in self.ENGS:
            waits = []
            wd = self.waited[eng]
            for k, v in self.cnt.items():
                if v > 0 and wd.get(k, 0) < v:
                    wd[k] = v
                    waits.append((k, v))
            if waits:
                self.q[eng].append((waits, None, None, 0))

    def emit(self):
        nc = self.nc
        q = self.q
        sems = self.sems

        def run(e, items):
            for waits, fn, key, inc in items:
                for k, v in waits:
                    e.wait_ge(sems[k], v)
                if fn is not None:
                    ins = fn(e)
                    ins.then_inc(sems[key], inc)

        with nc.Block() as block:
            @block.tensor
            def _(e):
                run(e, q["pe"])

            @block.scalar
            def _(e):
                run(e, q["act"])

            @block.vector
            def _(e):
                run(e, q["dve"])

            @block.gpsimd
            def _(e):
                run(e, q["pool"])

            @block.sync
            def _(e):
                run(e, q["sp"])

    def close(self):
        for cm in reversed(self._ctx):
            cm.__exit__(None, None, None)


class Arena:
    def __init__(self, ap, nwords):
        self.ap = ap
        self.n = nwords
        self.off = 0

    def mark(self):
        return self.off

    def reset(self, m):
        self.off = m

    def alloc(self, free_shape, dtype):
        n = 1
        for s in free_shape:
            n *= s
        words = n if dtype == F32 else (n + 1) // 2
        words = (words + 7) // 8 * 8
        assert self.off + words <= self.n, ("arena overflow", self.off, words, self.n)
        v = self.ap[:, self.off:self.off + words]
        self.off += words
        if dtype != F32:
            v = v.bitcast(dtype)
        v = v[:, 0:n]
        if len(free_shape) == 2:
            v = v.rearrange("p (a b) -> p a b", a=free_shape[0])
        elif len(free_shape) == 3:
            v = v.rearrange("p (a b c) -> p a b c", a=free_shape[0], b=free_shape[1])
        return v


def build_program(debug=False, stop_after=None):
    nc = bass.Bass("TRN2", target_bir_lowering=False)
    P = Prog(nc)

    def din(name, shape, dt=F32):
        return nc.dram_tensor(name, list(shape), dt, kind="ExternalInput").ap()

    def dscr(name, shape, dt):
        return nc.dram_tensor(name, list(shape), dt,
                              kind="ExternalOutput" if debug else "Internal").ap()

    xc = din("xc", [NTOK, D])
    cT_d = din("cT", [128, 2, 16])
    cos_d = din("rope_cos", [128, NTOK])
    sin_d = din("rope_sin", [128, NTOK])
    identb_d = din("ident_bf", [128, 128], BF16)
    identf_d = din("ident_f", [128, 128])
    rotT_d = din("rotT", [128, 128], BF16)
    blk64_d = din("blk64", [128, 128], BF16)
    halo_d = din("halo_mask", [128, 2])
    W = {}
    for L, nin in (("e", 7168), ("o", 6144)):
        W[L + "_w_in"] = din(L + "_w_in", [D, nin])
        W[L + "_w_out"] = din(L + "_w_out", [D, D])
        W[L + "_ada_w"] = din(L + "_ada_w", [D, 6144])
        W[L + "_ada_bT"] = din(L + "_ada_bT", [128, 48])
        W[L + "_norm_gT"] = din(L + "_norm_gT", [128, 16])
    vng_d = din("e_vng_bc", [128, 1024])
    wsT_d = din("e_wsT", [128, 8, 128])
    bs_d = din("e_bs_bc", [128, 8, 128])
    gq_d = din("e_gq", [128, 1])
    gk_d = din("e_gk", [128, 1])
    lam_d = din("e_lam", [1, 256])
    gon_d = din("e_gon", [128, 1])
    dww_d = din("o_dw_wT", [128, 16, 31])
    dwb_d = din("o_dw_bT", [128, 16])
    lng_d = din("o_ln_gT", [128, 16])
    lnb_d = din("o_ln_bT", [128, 16])
    out_d = nc.dram_tensor("out", [NOWN, D], F32, kind="ExternalOutput").ap()

    K_d = dscr("K_s", [8, 128, NTOK], BF16)
    V_d = dscr("V_s", [NTOK, 1024], BF16)
    Q_d = dscr("Q_s", [8, 128, NF], BF16)
    G_d = dscr("G_s", [8, 128, NF], BF16)
    AU_d = dscr("AU_s", [8, 128, NF], BF16)
    AG_d = dscr("AG_s", [8, 128, NF], BF16)
    Y_d = dscr("Y_s", [16, 128, NF], BF16)
    X1_d = dscr("X1_s", [NF, D], F32)
    YC_d = dscr("YC_s", [16, 128, NOWN], F32)
    SG_d = dscr("SG_s", [16, 128, NOWN], BF16)
    if debug:
        HT_d = dscr("HT_s", [16, 128, NF], BF16)
        MOD_d = dscr("MOD_s", [128, 2 * 2 * 48 + 2 * 32], F32)
        GT_d = dscr("GT_s", [2, 128, D], F32)

    ARENA_WORDS = 53200
    arena_t = nc.alloc_sbuf_tensor("arena", [128, ARENA_WORDS], F32)
    AR = Arena(arena_t[:, :], ARENA_WORDS)
    pp = [nc.alloc_psum_tensor("pp%d" % i, [128, 2, 512], F32) for i in range(4)]

    def bank(k):
        return pp[k // 2][:, k % 2, :]
    pbuf = [Buf("bank%d" % k) for k in range(8)]

    identb = AR.alloc([128], BF16)
    identf = AR.alloc([128], F32)
    rotT = AR.alloc([128], BF16)
    blk64 = AR.alloc([128], BF16)
    onesb = AR.alloc([128], BF16)
    onesf = AR.alloc([128], F32)
    epsc = AR.alloc([1], F32)
    halo = AR.alloc([2], F32)
    mT = [AR.alloc([2, 48], F32) for _ in range(2)]
    Gp = [AR.alloc([2, 16], F32) for _ in range(2)]
    gt_bc = [AR.alloc([D], F32) for _ in range(2)]
    adabT = [AR.alloc([48], F32) for _ in range(2)]
    ngT = [AR.alloc([16], F32) for _ in range(2)]
    vng = AR.alloc([1024], F32)
    wsTb = AR.alloc([8, 128], BF16)
    bsb = AR.alloc([8, 128], F32)
    bsh = AR.alloc([8, 128], BF16)
    bsl = AR.alloc([8, 128], BF16)
    gq = AR.alloc([1], F32)
    gk = AR.alloc([1], F32)
    gon = AR.alloc([1], F32)
    nlam = AR.alloc([1], F32)
    lamrow = AR.alloc([256], F32)
    lamtmp = AR.alloc([8], F32)
    dww = AR.alloc([16, 31], F32)
    dwb = AR.alloc([16], F32)
    lng = AR.alloc([16], F32)
    lnb = AR.alloc([16], F32)
    cTs = AR.alloc([2, 16], F32)
    scT = AR.alloc([2, 16], F32)
    scTb = AR.alloc([2, 16], BF16)
    b_const = Buf("const")
    b_mod = Buf("mod")
    PERSIST = AR.mark()

    LNAME = ("e", "o")

    def ld(dst, src):
        P.dma("sp", lambda e: e.dma_start(out=dst, in_=src), writes=[b_const])
    ld(identb, identb_d)
    ld(identf, identf_d)
    ld(rotT, rotT_d)
    ld(blk64, blk64_d)
    ld(halo, halo_d)
    for l in range(2):
        ld(adabT[l], W[LNAME[l] + "_ada_bT"])
        ld(ngT[l], W[LNAME[l] + "_norm_gT"])
    ld(vng, vng_d)
    P.dma("pool", lambda e: e.dma_start(out=wsTb, in_=wsT_d), writes=[b_const])
    ld(bsb, bs_d)
    ld(gq, gq_d)
    ld(gk, gk_d)
    ld(gon, gon_d)
    ld(dww, dww_d)
    ld(dwb, dwb_d)
    ld(lng, lng_d)
    ld(lnb, lnb_d)
    ld(cTs, cT_d)
    P.dma("sp", lambda e: e.dma_start(out=lamrow[0:1, :], in_=lam_d), writes=[b_const])
    P.op("dve", lambda e: e.memset(onesb, 1.0), writes=[b_const])
    P.op("dve", lambda e: e.memset(onesf, 1.0), writes=[b_const])
    P.op("dve", lambda e: e.memset(epsc, EPS), writes=[b_const])
    P.op("dve", lambda e: e.tensor_copy(out=bsh[0:1], in_=bsb[0:1]), reads=[b_const], writes=[b_const])
    P.op("dve", lambda e: e.tensor_tensor(out=bsl[0:1], in0=bsb[0:1], in1=bsh[0:1], op=ALU.subtract),
         reads=[b_const], writes=[b_const])
    P.op("dve", lambda e: e.tensor_scalar(out=gon, in0=gon, scalar1=1.0 - LAMBDA_INIT0, scalar2=None,
                                          op0=ALU.mult), reads=[b_const], writes=[b_const])
    P.op("dve", lambda e: e.tensor_tensor(out=lamrow[0:1, 0:64], in0=lamrow[0:1, 0:64], in1=lamrow[0:1, 64:128],
                                          op=ALU.mult), reads=[b_const], writes=[b_const])
    P.op("dve", lambda e: e.tensor_tensor(out=lamrow[0:1, 128:192], in0=lamrow[0:1, 128:192],
                                          in1=lamrow[0:1, 192:256], op=ALU.mult), reads=[b_const], writes=[b_const])
    P.op("dve", lambda e: e.reduce_sum(out=lamtmp[0:1, 0:1], in_=lamrow[0:1, 0:64], axis=mybir.AxisListType.X),
         reads=[b_const], writes=[b_const])
    P.op("dve", lambda e: e.reduce_sum(out=lamtmp[0:1, 1:2], in_=lamrow[0:1, 128:192], axis=mybir.AxisListType.X),
         reads=[b_const], writes=[b_const])
    P.op("act", lambda e: e.activation(out=lamtmp[0:1, 2:4], in_=lamtmp[0:1, 0:2], func=AF.Exp),
         reads=[b_const], writes=[b_const])
    P.op("dve", lambda e: e.tensor_tensor(out=lamtmp[0:1, 4:5], in0=lamtmp[0:1, 3:4], in1=lamtmp[0:1, 2:3],
                                          op=ALU.subtract), reads=[b_const], writes=[b_const])
    P.op("dve", lambda e: e.tensor_scalar(out=lamtmp[0:1, 4:5], in0=lamtmp[0:1, 4:5], scalar1=-LAMBDA_INIT0,
                                          scalar2=None, op0=ALU.add), reads=[b_const], writes=[b_const])
    P.op("pe", lambda e: e.matmul(bank(0)[:, 0:1], lhsT=onesf[0:1, :], rhs=lamtmp[0:1, 4:5], start=True, stop=True),
         reads=[b_const], writes=[pbuf[0]])
    P.op("dve", lambda e: e.tensor_copy(out=nlam, in_=bank(0)[:, 0:1]), reads=[pbuf[0]], writes=[b_const])
    P.op("act", lambda e: e.activation(out=scT, in_=cTs, func=AF.Silu), reads=[b_const], writes=[b_const])
    P.op("dve", lambda e: e.tensor_copy(out=scTb, in_=scT), reads=[b_const], writes=[b_const])

    m0 = AR.mark()
    NWB = 2
    wblk = [AR.alloc([16, 512], BF16) for _ in range(NWB)]
    b_wblk = [Buf("wblk%d" % i) for i in range(NWB)]
    wctr = [0]

    def wload(src_w, col0, ncols=512):
        i = wctr[0] % NWB
        wctr[0] += 1
        dst = wblk[i][:, :, 0:ncols]
        src = src_w[:, col0:col0 + ncols].rearrange("(c p) n -> p c n", p=128)
        P.dma("pool", lambda e: e.dma_start(out=dst, in_=src), writes=[b_wblk[i]])
        return wblk[i], b_wblk[i]

    dgt = AR.alloc([128], F32)
    b_dgt = Buf("dgt")

    def ada_block(l, blk, wt, wb, pk):
        def mm(e):
            ins = None
            for j in range(4):
                for c in range(16):
                    ins = e.matmul(bank(pk)[:, 2 * j:2 * j + 2], lhsT=wt[:, c, j * 128:(j + 1) * 128],
                                   rhs=scTb[:, :, c], start=(c == 0), stop=(c == 15))
            return ins
        P.op("pe", mm, reads=[wb, b_const], writes=[pbuf[pk]])
        for v in range(2):
            P.op("dve", lambda e, v=v: e.tensor_tensor(
                out=mT[l][:, v, blk * 4:blk * 4 + 4],
                in0=bank(pk)[:, 0:8].rearrange("p (j v) -> p j v", v=2)[:, :, v],
                in1=adabT[l][:, blk * 4:blk * 4 + 4], op=ALU.add),
                reads=[pbuf[pk], b_const], writes=[b_mod])

    def ada_gp(l):
        for v in range(2):
            P.op("dve", lambda e, v=v: e.scalar_tensor_tensor(
                out=Gp[l][:, v, :], in0=mT[l][:, v, 16:32], scalar=1.0, in1=ngT[l], op0=ALU.add, op1=ALU.mult),
                reads=[b_mod, b_const], writes=[b_mod])

    def ada_gate(l, pk):
        for c in range(16):
            P.op("dve", lambda e, c=c: e.tensor_scalar(out=dgt, in0=identf, scalar1=mT[l][:, 0, 32 + c:33 + c],
                                                       scalar2=None, op0=ALU.mult),
                 reads=[b_mod, b_const], writes=[b_dgt])
            P.op("pe", lambda e, c=c: e.matmul(bank(pk)[:, (c % 4) * 128:(c % 4 + 1) * 128], lhsT=onesf, rhs=dgt,
                                               start=True, stop=True),
                 reads=[b_dgt, b_const], writes=[pbuf[pk]])
            if c % 4 == 3:
                P.op("act", lambda e, c=c: e.activation(
                    out=gt_bc[l][:, (c - 3) * 128:(c + 1) * 128], in_=bank(pk), func=AF.Copy),
                    reads=[pbuf[pk]], writes=[b_mod])

    wada0 = W["e_ada_w"]
    nxt = wload(wada0, 0)
    for blk in range(8):
        wt, wb = nxt
        if blk + 1 < 8:
            nxt = wload(wada0, (blk + 1) * 512)
        ada_block(0, blk, wt, wb, blk % 2)
    ada_gp(0)
    lazy = {"items": [("blk", 0, b) for b in range(8, 12)] + [("gate", 0)] +
                     [("blk", 1, b) for b in range(12)] + [("gp", 1), ("gate", 1)],
            "i": 0, "nxt": None}

    def lazy_step(pk):
        it = lazy["items"]
        i = lazy["i"]
        if i >= len(it):
            return False
        lazy["i"] = i + 1
        item = it[i]
        if item[0] == "blk":
            if lazy["nxt"] is None:
                lazy["nxt"] = wload(W[LNAME[item[1]] + "_ada_w"], item[2] * 512)
            wt, wb = lazy["nxt"]
            lazy["nxt"] = None
            for j in range(i + 1, len(it)):
                if it[j][0] == "blk":
                    lazy["nxt"] = wload(W[LNAME[it[j][1]] + "_ada_w"], it[j][2] * 512)
                    break
            ada_block(item[1], item[2], wt, wb, pk)
        elif item[0] == "gp":
            ada_gp(item[1])
        else:
            ada_gate(item[1], pk)
        return True
    P.barrier()
    if stop_after == "P0":
        return _finish(nc, P)

    m_pre_h = AR.mark()
    hT = AR.alloc([16, NF], BF16)
    b_hTc = [Buf("hT%d" % c) for c in range(16)]
    m_h = AR.mark()

    def xprep(l, tile_src, ntiles, vec_of_tile, post_tile=None):
        xn4 = [AR.alloc([4, D], BF16) for _ in range(2)]
        b_xn4 = [Buf("xn4_%d" % i) for i in range(2)]
        junk = AR.alloc([D], BF16)
        b_junk = Buf("junk")
        st = AR.alloc([64], F32)
        b_stk = [Buf("st%d" % k) for k in range(32)]
        ngroups = (ntiles + 3) // 4
        for gi in range(ngroups):
            tl = list(range(gi * 4, min(ntiles, gi * 4 + 4)))
            xb = xn4[gi % 2]
            bx = b_xn4[gi % 2]
            for t in tl:
                xt_ap, xt_b = tile_src(t)
                k = t % 32
                P.op("dve", lambda e, xt_ap=xt_ap, k=k: e.scalar_tensor_tensor(
                    out=junk, in0=xt_ap, scalar=1.0, in1=xt_ap, op0=ALU.mult, op1=ALU.mult,
                    accum_out=st[:, k:k + 1]), reads=[xt_b], writes=[b_junk, b_stk[k]])
                P.op("act", lambda e, k=k: e.activation(out=st[:, 32 + k:33 + k], in_=st[:, k:k + 1], func=AF.Ln,
                                                        bias=epsc, scale=1.0 / D), reads=[b_stk[k], b_const], writes=[b_stk[k]])
                P.op("act", lambda e, k=k: e.activation(out=st[:, 32 + k:33 + k], in_=st[:, 32 + k:33 + k],
                                                        func=AF.Exp, scale=-0.5), reads=[b_stk[k]], writes=[b_stk[k]])
                P.op("act", lambda e, xt_ap=xt_ap, k=k, xb=xb, t=t: e.activation(
                    out=xb[:, t % 4, :], in_=xt_ap, func=AF.Copy, scale=st[:, 32 + k:33 + k]),
                    reads=[xt_b, b_stk[k]], writes=[bx])
                if post_tile is not None:
                    post_tile(t, xt_ap, xt_b)
            nt = len(tl)
            runs = []
            for tt, t in enumerate(tl):
                v = vec_of_tile(t)
                if runs and runs[-1][0] == v:
                    runs[-1][2] += 1
                else:
                    runs.append([v, tt, 1])
            for c in range(16):
                pk = c % 4

                def tr(e, c=c, pk=pk, xb=xb, nt=nt):
                    ins = None
                    pv = bank(pk)[:, 0:256].bitcast(BF16).rearrange("p (a b) -> p a b", a=4)
                    for tt in range(nt):
                        ins = e.transpose(out=pv[:, tt, :], in_=xb[:, tt, c * 128:(c + 1) * 128], identity=identb)
                    return ins
                P.op("pe", tr, reads=[bx, b_const], writes=[pbuf[pk]])
                for (v, tt0, ntt) in runs:
                    if c % 2 == 0:
                        P.op("dve", lambda e, c=c, pk=pk, tt0=tt0, ntt=ntt, t0=tl[0], v=v, l=l: e.tensor_scalar(
                            out=hT[:, c, (t0 + tt0) * 128:(t0 + tt0 + ntt) * 128],
                            in0=bank(pk)[:, 0:256].bitcast(BF16)[:, tt0 * 128:(tt0 + ntt) * 128],
                            scalar1=Gp[l][:, v, c:c + 1], scalar2=mT[l][:, v, c:c + 1], op0=ALU.mult, op1=ALU.add),
                            reads=[pbuf[pk], b_mod], writes=[b_hTc[c]])
                    else:
                        P.op("act", lambda e, c=c, pk=pk, tt0=tt0, ntt=ntt, t0=tl[0], v=v, l=l: e.activation(
                            out=hT[:, c, (t0 + tt0) * 128:(t0 + tt0 + ntt) * 128],
                            in_=bank(pk)[:, 0:256].bitcast(BF16)[:, tt0 * 128:(tt0 + ntt) * 128],
                            func=AF.Identity, scale=Gp[l][:, v, c:c + 1], bias=mT[l][:, v, c:c + 1]),
                            reads=[pbuf[pk], b_mod], writes=[b_hTc[c]])

    def dram_tile_src(src_d, base_tile, src_bufs=None):
        xt = [AR.alloc([D], F32) for _ in range(3)]
        b_xt = [Buf("xt%d" % i) for i in range(3)]

        def src(t):
            i = t % 3
            g = base_tile + t
            rd = [src_bufs[t]] if src_bufs is not None else []
            P.dma("sp", lambda e, i=i, g=g: e.dma_start(out=xt[i], in_=src_d[g * 128:(g + 1) * 128, :]),
                  reads=rd, writes=[b_xt[i]])
            return xt[i], b_xt[i]
        return src

    def proj_phase(which):
        tok_off = NF if which == "B" else 0
        if which == "AV":
            al = lambda shape, dt: None
        else:
            al = AR.alloc
        kg = [al([512], BF16) for _ in range(2)]
        ksq = [al([512], BF16) for _ in range(2)]
        b_kg = [Buf() for _ in range(2)]
        b_ksq = [Buf() for _ in range(2)]
        sq = al([512], F32)
        t1 = al([512], F32)
        t2 = al([512], F32)
        b_sq, b_t1, b_t2 = Buf(), Buf(), Buf()
        cs = [al([2, 512], F32) for _ in range(2)]
        b_cs = [Buf() for _ in range(2)]
        stg = [al([NF], BF16) for _ in range(2)]
        b_stg = [Buf() for _ in range(2)]
        vst = [al([1024], BF16) for _ in range(2)]
        b_vst = [Buf() for _ in range(2)]
        sctr = [0]
        pctr = [0]

        def next_bank(lo=0, n=4):
            k = lo + pctr[0] % n
            pctr[0] += 1
            return k

        def fm_proj(wsrc, col0, nheads, epi, dst_d):
            nblk = (nheads + 3) // 4
            pending = [None]

            def flush():
                if pending[0] is not None:
                    pending[0]()
                    pending[0] = None
            nxt = wload(wsrc, col0)
            for bi in range(nblk):
                wt, wb = nxt
                if bi + 1 < nblk:
                    nxt = wload(wsrc, col0 + (bi + 1) * 512)
                for hh in range(4):
                    hd = bi * 4 + hh
                    si = sctr[0] % 2
                    sctr[0] += 1
                    for ti, (t0, n) in enumerate(FT):
                        pk = next_bank(0, 4)

                        def mm(e, wt=wt, hh=hh, t0=t0, n=n, pk=pk):
                            ins = None
                            for c in range(16):
                                ins = e.matmul(bank(pk)[:, 0:n], lhsT=wt[:, c, hh * 128:(hh + 1) * 128],
                                               rhs=hT[:, c, t0:t0 + n], start=(c == 0), stop=(c == 15))
                            return ins
                        P.op("pe", mm, reads=[wb] + b_hTc, writes=[pbuf[pk]])
                        flush()

                        def ep(pk=pk, t0=t0, n=n, si=si, hd=hd, last=(ti == len(FT) - 1)):
                            epi(pk, t0, n, stg[si], b_stg[si])
                            if last:
                                P.dma("sp", lambda e: e.dma_start(
                                    out=dst_d[hd, :, tok_off:tok_off + NF] if dst_d is K_d else dst_d[hd],
                                    in_=stg[si]), reads=[b_stg[si]])
                        pending[0] = ep
            flush()

        def epi_act(func):
            def epi(pk, t0, n, sg_ap, sg_b):
                P.op("act", lambda e: e.activation(out=sg_ap[:, t0:t0 + n], in_=bank(pk)[:, 0:n], func=func),
                     reads=[pbuf[pk]], writes=[sg_b])
            return epi

        qk_ctr = [0]

        def epi_qk(gcol):
            def epi(pk, t0, n, sg_ap, sg_b):
                i = qk_ctr[0] % 2
                qk_ctr[0] += 1
                P.dma("sp", lambda e: e.dma_start(out=cs[i][:, 0, 0:n], in_=cos_d[:, tok_off + t0:tok_off + t0 + n]),
                      writes=[b_cs[i]])
                P.dma("sp", lambda e: e.dma_start(out=cs[i][:, 1, 0:n], in_=sin_d[:, tok_off + t0:tok_off + t0 + n]),
                      writes=[b_cs[i]])
                P.op("act", lambda e: e.activation(out=kg[i][:, 0:n], in_=bank(pk)[:, 0:n], func=AF.Copy, scale=gcol),
                     reads=[pbuf[pk], b_const], writes=[b_kg[i]])
                P.op("act", lambda e: e.activation(out=ksq[i][:, 0:n], in_=bank(pk)[:, 0:n], func=AF.Square),
                     reads=[pbuf[pk]], writes=[b_ksq[i]])
                p2 = next_bank(4, 4)
                p3 = next_bank(4, 4)
                P.op("pe", lambda e: e.matmul(bank(p2)[:, 0:n], lhsT=blk64, rhs=ksq[i][:, 0:n], start=True, stop=True),
                     reads=[b_ksq[i], b_const], writes=[pbuf[p2]])
                P.op("pe", lambda e: e.matmul(bank(p3)[:, 0:n], lhsT=rotT, rhs=kg[i][:, 0:n], start=True, stop=True),
                     reads=[b_kg[i], b_const], writes=[pbuf[p3]])
                P.op("act", lambda e: e.activation(out=sq[:, 0:n], in_=bank(p2)[:, 0:n], func=AF.Ln, bias=epsc,
                                                   scale=1.0 / 64), reads=[pbuf[p2], b_const], writes=[b_sq])
                P.op("act", lambda e: e.activation(out=sq[:, 0:n], in_=sq[:, 0:n], func=AF.Exp, scale=-0.5),
                     reads=[b_sq], writes=[b_sq])
                P.op("dve", lambda e: e.tensor_tensor(out=t1[:, 0:n], in0=kg[i][:, 0:n], in1=cs[i][:, 0, 0:n],
                                                      op=ALU.mult), reads=[b_kg[i], b_cs[i]], writes=[b_t1])
                P.op("dve", lambda e: e.tensor_tensor(out=t2[:, 0:n], in0=bank(p3)[:, 0:n], in1=cs[i][:, 1, 0:n],
                                                      op=ALU.mult), reads=[pbuf[p3], b_cs[i]], writes=[b_t2])
                P.op("dve", lambda e: e.tensor_tensor(out=t1[:, 0:n], in0=t1[:, 0:n], in1=t2[:, 0:n], op=ALU.add),
                     reads=[b_t1, b_t2], writes=[b_t1])
                P.op("dve", lambda e: e.tensor_tensor(out=sg_ap[:, t0:t0 + n], in0=t1[:, 0:n], in1=sq[:, 0:n],
                                                      op=ALU.mult), reads=[b_t1, b_sq], writes=[sg_b])
            return epi

        def tm_proj(wsrc, col0, epi_tm):
            wA = wload(wsrc, col0)
            wB = wload(wsrc, col0 + 512)
            for t in range(17):
                for bi, (wt, wb) in enumerate((wA, wB)):
                    pk = next_bank(0, 4)

                    def mm(e, wt=wt, t=t, pk=pk):
                        ins = None
                        for c in range(16):
                            ins = e.matmul(bank(pk), lhsT=hT[:, c, t * 128:(t + 1) * 128], rhs=wt[:, c, :],
                                           start=(c == 0), stop=(c == 15))
                        return ins
                    P.op("pe", mm, reads=[wb] + b_hTc, writes=[pbuf[pk]])
                    epi_tm(t, bi, pk)

        def epi_v(t, bi, pk):
            i = t % 2
            P.op("act", lambda e: e.activation(out=vst[i][:, bi * 512:(bi + 1) * 512], in_=bank(pk), func=AF.Copy),
                 reads=[pbuf[pk]], writes=[b_vst[i]])
            if bi == 1:
                g = tok_off + t * 128
                P.dma("sp", lambda e: e.dma_start(out=V_d[g:g + 128, :], in_=vst[i]), reads=[b_vst[i]])

        w_in0 = W["e_w_in"]
        if which != "AV":
            fm_proj(w_in0, 4096, 8, epi_qk(gk), K_d)
            tm_proj(w_in0, 5120, epi_v)
        if which == "A":
            fm_proj(w_in0, 3072, 8, epi_qk(gq), Q_d)
            fm_proj(w_in0, 6144, 8, epi_act(AF.Silu), G_d)
            fm_proj(w_in0, 0, 8, epi_act(AF.Gelu), AU_d)
            fm_proj(w_in0, 2048, 8, epi_act(AF.Silu), AG_d)
        if which == "AV":
            gv = [AR.alloc([1024], F32) for _ in range(2)]
            b_gv = [Buf() for _ in range(2)]
            vjunk = AR.alloc([1024], BF16)
            b_vjunk = Buf()
            ssv = AR.alloc([64], F32)
            b_ssvt = [Buf() for _ in range(17)]

            def epi_av(t, bi, pk):
                i = t % 2
                P.op("act", lambda e: e.activation(out=gv[i][:, bi * 512:(bi + 1) * 512], in_=bank(pk), func=AF.Gelu),
                     reads=[pbuf[pk]], writes=[b_gv[i]])
                if bi == 1:
                    P.op("act", lambda e: e.activation(out=vjunk, in_=gv[i], func=AF.Square,
                                                       accum_out=ssv[:, t:t + 1]),
                         reads=[b_gv[i]], writes=[b_vjunk, b_ssvt[t]])
                    P.op("act", lambda e: e.activation(out=ssv[:, 32 + t:33 + t], in_=ssv[:, t:t + 1], func=AF.Sqrt,
                                                       bias=epsc, scale=1.0 / 1024), reads=[b_ssvt[t], b_const],
                         writes=[b_ssvt[t]])
                    P.op("dve", lambda e: e.reciprocal(out=ssv[:, 32 + t:33 + t], in_=ssv[:, 32 + t:33 + t]),
                         reads=[b_ssvt[t]], writes=[b_ssvt[t]])
                    P.op("dve", lambda e: e.scalar_tensor_tensor(out=vn_all[:, t, :], in0=gv[i],
                                                                 scalar=ssv[:, 32 + t:33 + t], in1=vng,
                                                                 op0=ALU.mult, op1=ALU.mult),
                         reads=[b_gv[i], b_ssvt[t], b_const], writes=[b_vn])
            tm_proj(w_in0, 1024, epi_av)

    vn_all = None
    b_vn = Buf("vn")
    for which in ("B", "A"):
        AR.reset(m_h)
        base_tile = 17 if which == "B" else 0
        xprep(0, dram_tile_src(xc, base_tile), 17,
              (lambda t: 1 if (which == "B" and t >= 15) else 0))
        if debug and which == "A":
            P.dma("sp", lambda e: e.dma_start(out=HT_d.rearrange("c p t -> p c t"), in_=hT), reads=b_hTc)
        P.barrier()
        AR.reset(m_h)
        proj_phase(which)
        P.barrier()
        if which == "A":
            AR.reset(m_h)
            vn_all = AR.alloc([17, 1024], BF16)
            proj_phase("AV")
            P.barrier()
        if stop_after == "P2" + which:
            return _finish(nc, P)
    m_vn = m_h + 17 * 512
    AR.reset(m_vn)

    def mix_phase():
        au = [AR.alloc([NF], BF16) for _ in range(2)]
        ag = [AR.alloc([NF], BF16) for _ in range(2)]
        b_au = [Buf() for _ in range(2)]
        b_ag = [Buf() for _ in range(2)]
        ystg = [AR.alloc([NF], BF16) for _ in range(2)]
        b_ystg = [Buf() for _ in range(2)]
        pc = [0]

        def load(g):
            i = g % 2
            P.dma("sp", lambda e: e.dma_start(out=au[i], in_=AU_d[g]), writes=[b_au[i]])
            P.dma("sp", lambda e: e.dma_start(out=ag[i], in_=AG_d[g]), writes=[b_ag[i]])
        load(0)
        for g in range(8):
            i = g % 2
            if g + 1 < 8:
                load(g + 1)
            P.op("dve", lambda e, i=i: e.tensor_tensor(out=au[i], in0=au[i], in1=ag[i], op=ALU.mult),
                 reads=[b_au[i], b_ag[i]], writes=[b_au[i]])
            for nb in range(5):
                tl = list(range(nb * 4, min(17, nb * 4 + 4)))
                nt = len(tl)
                pk = pc[0] % 4
                pc[0] += 1

                def mm(e, g=g, tl=tl, pk=pk):
                    ins = None
                    for sl, n in enumerate(tl):
                        o = bank(pk)[:, sl * 128:(sl + 1) * 128]
                        e.matmul(o, lhsT=vn_all[:, n, g * 128:(g + 1) * 128], rhs=wsTb[:, g, :], start=True, stop=False)
                        e.matmul(o, lhsT=onesb[0:1, :], rhs=bsh[0:1, g, :], start=False, stop=False)
                        ins = e.matmul(o, lhsT=onesb[0:1, :], rhs=bsl[0:1, g, :], start=False, stop=True)
                    return ins
                P.op("pe", mm, reads=[b_vn, b_const], writes=[pbuf[pk]])
                P.op("dve", lambda e, i=i, nt=nt, t0=tl[0], pk=pk: e.tensor_tensor(
                    out=ystg[i][:, t0 * 128:(t0 + nt) * 128], in0=bank(pk)[:, 0:nt * 128],
                    in1=au[i][:, t0 * 128:(t0 + nt) * 128], op=ALU.mult),
                    reads=[pbuf[pk], b_au[i]], writes=[b_ystg[i]])
            P.dma("sp", lambda e, g=g, i=i: e.dma_start(out=Y_d[g], in_=ystg[i]), reads=[b_ystg[i]], writes=[b_Y])
    b_Y = Buf("Y_d")
    mix_phase()
    P.barrier()
    if stop_after == "P3":
        return _finish(nc, P)

    AR.reset(m_pre_h)

    def attn_phase():
        kT = [AR.alloc([NTOK], BF16) for _ in range(2)]
        vh = [AR.alloc([34, 128], BF16) for _ in range(2)]
        qT = [AR.alloc([NF], BF16) for _ in range(2)]
        sbg = [AR.alloc([NF], BF16) for _ in range(2)]
        b_in = [Buf() for _ in range(2)]
        pT = [AR.alloc([2, 512], BF16) for _ in range(3)]
        b_pT = [Buf() for _ in range(3)]
        rz = AR.alloc([2, 512], F32)
        o1 = AR.alloc([512], F32)
        o2 = AR.alloc([512], F32)
        rs = AR.alloc([512], F32)
        osq = AR.alloc([512], BF16)
        b_rz, b_o1, b_o2, b_rs, b_osq = Buf(), Buf(), Buf(), Buf(), Buf()
        ystg = [AR.alloc([NF], BF16) for _ in range(2)]
        b_ystg = [Buf() for _ in range(2)]
        b_s = [Buf(), Buf()]
        b_o = [pbuf[4], pbuf[5]]
        b_z = pbuf[6]
        zs = AR.alloc([512], F32)
        b_zs = Buf()
        zh = AR.alloc([512], BF16)
        zl = AR.alloc([512], BF16)
        b_zh, b_zl = Buf(), Buf()
        sctr = [0]
        pctr = [0]

        def load_head(hd):
            i = hd % 2
            P.dma("sp", lambda e: e.dma_start(out=kT[i], in_=K_d[hd]), writes=[b_in[i]])
            P.dma("sp", lambda e: e.dma_start(
                out=vh[i], in_=V_d[:, hd * 128:(hd + 1) * 128].rearrange("(t p) d -> p t d", p=128)),
                writes=[b_in[i]])
            P.dma("sp", lambda e: e.dma_start(out=qT[i], in_=Q_d[hd]), writes=[b_in[i]])
            P.dma("sp", lambda e: e.dma_start(out=sbg[i], in_=G_d[hd]), writes=[b_in[i]])

        pend = []

        def flush_one():
            if pend:
                pend.pop(0)()

        its = [(hd, qi, kt) for hd in range(8) for qi in range(len(FT)) for kt in range(34)]
        sis = {}

        def issue_s(j):
            hd, qi, kt = its[j]
            i = hd % 2
            t0, n = FT[qi]
            si = sctr[0] % 2
            sctr[0] += 1
            sis[j] = si

            def f(e):
                e.matmul(pp[si][:, 0, 0:n], lhsT=kT[i][0:64, kt * 128:(kt + 1) * 128],
                         rhs=qT[i][0:64, t0:t0 + n], start=True, stop=True)
                return e.matmul(pp[si][:, 1, 0:n], lhsT=kT[i][64:128, kt * 128:(kt + 1) * 128],
                                rhs=qT[i][64:128, t0:t0 + n], start=True, stop=True)
            P.op("pe", f, reads=[b_in[i]], writes=[b_s[si]])

        def epilogue(hd, qi):
            i = hd % 2
            t0, n = FT[qi]
            P.op("dve", lambda e: e.tensor_copy(out=o1[:, 0:n], in_=bank(4)[:, 0:n]), reads=[b_o[0]], writes=[b_o1])
            P.op("dve", lambda e: e.tensor_copy(out=o2[:, 0:n], in_=bank(5)[:, 0:n]), reads=[b_o[1]], writes=[b_o2])
            P.op("dve", lambda e: e.tensor_copy(out=zs[0:64, 0:n], in_=bank(6)[0:64, 0:n]), reads=[b_z], writes=[b_zs])
            lazy_step(7)

            def stage_0():
                P.op("dve", lambda e: e.reciprocal(out=zs[0:64, 0:n], in_=zs[0:64, 0:n]), reads=[b_zs], writes=[b_zs])
                P.op("dve", lambda e: e.tensor_copy(out=zh[0:64, 0:n], in_=zs[0:64, 0:n]), reads=[b_zs], writes=[b_zh])
                P.op("dve", lambda e: e.tensor_tensor(out=zl[0:64, 0:n], in0=zs[0:64, 0:n], in1=zh[0:64, 0:n],
                                                      op=ALU.subtract), reads=[b_zs, b_zh], writes=[b_zl])

            def zbc(row):
                def f(e):
                    e.matmul(bank(7)[:, 0:n], lhsT=onesb[row:row + 1, :], rhs=zh[row:row + 1, 0:n],
                             start=True, stop=False)
                    return e.matmul(bank(7)[:, 0:n], lhsT=onesb[row:row + 1, :], rhs=zl[row:row + 1, 0:n],
                                    start=False, stop=True)
                return f

            def stage_a():
                P.op("pe", zbc(0), reads=[b_zh, b_zl, b_const], writes=[pbuf[7]])
                P.op("dve", lambda e: e.tensor_tensor(out=o1[:, 0:n], in0=o1[:, 0:n], in1=bank(7)[:, 0:n],
                                                      op=ALU.mult), reads=[b_o1, pbuf[7]], writes=[b_o1])

            def stage_b():
                P.op("pe", zbc(32), reads=[b_zh, b_zl, b_const], writes=[pbuf[7]])
                P.op("dve", lambda e: e.tensor_tensor(out=o2[:, 0:n], in0=o2[:, 0:n], in1=bank(7)[:, 0:n],
                                                      op=ALU.mult), reads=[b_o2, pbuf[7]], writes=[b_o2])
                P.op("dve", lambda e: e.scalar_tensor_tensor(out=o1[:, 0:n], in0=o2[:, 0:n], scalar=nlam,
                                                             in1=o1[:, 0:n], op0=ALU.mult, op1=ALU.add),
                     reads=[b_o1, b_o2, b_const], writes=[b_o1])
                P.op("dve", lambda e: e.tensor_tensor(out=osq[:, 0:n], in0=o1[:, 0:n], in1=o1[:, 0:n], op=ALU.mult),
                     reads=[b_o1], writes=[b_osq])

            def stage_c1():
                P.op("pe", lambda e: e.matmul(bank(7)[:, 0:n], lhsT=onesb, rhs=osq[:, 0:n], start=True, stop=True),
                     reads=[b_osq, b_const], writes=[pbuf[7]])

            def stage_c():
                P.op("act", lambda e: e.activation(out=rs[:, 0:n], in_=bank(7)[:, 0:n], func=AF.Ln,
                                                   bias=epsc, scale=1.0 / 128),
                     reads=[pbuf[7], b_const], writes=[b_rs])
                P.op("act", lambda e: e.activation(out=rs[:, 0:n], in_=rs[:, 0:n], func=AF.Exp, scale=-0.5),
                     reads=[b_rs], writes=[b_rs])
                P.op("dve", lambda e: e.tensor_tensor(out=o1[:, 0:n], in0=o1[:, 0:n], in1=rs[:, 0:n],
                                                      op=ALU.mult), reads=[b_o1, b_rs], writes=[b_o1])
                P.op("dve", lambda e: e.scalar_tensor_tensor(
                    out=ystg[i][:, t0:t0 + n], in0=o1[:, 0:n], scalar=gon, in1=sbg[i][:, t0:t0 + n],
                    op0=ALU.mult, op1=ALU.mult), reads=[b_o1, b_in[i], b_const], writes=[b_ystg[i]])
                if qi == len(FT) - 1:
                    P.dma("sp", lambda e: e.dma_start(out=Y_d[8 + hd], in_=ystg[i]),
                          reads=[b_ystg[i]], writes=[b_Y])
            pend.extend([stage_0, stage_a, stage_b, stage_c1, stage_c])

        load_head(0)
        issue_s(0)
        issue_s(1)
        for j, (hd, qi, kt) in enumerate(its):
            i = hd % 2
            t0, n = FT[qi]
            si = sis[j]
            pi = pctr[0] % 3
            pctr[0] += 1
            P.op("act", lambda e, si=si, pi=pi, n=n: e.activation(
                out=pT[pi][:, :, 0:n], in_=pp[si][:, :, 0:n], func=AF.Exp, scale=0.125),
                reads=[b_s[si]], writes=[b_pT[pi]])
            if j + 2 < len(its):
                issue_s(j + 2)

            def pv(e, kt=kt, pi=pi, i=i, n=n):
                st, sp_ = (kt == 0), (kt == 33)
                e.matmul(bank(4)[:, 0:n], lhsT=vh[i][:, kt, :], rhs=pT[pi][:, 0, 0:n], start=st, stop=sp_)
                e.matmul(bank(5)[:, 0:n], lhsT=vh[i][:, kt, :], rhs=pT[pi][:, 1, 0:n], start=st, stop=sp_)
                e.matmul(bank(6)[0:32, 0:n], lhsT=onesb[:, 0:32], rhs=pT[pi][:, 0, 0:n], start=st, stop=sp_,
                         tile_position=(0, 0))
                return e.matmul(bank(6)[32:64, 0:n], lhsT=onesb[:, 0:32], rhs=pT[pi][:, 1, 0:n], start=st,
                                stop=sp_, tile_position=(0, 32))
            P.op("pe", pv, reads=[b_pT[pi], b_in[i], b_const], writes=[b_o[0], b_o[1], b_z])
            if kt in (3, 8, 13, 18, 21):
                flush_one()
            if kt == 23 and qi == 0 and hd + 1 < 8:
                load_head(hd + 1)
            if kt == 33:
                epilogue(hd, qi)
        while pend:
            flush_one()
    attn_phase()
    while lazy_step(7):
        pass
    if debug:
        for l in range(2):
            P.dma("sp", lambda e, l=l: e.dma_start(out=MOD_d[:, l * 96:(l + 1) * 96],
                                                   in_=mT[l].rearrange("p v c -> p (v c)")), reads=[b_mod])
            P.dma("sp", lambda e, l=l: e.dma_start(out=MOD_d[:, 192 + l * 32:192 + (l + 1) * 32],
                                                   in_=Gp[l].rearrange("p v c -> p (v c)")), reads=[b_mod])
            P.dma("sp", lambda e, l=l: e.dma_start(out=GT_d[l], in_=gt_bc[l]), reads=[b_mod])
    P.barrier()
    if stop_after == "P4":
        return _finish(nc, P)

    def outproj(l, w_out, ntiles, resid_d, dst_d, b_dst, pre=None):
        xr = [AR.alloc([512], F32) for _ in range(4)]
        b_xr = [Buf() for _ in range(4)]
        t1 = [AR.alloc([512], F32) for _ in range(3)]
        b_t1 = [Buf() for _ in range(3)]
        units = [(j, t) for j in range(4) for t in range(ntiles)]

        def load(u):
            j, t = units[u]
            xi = u % 4
            P.dma("sp", lambda e: e.dma_start(
                out=xr[xi], in_=resid_d[t * 128:(t + 1) * 128, j * 512:(j + 1) * 512]), writes=[b_xr[xi]])
        load(0)
        load(1)
        nxt = pre if pre is not None else wload(w_out, 0)
        wt = wb = None
        for u, (j, t) in enumerate(units):
            if t == 0:
                wt, wb = nxt
                if j + 1 < 4:
                    nxt = wload(w_out, (j + 1) * 512)
            if u + 2 < len(units):
                load(u + 2)
            pk = u % 4
            xi = u % 4
            ti = u % 3

            def mm(e, wt=wt, t=t, pk=pk):
                ins = None
                for c in range(16):
                    ins = e.matmul(bank(pk), lhsT=hT[:, c, t * 128:(t + 1) * 128], rhs=wt[:, c, :],
                                   start=(c == 0), stop=(c == 15))
                return ins
            P.op("pe", mm, reads=[wb] + b_hTc, writes=[pbuf[pk]])
            P.op("dve", lambda e, pk=pk, ti=ti, j=j: e.tensor_tensor(
                out=t1[ti], in0=bank(pk), in1=gt_bc[l][:, j * 512:(j + 1) * 512], op=ALU.mult),
                reads=[pbuf[pk], b_mod], writes=[b_t1[ti]])
            P.op("dve", lambda e, ti=ti, xi=xi: e.tensor_tensor(out=t1[ti], in0=t1[ti], in1=xr[xi], op=ALU.add),
                 reads=[b_t1[ti], b_xr[xi]], writes=[b_t1[ti]])
            P.dma("act", lambda e, ti=ti, t=t, j=j: e.dma_start(
                out=dst_d[t * 128:(t + 1) * 128, j * 512:(j + 1) * 512], in_=t1[ti]),
                reads=[b_t1[ti]], writes=[b_dst[t]])

    AR.reset(m_h)
    for c in range(16):
        P.dma("sp", lambda e, c=c: e.dma_start(out=hT[:, c, :], in_=Y_d[c]), reads=[b_Y], writes=[b_hTc[c]])
    b_X1 = [Buf("x1_%d" % t) for t in range(17)]
    outproj(0, W["e_w_out"], 17, xc, X1_d, b_X1)
    P.barrier()
    if stop_after == "P5":
        return _finish(nc, P)

    AR.reset(m_h)
    pre_w3 = []
    xprep(1, dram_tile_src(X1_d, 0, b_X1), 17, (lambda t: 0))
    if debug:
        P.dma("sp", lambda e: e.dma_start(out=HT_d.rearrange("c p t -> p c t"), in_=hT), reads=b_hTc)
    P.barrier()
    if stop_after == "P6":
        return _finish(nc, P)

    AR.reset(m_h)
    acc_s = AR.alloc([NOWN], F32)
    acc_q = AR.alloc([NOWN], F32)
    b_acc = Buf("acc")
    m_acc = AR.mark()
    GL = 15 + NOWN + 15
    w_in1 = W["o_w_in"]
    glub = [AR.alloc([GL + 2], BF16) for _ in range(2)]
    b_glub = [Buf() for _ in range(2)]
    _sig0 = vng[:, 512:1024]
    sig = [_sig0, _sig0]
    _bsig0 = Buf()
    b_sig = [_bsig0, _bsig0]
    gh = lamrow[:, 0:128]
    b_gh = Buf()
    sgs = AR.alloc([NOWN], BF16)
    b_sgs = Buf()
    NPE = 19
    dg2 = [AR.alloc([NPE, 128], BF16) for _ in range(2)]
    b_dg2 = [Buf() for _ in range(2)]
    dacc2 = [AR.alloc([NOWN], F32) for _ in range(2)]
    b_dacc2 = [Buf() for _ in range(2)]
    ych = [AR.alloc([NOWN], F32) for _ in range(2)]
    b_ych = [Buf() for _ in range(2)]
    _sq0 = vng[:, 0:512]
    sq5 = [_sq0, _sq0]
    _bsq0 = Buf()
    b_sq5 = [_bsq0, _bsq0]
    P.op("pool", lambda e: e.memset(acc_s, 0.0), writes=[b_acc])
    P.op("pool", lambda e: e.memset(acc_q, 0.0), writes=[b_acc])
    b_YC = [Buf() for _ in range(16)]
    b_SG = [Buf() for _ in range(16)]

    def w3load(cc):
        i = wctr[0] % NWB
        wctr[0] += 1
        for k3 in range(3):
            dst = wblk[i][:, :, k3 * 128:(k3 + 1) * 128]
            src = w_in1[:, k3 * 2048 + cc * 128:k3 * 2048 + (cc + 1) * 128].rearrange("(c p) n -> p c n", p=128)
            P.dma("pool", lambda e, dst=dst, src=src: e.dma_start(out=dst, in_=src), writes=[b_wblk[i]])
        return wblk[i], b_wblk[i]

    nxt = w3load(0)
    ctr = 0
    tap_q = []
    fin_q = []

    def drain_taps(k):
        for _ in range(min(k, len(tap_q))):
            tap_q.pop(0)()

    for cc in range(16):
        wt, wb = nxt
        if cc + 1 < 16:
            nxt = w3load(cc + 1)
        gi = cc % 2
        dg = dg2[gi]
        b_dg = b_dg2[gi]
        for tap in range(NPE):
            P.op("act", lambda e, tap=tap, cc=cc, dg=dg: e.activation(
                out=dg[:, tap, :], in_=identf, func=AF.Copy, scale=dww[:, cc, tap:tap + 1]),
                reads=[b_const], writes=[b_dg])
        for (t0, n) in FT:
            is_halo = (t0 == NOWN)
            pa, pb_, pg = (ctr * 3) % 6, (ctr * 3 + 1) % 6, (ctr * 3 + 2) % 6
            si = ctr % 2
            ctr += 1

            def mm3(e, wt=wt, t0=t0, n=n, pa=pa, pb_=pb_, pg=pg, is_halo=is_halo):
                ins = None
                for k3, pk in enumerate((pa, pb_, pg)):
                    if k3 == 2 and is_halo:
                        continue
                    for c in range(16):
                        ins = e.matmul(bank(pk)[:, 0:n], lhsT=wt[:, c, k3 * 128:(k3 + 1) * 128],
                                       rhs=hT[:, c, t0:t0 + n], start=(c == 0), stop=(c == 15))
                return ins
            P.op("pe", mm3, reads=[wb] + b_hTc, writes=[pbuf[pa], pbuf[pb_]] + ([] if is_halo else [pbuf[pg]]))
            P.op("act", lambda e, si=si, pb_=pb_, n=n: e.activation(out=sig[si][:, 0:n], in_=bank(pb_)[:, 0:n],
                                                                    func=AF.Sigmoid),
                 reads=[pbuf[pb_]], writes=[b_sig[si]])
            if not is_halo:
                P.op("dve", lambda e, si=si, pa=pa, n=n, t0=t0, gi=gi: e.tensor_tensor(
                    out=glub[gi][:, 15 + t0:15 + t0 + n], in0=bank(pa)[:, 0:n], in1=sig[si][:, 0:n], op=ALU.mult),
                    reads=[pbuf[pa], b_sig[si]], writes=[b_glub[gi]])
                P.op("act", lambda e, pg=pg, n=n, t0=t0: e.activation(out=sgs[:, t0:t0 + n], in_=bank(pg)[:, 0:n],
                                                                      func=AF.Silu),
                     reads=[pbuf[pg]], writes=[b_sgs])
            else:
                P.op("dve", lambda e, si=si, pa=pa: e.tensor_tensor(out=gh, in0=bank(pa)[:, 0:128],
                                                                    in1=sig[si][:, 0:128], op=ALU.mult),
                     reads=[pbuf[pa], b_sig[si]], writes=[b_gh])
                P.op("dve", lambda e, gi=gi: e.tensor_scalar(out=glub[gi][:, 0:15], in0=gh[:, 113:128],
                                                             scalar1=halo[:, 0:1], scalar2=None, op0=ALU.mult),
                     reads=[b_gh, b_const], writes=[b_glub[gi]])
                P.op("dve", lambda e, gi=gi: e.tensor_scalar(out=glub[gi][:, 15 + NOWN:30 + NOWN], in0=gh[:, 0:15],
                                                             scalar1=halo[:, 1:2], scalar2=None, op0=ALU.mult),
                     reads=[b_gh, b_const], writes=[b_glub[gi]])
            drain_taps(4 if t0 < 3 * 512 else 0)
        drain_taps(99)
        while fin_q:
            fin_q.pop(0)()
        P.dma("sp", lambda e, cc=cc: e.dma_start(out=SG_d[cc], in_=sgs), reads=[b_sgs], writes=[b_SG[cc]])
        dacc = dacc2[gi]
        b_dacc = b_dacc2[gi]
        tap_q.append(lambda cc=cc, gi=gi, dacc=dacc, b_dacc=b_dacc: P.op("dve", lambda e: e.tensor_scalar(
            out=dacc, in0=glub[gi][:, NPE:NPE + NOWN], scalar1=dww[:, cc, NPE:NPE + 1], scalar2=None, op0=ALU.mult),
            reads=[b_glub[gi], b_const], writes=[b_dacc]))
        for tap in range(NPE + 1, 31):
            tap_q.append(lambda cc=cc, gi=gi, tap=tap, dacc=dacc, b_dacc=b_dacc: P.op(
                "dve", lambda e: e.scalar_tensor_tensor(
                    out=dacc, in0=glub[gi][:, tap:tap + NOWN], scalar=dww[:, cc, tap:tap + 1], in1=dacc,
                    op0=ALU.mult, op1=ALU.add), reads=[b_glub[gi], b_const, b_dacc], writes=[b_dacc]))
        yi = cc % 2
        for j4 in range(4):
            pk = 6 + j4 % 2

            def cv(e, j4=j4, pk=pk, gi=gi, dg=dg):
                ins = None
                for tap in range(NPE):
                    ins = e.matmul(bank(pk), lhsT=dg[:, tap, :], rhs=glub[gi][:, j4 * 512 + tap:j4 * 512 + tap + 512],
                                   start=(tap == 0), stop=(tap == NPE - 1))
                return ins
            P.op("pe", cv, reads=[b_dg, b_glub[gi]], writes=[pbuf[pk]])
            P.op("act", lambda e, j4=j4, pk=pk, yi=yi, cc=cc: e.activation(
                out=ych[yi][:, j4 * 512:(j4 + 1) * 512], in_=bank(pk), func=AF.Identity, bias=dwb[:, cc:cc + 1]),
                reads=[pbuf[pk], b_const], writes=[b_ych[yi]])

        def finish(cc=cc, yi=yi, dacc=dacc, b_dacc=b_dacc):
            for j4 in range(4):
                qi = j4 % 2
                sl = slice(j4 * 512, (j4 + 1) * 512)
                P.op("pool", lambda e, sl=sl: e.tensor_tensor(out=ych[yi][:, sl], in0=ych[yi][:, sl], in1=dacc[:, sl],
                                                              op=ALU.add),
                     reads=[b_ych[yi], b_dacc], writes=[b_ych[yi]])
                P.op("pool", lambda e, sl=sl, qi=qi: e.tensor_tensor(out=sq5[qi], in0=ych[yi][:, sl],
                                                                     in1=ych[yi][:, sl], op=ALU.mult),
                     reads=[b_ych[yi]], writes=[b_sq5[qi]])
                P.op("pool", lambda e, sl=sl: e.tensor_tensor(out=acc_s[:, sl], in0=acc_s[:, sl], in1=ych[yi][:, sl],
                                                              op=ALU.add), reads=[b_ych[yi], b_acc], writes=[b_acc])
                P.op("pool", lambda e, sl=sl, qi=qi: e.tensor_tensor(out=acc_q[:, sl], in0=acc_q[:, sl], in1=sq5[qi],
                                                                     op=ALU.add), reads=[b_sq5[qi], b_acc],
                     writes=[b_acc])
            P.dma("sp", lambda e: e.dma_start(out=YC_d[cc], in_=ych[yi]), reads=[b_ych[yi]], writes=[b_YC[cc]])
        fin_q.append(finish)
    drain_taps(99)
    while fin_q:
        fin_q.pop(0)()
    P.barrier()
    if stop_after == "P7":
        return _finish(nc, P)

    AR.reset(m_acc)
    pre_wout1 = wload(W["o_w_out"], 0)
    mean_bc = AR.alloc([NOWN], F32)
    rstd_bc = AR.alloc([NOWN], F32)
    b_stat = Buf("lnstat")
    tmpv = AR.alloc([512], F32)
    b_tmpv = Buf()
    for j4 in range(4):
        sl = slice(j4 * 512, (j4 + 1) * 512)
        P.op("pe", lambda e, sl=sl: e.matmul(bank(0), lhsT=onesf, rhs=acc_s[:, sl], start=True, stop=True),
             reads=[b_acc, b_const], writes=[pbuf[0]])
        P.op("pe", lambda e, sl=sl: e.matmul(bank(1), lhsT=onesf, rhs=acc_q[:, sl], start=True, stop=True),
             reads=[b_acc, b_const], writes=[pbuf[1]])
        P.op("dve", lambda e, sl=sl: e.tensor_scalar(out=mean_bc[:, sl], in0=bank(0), scalar1=1.0 / D, scalar2=None,
                                                     op0=ALU.mult), reads=[pbuf[0]], writes=[b_stat])
        P.op("dve", lambda e, sl=sl: e.tensor_tensor(out=tmpv, in0=mean_bc[:, sl], in1=mean_bc[:, sl], op=ALU.mult),
             reads=[b_stat], writes=[b_tmpv])
        P.op("dve", lambda e, sl=sl: e.scalar_tensor_tensor(out=tmpv, in0=bank(1), scalar=1.0 / D, in1=tmpv,
                                                            op0=ALU.mult, op1=ALU.subtract),
             reads=[pbuf[1], b_tmpv], writes=[b_tmpv])
        P.op("act", lambda e, sl=sl: e.activation(out=rstd_bc[:, sl], in_=tmpv, func=AF.Sqrt, bias=epsc),
             reads=[b_tmpv, b_const], writes=[b_stat])
        P.op("dve", lambda e, sl=sl: e.reciprocal(out=rstd_bc[:, sl], in_=rstd_bc[:, sl]), reads=[b_stat],
             writes=[b_stat])
    yl = [AR.alloc([NOWN], F32) for _ in range(2)]
    sl_ = [AR.alloc([NOWN], BF16) for _ in range(2)]
    b_yl = [Buf() for _ in range(2)]
    b_sl = [Buf() for _ in range(2)]
    for cc in range(16):
        i = cc % 2
        P.dma("sp", lambda e, cc=cc, i=i: e.dma_start(out=yl[i], in_=YC_d[cc]), reads=[b_YC[cc]], writes=[b_yl[i]])
        P.dma("sp", lambda e, cc=cc, i=i: e.dma_start(out=sl_[i], in_=SG_d[cc]), reads=[b_SG[cc]], writes=[b_sl[i]])
        P.op("pool", lambda e, i=i: e.tensor_tensor(out=yl[i], in0=yl[i], in1=mean_bc, op=ALU.subtract),
             reads=[b_yl[i], b_stat], writes=[b_yl[i]])
        P.op("dve", lambda e, i=i: e.tensor_tensor(out=yl[i], in0=yl[i], in1=rstd_bc, op=ALU.mult),
             reads=[b_yl[i], b_stat], writes=[b_yl[i]])
        P.op("act", lambda e, i=i, cc=cc: e.activation(out=yl[i], in_=yl[i], func=AF.Silu, bias=lnb[:, cc:cc + 1],
                                                       scale=lng[:, cc:cc + 1]),
             reads=[b_yl[i], b_const], writes=[b_yl[i]])
        P.op("dve", lambda e, i=i, cc=cc: e.tensor_tensor(out=hT[:, cc, 0:NOWN], in0=yl[i], in1=sl_[i], op=ALU.mult),
             reads=[b_yl[i], b_sl[i]], writes=[b_hTc[cc]])
    P.barrier()
    if stop_after == "P7b":
        return _finish(nc, P)

    AR.reset(m_h)
    b_out = [Buf("out%d" % t) for t in range(16)]
    outproj(1, W["o_w_out"], 16, X1_d, out_d, b_out, pre=pre_wout1)
    return _finish(nc, P)


def _finish(nc, P):
    P.barrier()
    P.emit()
    P.close()
    return nc


def _rope_tables(pos):
    axis_dim = 32
    inv = (10000.0 ** (-np.arange(0, axis_dim, 2, dtype=np.float32) / np.float32(axis_dim))).astype(np.float32)
    row = (pos // 64).astype(np.float32)
    col = (pos % 64).astype(np.float32)
    ar = row[:, None] * inv[None, :]
    ac = col[:, None] * inv[None, :]
    ang = np.concatenate([ar, ar, ac, ac], axis=-1).astype(np.float32)
    return np.cos(ang).astype(np.float32), np.sin(ang).astype(np.float32)


def _consts():
    ident = np.eye(128, dtype=np.float32)
    R = np.zeros((64, 64), np.float32)
    for i in range(64):
        blk = i // 16
        if blk % 2 == 0:
            R[i, i + 16] = -1.0
        else:
            R[i, i - 16] = 1.0
    R2 = np.zeros((128, 128), np.float32)
    R2[:64, :64] = R
    R2[64:, 64:] = R
    blk = np.zeros((128, 128), np.float32)
    blk[:64, :64] = 1.0
    blk[64:, 64:] = 1.0
    bf = ml_dtypes.bfloat16
    return {"ident_bf": ident.astype(bf), "ident_f": ident, "rotT": np.ascontiguousarray(R2.T).astype(bf),
            "blk64": blk.astype(bf)}


def _pp(v, nchunk):
    return np.ascontiguousarray(np.asarray(v, np.float32).reshape(nchunk, 128).T)


def make_in_maps(inputs, cores=range(8)):
    f = lambda k: np.asarray(inputs[k], np.float32)
    x, c, ctx, c_ctx = f("x"), f("c"), f("ctx"), f("c_ctx")
    shared = dict(_consts())
    for L in ("e", "o"):
        shared[L + "_w_in"] = np.ascontiguousarray(f(L + "_w_in")[0])
        shared[L + "_w_out"] = np.ascontiguousarray(f(L + "_w_out")[0])
        shared[L + "_ada_w"] = np.ascontiguousarray(f(L + "_ada_w")[0])
        shared[L + "_ada_bT"] = _pp(f(L + "_ada_b")[0], 48)
        shared[L + "_norm_gT"] = _pp(f(L + "_norm_g")[0], 16)
    shared["e_vng_bc"] = np.ascontiguousarray(np.broadcast_to(f("e_a_vnorm_g")[0][None, :], (128, 1024)))
    shared["e_wsT"] = np.ascontiguousarray(f("e_a_ws")[0].transpose(2, 0, 1))
    shared["e_bs_bc"] = np.ascontiguousarray(np.broadcast_to(f("e_a_bs")[0][None], (128, 8, 128)))
    shared["e_gq"] = np.ascontiguousarray(np.tile(f("e_b_qnorm_g")[0], 2)[:, None])
    shared["e_gk"] = np.ascontiguousarray(np.tile(f("e_b_knorm_g")[0], 2)[:, None])
    shared["e_lam"] = np.ascontiguousarray(f("e_b_lambda")[0].reshape(1, 256))
    shared["e_gon"] = np.ascontiguousarray(f("e_b_onorm_g")[0][:, None])
    shared["o_dw_wT"] = np.ascontiguousarray(f("o_dw_w")[0].T.reshape(16, 128, 31).transpose(1, 0, 2))
    shared["o_dw_bT"] = _pp(f("o_dw_b")[0], 16)
    shared["o_ln_gT"] = _pp(f("o_ln_g")[0], 16)
    shared["o_ln_bT"] = _pp(f("o_ln_b")[0], 16)
    maps = []
    for core in cores:
        b, s = core // 2, core % 2
        if s == 0:
            order = np.concatenate([np.arange(0, 2048), np.arange(2048, 2176), np.arange(2176, 4096)])
            hm = np.array([0.0, 1.0], np.float32)
        else:
            order = np.concatenate([np.arange(2048, 4096), np.arange(1920, 2048), np.arange(0, 1920)])
            hm = np.array([1.0, 0.0], np.float32)
        xcore = np.concatenate([x[b][order], ctx[b]], axis=0)
        cos, sin = _rope_tables(order)
        cosT = np.ones((128, NTOK), np.float32)
        sinT = np.zeros((128, NTOK), np.float32)
        cosT[:, :SEQ] = np.tile(cos.T, (2, 1))
        sinT[:, :SEQ] = np.tile(sin.T, (2, 1))
        cvec = np.stack([c[b], c_ctx], axis=0)
        cT = np.ascontiguousarray(cvec.reshape(2, 16, 128).transpose(2, 0, 1))
        m = dict(shared)
        m.update({"xc": np.ascontiguousarray(xcore), "cT": cT, "rope_cos": cosT, "rope_sin": sinT,
                  "halo_mask": np.ascontiguousarray(np.broadcast_to(hm[None, :], (128, 2)))})
        maps.append(m)
    return maps


_NC_CACHE = {}


def kernel(**inputs):
    if "nc" not in _NC_CACHE:
        _NC_CACHE["nc"] = build_program()
    nc = _NC_CACHE["nc"]
    maps = make_in_maps(inputs)
    res = run_bass_kernel_spmd(nc, maps, core_ids=list(range(8)))
    out = np.empty((4, SEQ, D), np.float32)
    for core in range(8):
        b, s = core // 2, core % 2
        out[b, s * 2048:(s + 1) * 2048] = np.asarray(res.results[core]["out"])
    return out
```
